# Optimizing a Trainium2 kernel written in Bass

```python
import math
import jax, jax.numpy as jnp
from jax import lax
import numpy as np

D_MODEL = 1024
BATCH = 8
SEQ = 4096
DEPTH = 4

MEM_LEN = 256
EPS = 1e-6
NEG_INF = -1e30

CONV_DIM = D_MODEL
CONV_KERNEL = 31

SSD_INNER = 2 * D_MODEL
SSD_HEAD_DIM = 64
SSD_HEADS = SSD_INNER // SSD_HEAD_DIM
SSD_GROUPS = 4
SSD_STATE = 128
SSD_CONV = 4
SSD_CHUNK = 128
SSD_XBC = SSD_INNER + 2 * SSD_GROUPS * SSD_STATE

ATTN_HEADS = 16
ATTN_KV_HEADS = 4
ATTN_HEAD_DIM = 64
ATTN_DIM = ATTN_HEADS * ATTN_HEAD_DIM
ATTN_KV_DIM = ATTN_KV_HEADS * ATTN_HEAD_DIM
ATTN_WINDOW = 128
ATTN_BLOCK = 128

REL_BUCKETS = 32
REL_MAX_DIST = 128

XATTN_HEADS = 4
XATTN_HEAD_DIM = D_MODEL // XATTN_HEADS

N_BRANCH = 3
MLP_HIDDEN = 4 * D_MODEL

OFF_CONV = 0
OFF_Z = OFF_CONV + 2 * CONV_DIM
OFF_XBC = OFF_Z + SSD_INNER
OFF_DT = OFF_XBC + SSD_XBC
OFF_Q = OFF_DT + SSD_HEADS
OFF_K = OFF_Q + ATTN_DIM
OFF_V = OFF_K + ATTN_KV_DIM
OFF_GATE = OFF_V + ATTN_KV_DIM
IN_COLS = OFF_GATE + N_BRANCH * D_MODEL

kernel_name = "hybrid_conv_ssd_swa_gated_trunk"


def rms_norm(x, g):
    x32 = x.astype(jnp.float32)
    y = x32 * lax.rsqrt(jnp.mean(x32 * x32, axis=-1, keepdims=True) + EPS)
    return (y * g.astype(jnp.float32)).astype(x.dtype)


def layer_norm(x, g, b):
    x32 = x.astype(jnp.float32)
    mu = jnp.mean(x32, axis=-1, keepdims=True)
    xc = x32 - mu
    y = xc * lax.rsqrt(jnp.mean(xc * xc, axis=-1, keepdims=True) + EPS)
    return (y * g.astype(jnp.float32) + b.astype(jnp.float32)).astype(x.dtype)


def causal_depthwise_conv(x, w, b):
    k, c = w.shape
    y = lax.conv_general_dilated(
        x, w[:, None, :].astype(x.dtype), window_strides=(1,), padding=[(k - 1, 0)],
        dimension_numbers=("NWC", "WIO", "NWC"), feature_group_count=c)
    return y + b.astype(x.dtype)


def conformer_conv_branch(u, dw_w, dw_b, ln_g, ln_b):
    a, gate = jnp.split(u, 2, axis=-1)
    h = a * jax.nn.sigmoid(gate)
    h = causal_depthwise_conv(h, dw_w, dw_b)
    h = layer_norm(h, ln_g, ln_b)
    return jax.nn.silu(h)


def segsum_exp(a):
    t = a.shape[-1]
    cs = jnp.cumsum(a, axis=-1)
    diff = cs[..., :, None] - cs[..., None, :]
    mask = jnp.tril(jnp.ones((t, t), dtype=bool))
    return jnp.where(mask, jnp.exp(jnp.where(mask, diff, 0.0)), 0.0)


def ssd_chunked(x, dt, a, bm, cm):
    b, l, h, p = x.shape
    g, n = bm.shape[-2:]
    r = h // g
    q = SSD_CHUNK
    nc = l // q
    xdt = (x * dt[..., None].astype(x.dtype)).reshape(b, nc, q, g, r, p)
    bc = bm.reshape(b, nc, q, g, n)
    cc = cm.reshape(b, nc, q, g, n)
    da = (dt * a).reshape(b, nc, q, g, r).transpose(0, 1, 3, 4, 2)
    cs = jnp.cumsum(da, axis=-1)
    decay = segsum_exp(da).astype(x.dtype)
    cb = jnp.einsum("bcqgn,bcsgn->bcgqs", cc, bc)
    scores = cb[:, :, :, None] * decay
    y_diag = jnp.einsum("bcgrqs,bcsgrp->bcqgrp", scores, xdt)
    decay_to_end = jnp.exp(cs[..., -1:] - cs).astype(x.dtype)
    chunk_states = jnp.einsum("bcqgn,bcgrq,bcqgrp->bcgrpn", bc, decay_to_end, xdt)
    chunk_decay = jnp.exp(cs[..., -1])

    def step(state, inp):
        s_c, d_c = inp
        return state * d_c[..., None, None] + s_c, state

    init = jnp.zeros((b, g, r, p, n), jnp.float32)
    _, states_in = lax.scan(
        step, init,
        (jnp.moveaxis(chunk_states.astype(jnp.float32), 1, 0), jnp.moveaxis(chunk_decay, 1, 0)))
    states_in = jnp.moveaxis(states_in, 0, 1).astype(x.dtype)
    decay_from_start = jnp.exp(cs).astype(x.dtype)
    y_off = jnp.einsum("bcqgn,bcgrpn,bcgrq->bcqgrp", cc, states_in, decay_from_start)
    return (y_diag + y_off).reshape(b, l, h, p)


def ssd_branch(z, xbc, dt_raw, conv_w, conv_b, dt_bias, a_log, d_skip, norm_g):
    b, l, _ = xbc.shape
    xbc = jax.nn.silu(causal_depthwise_conv(xbc, conv_w, conv_b))
    xs = xbc[..., :SSD_INNER].reshape(b, l, SSD_HEADS, SSD_HEAD_DIM)
    bm = xbc[..., SSD_INNER:SSD_INNER + SSD_GROUPS * SSD_STATE].reshape(b, l, SSD_GROUPS, SSD_STATE)
    cm = xbc[..., SSD_INNER + SSD_GROUPS * SSD_STATE:].reshape(b, l, SSD_GROUPS, SSD_STATE)
    dt = jax.nn.softplus(dt_raw.astype(jnp.float32) + dt_bias.astype(jnp.float32))
    a = -jnp.exp(a_log.astype(jnp.float32))
    y = ssd_chunked(xs, dt, a, bm, cm) + xs * d_skip[:, None].astype(xs.dtype)
    y = y.reshape(b, l, SSD_INNER) * jax.nn.silu(z)
    y = rms_norm(y.reshape(b, l, SSD_GROUPS, SSD_INNER // SSD_GROUPS),
                 norm_g.reshape(SSD_GROUPS, SSD_INNER // SSD_GROUPS))
    return y.reshape(b, l, SSD_INNER)


def t5_band_bias(rel_table):
    qi = jnp.arange(ATTN_BLOCK)[:, None] + ATTN_BLOCK
    kj = jnp.arange(2 * ATTN_BLOCK)[None, :]
    dist = qi - kj
    max_exact = REL_BUCKETS // 2
    d = jnp.maximum(dist, 1).astype(jnp.float32)
    large = max_exact + (jnp.log(d / max_exact) / math.log(REL_MAX_DIST / max_exact)
                         * (REL_BUCKETS - max_exact)).astype(jnp.int32)
    large = jnp.minimum(large, REL_BUCKETS - 1)
    bucket = jnp.where(dist < max_exact, jnp.maximum(dist, 0), large)
    bias = jnp.transpose(rel_table[bucket], (2, 0, 1)).astype(jnp.float32)
    return bias, dist


def swa_branch(q, k, v, q_g, k_g, sinks, rel_bias, band_dist):
    b, l, _ = q.shape
    nb = l // ATTN_BLOCK
    r = ATTN_HEADS // ATTN_KV_HEADS
    q = rms_norm(q.reshape(b, l, ATTN_HEADS, ATTN_HEAD_DIM), q_g)
    k = rms_norm(k.reshape(b, l, ATTN_KV_HEADS, ATTN_HEAD_DIM), k_g)
    v = v.reshape(b, l, ATTN_KV_HEADS, ATTN_HEAD_DIM)

    def band(t):
        tp = jnp.pad(t, ((0, 0), (ATTN_BLOCK, 0), (0, 0), (0, 0)))
        prev = tp[:, :l].reshape(b, nb, ATTN_BLOCK, ATTN_KV_HEADS, ATTN_HEAD_DIM)
        cur = t.reshape(b, nb, ATTN_BLOCK, ATTN_KV_HEADS, ATTN_HEAD_DIM)
        return jnp.concatenate([prev, cur], axis=2)

    kb, vb = band(k), band(v)
    qb = q.reshape(b, nb, ATTN_BLOCK, ATTN_KV_HEADS, r, ATTN_HEAD_DIM)
    logits = jnp.einsum("bnqgrd,bnkgd->bngrqk", qb, kb).astype(jnp.float32) * (ATTN_HEAD_DIM ** -0.5)
    logits = logits + rel_bias.reshape(ATTN_KV_HEADS, r, ATTN_BLOCK, 2 * ATTN_BLOCK)
    key_pos = (jnp.arange(nb)[:, None] * ATTN_BLOCK - ATTN_BLOCK
               + jnp.arange(2 * ATTN_BLOCK)[None, :])
    in_window = (band_dist >= 0) & (band_dist < ATTN_WINDOW)
    mask = in_window[None] & (key_pos >= 0)[:, None, :]
    logits = jnp.where(mask[None, :, None, None], logits, NEG_INF)
    sink = sinks.astype(jnp.float32).reshape(ATTN_KV_HEADS, r)[None, None, :, :, None, None]
    m = jnp.maximum(jnp.max(logits, axis=-1, keepdims=True), sink)
    pexp = jnp.exp(logits - m)
    probs = pexp / (jnp.sum(pexp, axis=-1, keepdims=True) + jnp.exp(sink - m))
    out = jnp.einsum("bngrqk,bnkgd->bnqgrd", probs.astype(v.dtype), vb)
    return out.reshape(b, l, ATTN_DIM)


def memory_cross_attention(h, mem_h, w_q, w_kv, q_g, k_g, w_o):
    b, l, _ = h.shape
    m = mem_h.shape[1]
    q = rms_norm((h @ w_q).reshape(b, l, XATTN_HEADS, XATTN_HEAD_DIM), q_g)
    kv = mem_h @ w_kv
    k = rms_norm(kv[..., :D_MODEL].reshape(b, m, XATTN_HEADS, XATTN_HEAD_DIM), k_g)
    v = kv[..., D_MODEL:].reshape(b, m, XATTN_HEADS, XATTN_HEAD_DIM)
    logits = jnp.einsum("bqhd,bkhd->bhqk", q, k).astype(jnp.float32) * (XATTN_HEAD_DIM ** -0.5)
    probs = jax.nn.softmax(logits, axis=-1)
    out = jnp.einsum("bhqk,bkhd->bqhd", probs.astype(v.dtype), v).reshape(b, l, D_MODEL)
    return out @ w_o


def setup_inputs(seed: int = 0) -> dict:
    key = jax.random.key(seed)
    ks = iter(jax.random.split(key, 48))
    f32 = jnp.float32
    L = DEPTH

    def nrm(shape, scale):
        return jax.random.normal(next(ks), shape, f32) * scale

    def gain(shape):
        return 1.0 + nrm(shape, 0.02)

    out_scale = (2.0 * DEPTH) ** -0.5
    dt0 = jnp.exp(jax.random.uniform(next(ks), (L, SSD_HEADS), f32, math.log(1e-3), math.log(1e-1)))
    return {
        "x": nrm((BATCH, SEQ, D_MODEL), 1.0),
        "mem": nrm((BATCH, MEM_LEN, D_MODEL), 1.0),
        "rel_table": nrm((REL_BUCKETS, ATTN_HEADS), 0.1),
        "norm_mix": gain((L, D_MODEL)),
        "w_in": nrm((L, D_MODEL, IN_COLS), D_MODEL ** -0.5),
        "gate_bias": nrm((L, N_BRANCH, D_MODEL), 0.01),
        "conv_dw_w": nrm((L, CONV_KERNEL, CONV_DIM), CONV_KERNEL ** -0.5),
        "conv_dw_b": nrm((L, CONV_DIM), 0.01),
        "conv_ln_g": gain((L, CONV_DIM)),
        "conv_ln_b": nrm((L, CONV_DIM), 0.01),
        "w_conv_out": nrm((L, CONV_DIM, D_MODEL), CONV_DIM ** -0.5),
        "ssd_conv_w": nrm((L, SSD_CONV, SSD_XBC), SSD_CONV ** -0.5),
        "ssd_conv_b": nrm((L, SSD_XBC), 0.01),
        "ssd_dt_bias": dt0 + jnp.log(-jnp.expm1(-dt0)),
        "ssd_A_log": jnp.log(jax.random.uniform(next(ks), (L, SSD_HEADS), f32, 1.0, 16.0)),
        "ssd_D": gain((L, SSD_HEADS)),
        "ssd_norm_g": gain((L, SSD_INNER)),
        "w_ssd_out": nrm((L, SSD_INNER, D_MODEL), SSD_INNER ** -0.5),
        "attn_q_norm": gain((L, ATTN_HEAD_DIM)),
        "attn_k_norm": gain((L, ATTN_HEAD_DIM)),
        "attn_sinks": nrm((L, ATTN_HEADS), 0.5),
        "w_attn_out": nrm((L, ATTN_DIM, D_MODEL), ATTN_DIM ** -0.5),
        "w_mix_out": nrm((L, D_MODEL, D_MODEL), out_scale * D_MODEL ** -0.5),
        "norm_xattn": gain((L, D_MODEL)),
        "norm_mem": gain((L, D_MODEL)),
        "w_xq": nrm((L, D_MODEL, D_MODEL), D_MODEL ** -0.5),
        "w_xkv": nrm((L, D_MODEL, 2 * D_MODEL), D_MODEL ** -0.5),
        "xattn_q_norm": gain((L, XATTN_HEAD_DIM)),
        "xattn_k_norm": gain((L, XATTN_HEAD_DIM)),
        "w_xo": nrm((L, D_MODEL, D_MODEL), out_scale * D_MODEL ** -0.5),
        "norm_mlp": gain((L, D_MODEL)),
        "w_mlp_up": nrm((L, D_MODEL, MLP_HIDDEN), D_MODEL ** -0.5),
        "w_mlp_down": nrm((L, MLP_HIDDEN, D_MODEL), out_scale * MLP_HIDDEN ** -0.5),
    }


def reference(x, mem, rel_table, norm_mix, w_in, gate_bias, conv_dw_w, conv_dw_b, conv_ln_g,
              conv_ln_b, w_conv_out, ssd_conv_w, ssd_conv_b, ssd_dt_bias, ssd_A_log, ssd_D,
              ssd_norm_g, w_ssd_out, attn_q_norm, attn_k_norm, attn_sinks, w_attn_out, w_mix_out,
              norm_xattn, norm_mem, w_xq, w_xkv, xattn_q_norm, xattn_k_norm, w_xo, norm_mlp,
              w_mlp_up, w_mlp_down):
    b, l, _ = x.shape
    rel_bias, band_dist = t5_band_bias(rel_table)
    h = x
    for i in range(DEPTH):
        u = rms_norm(h, norm_mix[i])
        proj = u @ w_in[i]
        y_a = conformer_conv_branch(proj[..., OFF_CONV:OFF_Z], conv_dw_w[i], conv_dw_b[i],
                                    conv_ln_g[i], conv_ln_b[i]) @ w_conv_out[i]
        y_b = ssd_branch(proj[..., OFF_Z:OFF_XBC], proj[..., OFF_XBC:OFF_DT], proj[..., OFF_DT:OFF_Q],
                         ssd_conv_w[i], ssd_conv_b[i], ssd_dt_bias[i], ssd_A_log[i], ssd_D[i],
                         ssd_norm_g[i]) @ w_ssd_out[i]
        y_c = swa_branch(proj[..., OFF_Q:OFF_K], proj[..., OFF_K:OFF_V], proj[..., OFF_V:OFF_GATE],
                         attn_q_norm[i], attn_k_norm[i], attn_sinks[i], rel_bias, band_dist) @ w_attn_out[i]
        gates = jax.nn.sigmoid(proj[..., OFF_GATE:IN_COLS].reshape(b, l, N_BRANCH, D_MODEL)
                               + gate_bias[i].astype(proj.dtype))
        merged = gates[..., 0, :] * y_a + gates[..., 1, :] * y_b + gates[..., 2, :] * y_c
        h = h + merged @ w_mix_out[i]
        h = h + memory_cross_attention(rms_norm(h, norm_xattn[i]), rms_norm(mem, norm_mem[i]),
                                       w_xq[i], w_xkv[i], xattn_q_norm[i], xattn_k_norm[i], w_xo[i])
        u = rms_norm(h, norm_mlp[i])
        h = h + jnp.square(jax.nn.relu(u @ w_mlp_up[i])) @ w_mlp_down[i]
    return h
```

```python
import math
from contextlib import ExitStack

import numpy as np
import concourse.bass as bass
import concourse.mybir as mybir
from concourse.bass_utils import run_bass_kernel_spmd

F32 = mybir.dt.float32
BF16 = mybir.dt.bfloat16
ALU = mybir.AluOpType
AF = mybir.ActivationFunctionType

D = 1024
EPS = 1e-6
OFF_CONV, OFF_Z, OFF_XBC, OFF_DT, OFF_Q, OFF_K, OFF_V, OFF_GATE, IN_COLS = 0, 2048, 4096, 7168, 7200, 8224, 8480, 8736, 11808
NEG = -30000.0

WSHAPES = {
    "w_in": (1024, IN_COLS), "w_conv_out": (1024, 1024), "w_ssd_out": (2048, 1024), "w_attn_out": (1024, 1024),
    "w_mix_out": (1024, 1024), "w_xq": (1024, 1024), "w_xkv": (1024, 2048), "w_xo": (1024, 1024),
    "w_mlp_up": (1024, 4096), "w_mlp_down": (4096, 1024),
}
WORDER = ["w_in", "w_conv_out", "w_ssd_out", "w_attn_out", "w_mix_out", "w_xkv", "w_xq", "w_xo", "w_mlp_up", "w_mlp_down"]

PC_NMIX, PC_GB, PC_CW, PC_CB, PC_LG, PC_LB, PC_SW, PC_SB, PC_NG, PC_QN, PC_KN, PC_NX, PC_NM, PC_XQ, PC_XK, PC_NMLP, NPC = (
    0, 8, 32, 280, 288, 296, 304, 400, 424, 440, 441, 442, 450, 458, 460, 462, 470)
PR_DTB, PR_ALOG, PR_D, PR_SINK, NPR = 0, 32, 64, 96, 112

EPOCH = 6000


class Track:
    def __init__(self, name, is_dma=False):
        self.name = name
        self.is_dma = is_dma
        self.n = 0
        self.nsig = 0
        self.sems = []


class Buf:
    __slots__ = ("name", "w", "r", "excl")

    def __init__(self, name="b", excl=False):
        self.name = name
        self.w = None
        self.r = {}
        self.excl = excl


class Op:
    __slots__ = ("eng", "track", "seq", "eseq", "fn", "waits", "signal", "sem", "val", "tag")


class Sched:
    def __init__(self, nc, stack, dry=False):
        self.nc = nc
        self.stack = stack
        self.dry = dry
        self.ops = []
        self.engs = {"pe": nc.tensor, "act": nc.scalar, "dve": nc.vector, "pool": nc.gpsimd, "sp": nc.sync}
        self.etrack = {k: Track(k) for k in self.engs}
        self.ecount = {k: 0 for k in self.engs}
        self.seen = {k: {} for k in self.engs}
        self.nsem = 0

    def new_sem(self, name):
        self.nsem += 1
        return self.stack.enter_context(self.nc.semaphore(f"{name}_{self.nsem}"))

    def _record(self, eng, track, fn, reads, writes):
        if self.dry:
            return None
        if any(b.excl for b in reads):
            writes = list(writes) + [b for b in reads if b.excl and b not in writes]
            reads = [b for b in reads if not b.excl]
        op = Op()
        op.eng = eng
        op.track = track
        track.n += 1
        op.seq = track.n
        self.ecount[eng] += 1
        op.eseq = self.ecount[eng]
        op.fn = fn
        op.tag = getattr(self, 'tag', '')
        op.signal = track.is_dma
        op.sem = None
        op.val = 0
        deps = []
        for b in reads:
            if b.w is not None:
                deps.append(b.w)
        for b in writes:
            if b.w is not None:
                deps.append(b.w)
            deps.extend(b.r.values())
        waits = []
        seen = self.seen[eng]
        own = self.etrack[eng]
        for d in deps:
            if d.track is own:
                if eng == "pe" or eng == "sp":
                    continue
                if d.eseq < op.eseq - 2:
                    continue
            if seen.get(d.track, 0) >= d.seq:
                continue
            seen[d.track] = d.seq
            d.signal = True
            waits.append(d)
        op.waits = waits
        for b in reads:
            b.r[track] = op
        for b in writes:
            b.w = op
            b.r = {}
        self.ops.append(op)
        return op

    def op(self, eng, fn, reads=(), writes=()):
        return self._record(eng, self.etrack[eng], fn, reads, writes)

    def dma(self, track, fn, reads=(), writes=(), eng="sp"):
        return self._record(eng, track, fn, reads, writes)

    def wait_all(self, eng, bufs):
        return self._record(eng, self.etrack[eng], None, bufs, ())

    def emit(self):
        import os
        maxops = int(os.environ.get("MAXOPS", "0"))
        ops = self.ops[:maxops] if maxops else self.ops
        for op in ops:
            e = self.engs[op.eng]
            for d in op.waits:
                e.wait_ge(d.sem, d.val)
            if op.fn is None:
                continue
            inst = op.fn(e)
            if op.signal:
                t = op.track
                k = t.nsig
                t.nsig += 1
                inc = 16 if t.is_dma else 1
                epn = EPOCH // inc
                ep = k // epn
                if ep >= len(t.sems):
                    t.sems.append(self.new_sem(t.name))
                op.sem = t.sems[ep]
                op.val = (k % epn + 1) * inc
                inst.then_inc(op.sem, inc)


class Arena:
    def __init__(self, tensor, nwords):
        self.t = tensor
        self.n = nwords
        self.pos = 0
        self.live = []
        self.pending = {}

    def reset(self):
        for b in self.live:
            if b.w is not None:
                o = self.pending.get(b.w.track)
                if o is None or o.seq < b.w.seq:
                    self.pending[b.w.track] = b.w
            for tr, op in b.r.items():
                o = self.pending.get(tr)
                if o is None or o.seq < op.seq:
                    self.pending[tr] = op
        self.live = []
        self.pos = 0
        self.rawstg = None

    def bufs(self, n):
        out = []
        for _ in range(n):
            b = Buf("arb")
            b.r = dict(self.pending)
            self.live.append(b)
            out.append(b)
        return out

    def alloc(self, words, dtype=F32, pat=None, **kw):
        words = (words + 7) // 8 * 8
        assert self.pos + words <= self.n, f"arena overflow {self.pos}+{words}>{self.n}"
        ap = self.t[:, self.pos:self.pos + words]
        self.pos += words
        if dtype == BF16:
            ap = ap.bitcast(BF16)
        if pat is not None:
            ap = ap.rearrange(pat, **kw)
        b = Buf("ar")
        b.r = dict(self.pending)
        self.live.append(b)
        return ap, b


class WStream:
    NSLOT = 4

    def __init__(self, K):
        self.K = K
        self.plan = []
        self.run = False
        self.i = 0
        self.issued = 0
        nc = self.K.nc
        self.slots = [nc.alloc_sbuf_tensor(f"wslot{i}", [128, 8, 512], BF16) for i in range(self.NSLOT)]
        self.bufs = [Buf(f"wslot{i}") for i in range(self.NSLOT)]
        self.tracks = [Track(f"wsl{i}", True) for i in range(self.NSLOT)]

    def start(self):
        self.run = True
        self.i = 0
        self.issued = 0

    def _issue(self, j):
        name, l, kc0, n0, nk, ncols, kind = self.plan[j]
        s = j % self.NSLOT
        src = self.K.wb[(l, name)][:, kc0:kc0 + nk, n0:n0 + ncols]
        dst = self.slots[s][:, 0:nk, 0:ncols]
        if kind == "qperm":
            for e_ in range(2):
                for i_ in range(4):
                    sp = src[:, :, (e_ * 4 + i_) * 64:(e_ * 4 + i_ + 1) * 64]
                    dp = dst[:, :, (i_ * 2 + e_) * 64:(i_ * 2 + e_ + 1) * 64]
                    self.K.S.dma(self.tracks[s], lambda e, dp=dp, sp=sp: e.dma_start(out=dp, in_=sp),
                                 reads=[self.K.wbuf[(l, name)]], writes=[self.bufs[s]])
            return
        self.K.S.dma(self.tracks[s], lambda e, dst=dst, src=src: e.dma_start(out=dst, in_=src),
                     reads=[self.K.wbuf[(l, name)]], writes=[self.bufs[s]])

    def get(self, name, l, kc0, n0, nk=8, ncols=512, kind=None):
        key = (name, l, kc0, n0, nk, ncols, kind)
        if not self.run:
            self.plan.append(key)
            return self.slots[0], Buf("dummy")
        assert self.plan[self.i] == key, (self.plan[self.i], key)
        while self.issued < min(len(self.plan), self.i + self.NSLOT - 1):
            self._issue(self.issued)
            self.issued += 1
        s = self.i % self.NSLOT
        self.i += 1
        return self.slots[s], self.bufs[s]


class KB:
    def __init__(self, T, L, TC=256, dbg=()):
        self.T, self.L, self.TC = T, L, TC
        self.NT = TC // 128
        self.NCH = T // TC
        self.dbg = set(dbg)

    def PE(self, fn, reads, writes):
        self.S.op("pe", fn, reads, writes)

    def ACT(self, fn, reads, writes):
        self.S.op("act", fn, reads, writes)

    def DVE(self, fn, reads, writes):
        self.S.op("dve", fn, reads, writes)

    def mm(self, out, lhsT, rhs, start, stop, reads, writes):
        self.S.op("pe", lambda e: e.matmul(out, lhsT, rhs, start=start, stop=stop), reads, writes)

    def act(self, out, in_, func, reads, writes, **kw):
        self.S.op("act", lambda e: e.activation(out, in_, func, **kw), reads, writes)

    def tt(self, out, in0, in1, op, reads, writes):
        self.S.op("dve", lambda e: e.tensor_tensor(out=out, in0=in0, in1=in1, op=op), reads, writes)

    def stt(self, out, in0, scalar, in1, op0, op1, reads, writes):
        self.S.op("dve", lambda e: e.scalar_tensor_tensor(out=out, in0=in0, scalar=scalar, in1=in1, op0=op0, op1=op1), reads, writes)

    def ts(self, out, in0, s1, s2, op0, op1, reads, writes):
        if s2 is None:
            self.S.op("dve", lambda e: e.tensor_scalar(out=out, in0=in0, scalar1=s1, scalar2=None, op0=op0), reads, writes)
        else:
            self.S.op("dve", lambda e: e.tensor_scalar(out=out, in0=in0, scalar1=s1, scalar2=s2, op0=op0, op1=op1), reads, writes)

    def bank(self, grp):
        lst = self.bgrp[grp]
        i = self.bpos[grp]
        self.bpos[grp] = (i + 1) % len(lst)
        j = lst[i]
        return self.ps[j], self.psb[j]

    def rsqrt(self, out, in_, scale, reads, writes):
        self.act(out, in_, AF.Ln, reads, writes, scale=scale, bias=EPS)
        self.act(out, out, AF.Exp, writes, writes, scale=-0.5)

    def pcv(self, l, off, n=1):
        return self.pc[:, l * NPC + off: l * NPC + off + n]

    def prv(self, l, off, n):
        return self.pr[:, l * NPR + off: l * NPR + off + n]

    def build(self):
        nc = bass.Bass("TRN2", target_bir_lowering=False)
        self.nc = nc
        T, L, TC = self.T, self.L, self.TC

        def din(n, s, d=F32):
            return nc.dram_tensor(n, s, d, kind="ExternalInput").ap()

        self.x = din("x", [T, D])
        self.mem = din("mem", [256, D])
        self.consts_d = din("consts", [128, 512])
        self.swab_d = din("swab", [128, 2 * 16 * 128])
        self.pcol_d = din("pcol", [128, L * NPC])
        self.prow_d = din("prow", [128, L * NPR])
        self.wd = {n: din(n, [L, k, c]) for n, (k, c) in WSHAPES.items()}
        self.y = nc.dram_tensor("y", [T, D], F32, kind="ExternalOutput").ap()
        self.wb = {}
        self.wbuf = {}
        for l in range(L):
            for n, (k, c) in WSHAPES.items():
                self.wb[(l, n)] = nc.dram_tensor(f"wb_{n}_{l}", [128, k // 128, c], BF16, kind="Internal").ap()
                self.wbuf[(l, n)] = Buf(f"wb_{n}_{l}")
        self.kvs = [nc.dram_tensor(f"kvs_{l}", [128, 4096], BF16, kind="Internal").ap() for l in range(L)]
        self.kvs_b = [Buf(f"kvs{l}") for l in range(L)]
        self.dbg_d = {}
        for name in self.dbg:
            if name == "raw":
                continue
            self.dbg_d[name] = nc.dram_tensor("dbg_" + name, [128, 16 if name in ("ybin",) else 8, T], F32, kind="ExternalOutput").ap()
        self.dbg.discard("raw") if False else None
        self.dbg_b = {n: Buf("dbg" + n) for n in self.dbg}
        self.dbg_t = {n: Track("dbg" + n, True) for n in self.dbg}

        with ExitStack() as st:
            self.W = WStream(self)
            self.S = Sched(nc, st, dry=True)
            self.alloc_all(dry=True)
            self.program()
            self.S = Sched(nc, st, dry=False)
            self.W.start()
            self.program()
            self.S.emit()
        return nc

    def alloc_all(self, dry):
        nc = self.nc
        L, W = self.L, self.TC
        A = nc.alloc_sbuf_tensor

        def R(name, shape, dt=F32):
            return A(name, shape, dt), Buf(name)

        self.hT, _ = R("hT", [128, 8, W])
        self.hTb = [Buf(f"hT{c}") for c in range(8)]
        self.uT, self.uTb = R("uT", [128, 8, W], BF16)
        self.mrg, _ = R("mrg", [128, 8, W])
        self.mrgb = [Buf(f"mrg{c}") for c in range(8)]
        self.state, _ = R("state", [128, L, 2048])
        self.stateb = [[Buf(f"st{l}_{g}") for g in range(4)] for l in range(L)]
        self.bias, self.biasb = R("swabias", [128, 2, 16, 128])
        self.pc, self.pcb = R("pc", [128, L * NPC])
        self.pr, self.prb = R("pr", [128, L * NPR])
        self.arow, self.arowb = R("arow", [128, L * 32])
        self.esink, self.esinkb = R("esink", [128, L * 16])
        self.ctail, _ = R("ctail", [128, L, 8, 30])
        self.ctailb = [[Buf("ct") for c in range(8)] for l in range(L)]
        self.xtail, _ = R("xtail", [128, L, 24, 3])
        self.xtailb = [[Buf("xt") for c in range(24)] for l in range(L)]
        self.kprev, _ = R("kprev", [128, L, 4, 128], BF16)
        self.kprevb = [Buf("kp") for l in range(L)]
        self.vprev, _ = R("vprev", [128, L, 512], BF16)
        self.vprevb = [Buf("vp") for l in range(L)]
        self.cst, self.cstb = R("cst", [128, 512])
        self.ident = self.cst[:, 0:128]
        self.triu = self.cst[:, 128:256]
        self.negm = self.cst[:, 256:384]
        self.blk64 = self.cst[:, 384:512]
        self.identb, self.identbb = R("identb", [128, 128], BF16)
        self.negmb, self.negmbb = R("negmb", [128, 128], BF16)
        self.onesb, self.onesbb = R("onesb", [128, 128], BF16)
        self.onesf, self.onesfb = R("onesf", [128, 128])
        self.ps = [nc.alloc_psum_tensor(f"ps{i}", [128, 512], F32) for i in range(8)]
        self.psb = [Buf(f"ps{i}", excl=True) for i in range(8)]
        self.bgrp = {"mm": [0, 1, 2, 3], "st": [4, 5], "tr": [6, 7]}
        self.bpos = {"mm": 0, "st": 0, "tr": 0}
        nw = (nc.sbuf_bytes_remaining - 2048) // 4
        nw = nw // 8 * 8
        self.ar_t = A("arena", [128, nw], F32)
        self.arena = Arena(self.ar_t, nw)

    def program(self):
        if not self.S.dry:
            self.reset_bufs()
        self.init_phase()
        for ck in range(self.NCH):
            self.load_x(ck)
            import os
            PH = os.environ.get("PH", "ncswmxp")
            for l in range(self.L):
                if "n" in PH:
                    self.rmsnorm(l, PC_NMIX)
                self.dump("u", self.uT[:], self.uTb, ck) if l == 0 else None
                if "c" in PH:
                    self.conv_phase(l, ck)
                if "s" in PH:
                    self.ssd_phase(l, ck)
                if "w" in PH:
                    self.swa_phase(l, ck)
                if "m" in PH:
                    self.mix_phase(l, ck)
                if "x" in PH:
                    if ck == 0:
                        self.kv_precompute(l)
                    self.xattn_phase(l, ck)
                if "p" in PH:
                    self.mlp_phase(l, ck)
            self.store_y(ck)
        self.S.wait_all("sp", [self.yb])

    def reset_bufs(self):
        self.arena = Arena(self.ar_t, self.arena.n)
        self.bpos = {"mm": 0, "st": 0, "tr": 0}

    def dump(self, name, ap, bufs, ck, nchunks=8):
        if name not in self.dbg:
            return
        W = self.TC
        stg, sb = self.arena_dbg(nchunks)
        bl = bufs if isinstance(bufs, list) else [bufs]
        self.ACT(lambda e: e.activation(stg, ap, AF.Identity), bl, [sb])
        dst = self.dbg_d[name][:, :, ck * W:(ck + 1) * W]
        self.S.dma(self.dbg_t[name], lambda e: e.dma_start(out=dst, in_=stg), reads=[sb], writes=[self.dbg_b[name]])

    def rawdump(self, name, ap, bufs, n):
        if "raw" not in self.dbg:
            return
        if not hasattr(self, "raw_d"):
            self.raw_d = {}
        if name not in self.raw_d:
            self.raw_d[name] = self.nc.dram_tensor("raw_" + name, [128, n], F32, kind="ExternalOutput").ap()
        if getattr(self.arena, "rawstg", None) is None:
            self.arena.rawstg = self.arena.alloc(2048)
        stg, sb = self.arena.rawstg
        stg = stg[:, 0:n]
        bl = bufs if isinstance(bufs, list) else [bufs]
        self.ACT(lambda e: e.activation(stg, ap, AF.Identity), bl, [sb])
        dst = self.raw_d[name]
        self.S.dma(Track("raw" + name, True), lambda e: e.dma_start(out=dst, in_=stg), reads=[sb], writes=[Buf("rawo")])

    def arena_dbg(self, nchunks):
        W = self.TC
        return self.arena.alloc(nchunks * W, F32, "p (c w) -> p c w", c=nchunks)

    def init_phase(self):
        S, nc, L = self.S, self.nc, self.L
        self.yb = Buf("y")
        self.ytrack = Track("ytr", True)
        self.xtrack = Track("xtr", True)
        self.kvtrack = Track("kvtr", True)
        self.kvltrack = Track("kvltr", True)
        tr = [Track(f"init{i}", True) for i in range(4)]
        S.dma(tr[0], lambda e: e.dma_start(out=self.cst[:], in_=self.consts_d), writes=[self.cstb])
        S.dma(tr[1], lambda e: e.dma_start(out=self.bias[:].rearrange("p a h q -> p (a h q)"), in_=self.swab_d), writes=[self.biasb])
        S.dma(tr[2], lambda e: e.dma_start(out=self.pc[:], in_=self.pcol_d), writes=[self.pcb])
        S.dma(tr[3], lambda e: e.dma_start(out=self.pr[:], in_=self.prow_d), writes=[self.prb])
        self.cvtrack = {}
        for l in range(L):
            for n in WORDER:
                k, c = WSHAPES[n]
                t = Track(f"cv_{n}_{l}", True)
                for kc in range(k // 128):
                    for n0 in range(0, c, 4096):
                        n1 = min(c, n0 + 4096)
                        dst = self.wb[(l, n)][:, kc, n0:n1]
                        src = self.wd[n][l, kc * 128:(kc + 1) * 128, n0:n1]
                        S.dma(t, lambda e, dst=dst, src=src: e.dma_start(out=dst, in_=src), writes=[self.wbuf[(l, n)]], eng="pool")
        self.DVE(lambda e: e.memset(self.onesb[:], 1.0), [], [self.onesbb])
        self.DVE(lambda e: e.memset(self.onesf[:], 1.0), [], [self.onesfb])
        self.DVE(lambda e: e.tensor_copy(self.identb[:], self.ident), [self.cstb], [self.identbb])
        self.DVE(lambda e: e.tensor_copy(self.negmb[:], self.negm), [self.cstb], [self.negmbb])
        allst = [b for row in self.stateb for b in row]
        self.DVE(lambda e: e.memset(self.state[:].rearrange("p l n -> p (l n)"), 0.0), [], allst)
        self.DVE(lambda e: e.memset(self.ctail[:].rearrange("p l c j -> p (l c j)"), 0.0), [], [b for row in self.ctailb for b in row])
        self.DVE(lambda e: e.memset(self.xtail[:].rearrange("p l c j -> p (l c j)"), 0.0), [], [b for row in self.xtailb for b in row])
        self.DVE(lambda e: e.memset(self.kprev[:].rearrange("p l g k -> p (l g k)"), 0.0), [], self.kprevb)
        self.DVE(lambda e: e.memset(self.vprev[:].rearrange("p l n -> p (l n)"), 0.0), [], self.vprevb)
        for l in range(L):
            a = self.arow[:, l * 32:(l + 1) * 32]
            self.act(a, self.prv(l, PR_ALOG, 32), AF.Exp, [self.prb], [self.arowb])
            self.ts(a, a, -1.0, None, ALU.mult, None, [self.arowb], [self.arowb])
            self.act(self.esink[:, l * 16:(l + 1) * 16], self.prv(l, PR_SINK, 16), AF.Exp, [self.prb], [self.esinkb])

    def load_x(self, ck):
        self.S.tag = 'load_x' + str((ck))
        W, NT = self.TC, self.NT
        ar = self.arena
        ar.reset()
        xin, xb = ar.alloc(NT * 1024, F32, "p (t d) -> p t d", t=NT)
        src = self.x[ck * W:(ck + 1) * W, :].rearrange("(t p) d -> p t d", p=128)
        self.S.dma(self.xtrack, lambda e: e.dma_start(out=xin, in_=src), writes=[xb])
        cpb = 512 // W
        for c0 in range(0, 8, cpb):
            pt, pb = self.bank("tr")
            for cc in range(cpb):
                for t in range(NT):
                    o = pt[:, cc * W + t * 128: cc * W + (t + 1) * 128]
                    i = xin[:, t, (c0 + cc) * 128:(c0 + cc + 1) * 128]
                    self.PE(lambda e, o=o, i=i: e.transpose(o, i, self.ident), [xb, self.cstb], [pb])
            self.DVE(lambda e, c0=c0, pt=pt: e.tensor_copy(self.hT[:, c0:c0 + cpb, :], pt[:, 0:cpb * W].rearrange("p (c w) -> p c w", c=cpb)),
                     [pb], self.hTb[c0:c0 + cpb])

    def store_y(self, ck):
        self.S.tag = 'store_y' + str((ck))
        W, NT = self.TC, self.NT
        ar = self.arena
        ar.reset()
        yo, yob = ar.alloc(NT * 1024, F32, "p (t d) -> p t d", t=NT)
        for t in range(NT):
            for c0 in range(0, 8, 4):
                pt, pb = self.bank("tr")
                for cc in range(4):
                    o = pt[:, cc * 128:(cc + 1) * 128]
                    i = self.hT[:, c0 + cc, t * 128:(t + 1) * 128]
                    self.PE(lambda e, o=o, i=i: e.transpose(o, i, self.ident), [self.hTb[c0 + cc], self.cstb], [pb])
                self.ACT(lambda e, t=t, c0=c0, pt=pt: e.activation(yo[:, t, c0 * 128:(c0 + 4) * 128], pt[:, 0:512], AF.Identity), [pb], [yob])
        dst = self.y[ck * W:(ck + 1) * W, :].rearrange("(t p) d -> p t d", p=128)
        self.S.dma(self.ytrack, lambda e: e.dma_start(out=dst, in_=yo), reads=[yob], writes=[self.yb])

    def rmsnorm(self, l, off):
        self.S.tag = 'rmsnorm' + str((l, off))
        W = self.TC
        ar = self.arena
        ar.reset()
        sq = [ar.alloc(W) for _ in range(2)]
        rs, rsb = ar.alloc(W)
        pst, pstb = self.bank("st")
        for c in range(8):
            s, sb = sq[c % 2]
            self.act(s, self.hT[:, c, :], AF.Square, [self.hTb[c]], [sb])
            self.mm(pst[:, 0:W], self.onesf[:], s, c == 0, c == 7, [sb, self.onesfb], [pstb])
        self.rsqrt(rs, pst[:, 0:W], 1.0 / D, [pstb], [rsb])
        for c in range(8):
            self.stt(self.uT[:, c, :], self.hT[:, c, :], self.pcv(l, off + c), rs, ALU.mult, ALU.mult,
                     [self.hTb[c], rsb, self.pcb], [self.uTb])

    def branch_out(self, l, k, srcT, srcb, wname, KC, scratch=None):
        W = self.TC
        ar = self.arena
        if scratch is None:
            gs = [ar.alloc(W) for _ in range(4)]
            tmp = [ar.alloc(W) for _ in range(2)]
        else:
            gs, tmp = scratch[0:4], scratch[4:6]
        for half in range(2):
            wg, wgb = self.W.get("w_in", l, 0, OFF_GATE + k * 1024 + half * 512)
            for cc in range(4):
                oc = half * 4 + cc
                pg, pgb = self.bank("st")
                for kc in range(8):
                    self.mm(pg[:, 0:W], wg[:, kc, cc * 128:(cc + 1) * 128] if wg is not None else None, self.uT[:, kc, :], kc == 0, kc == 7, [wgb, self.uTb], [pgb])
                self.act(gs[cc][0], pg[:, 0:W], AF.Sigmoid, [pgb, self.pcb], [gs[cc][1]], bias=self.pcv(l, PC_GB + k * 8 + oc))
            banks = [self.bank("mm") for _ in range(4)]
            for j in range(KC // 8):
                wt, wtb = self.W.get(wname, l, j * 8, half * 512)
                for cc in range(4):
                    py, pyb = banks[cc]
                    for kk in range(8):
                        kc = j * 8 + kk
                        self.mm(py[:, 0:W], wt[:, kk, cc * 128:(cc + 1) * 128] if wt is not None else None, srcT[:, kc, :], kc == 0, kc == KC - 1, [wtb, srcb], [pyb])
            for cc in range(4):
                oc = half * 4 + cc
                py, pyb = banks[cc]
                if k == 0:
                    self.tt(self.mrg[:, oc, :], py[:, 0:W], gs[cc][0], ALU.mult, [pyb, gs[cc][1]], [self.mrgb[oc]])
                else:
                    t_, tb_ = tmp[cc % 2]
                    self.tt(t_, py[:, 0:W], gs[cc][0], ALU.mult, [pyb, gs[cc][1]], [tb_])
                    self.tt(self.mrg[:, oc, :], self.mrg[:, oc, :], t_, ALU.add, [tb_, self.mrgb[oc]], [self.mrgb[oc]])

    def conv_phase(self, l, ck):
        self.S.tag = 'conv_phase' + str((l, ck))
        W = self.TC
        ar = self.arena
        ar.reset()
        convT, _ = ar.alloc(8 * W, F32, "p (c w) -> p c w", c=8)
        convb = ar.bufs(8)
        actT, actb = ar.alloc(4 * W, BF16, "p (c w) -> p c w", c=8)
        glu = [ar.alloc(W + 32) for _ in range(2)]
        sig = [ar.alloc(W) for _ in range(2)]
        sq = [ar.alloc(W) for _ in range(2)]
        mean, meanb = ar.alloc(W)
        m2, m2b = ar.alloc(W)
        rstd, rstdb = ar.alloc(W)
        nmr, nmrb = ar.alloc(W)
        tmpn = [ar.alloc(W) for _ in range(2)]
        ps1, ps1b = self.bank("st")
        ps2, ps2b = self.bank("st")
        for half in range(2):
            wa, wab = self.W.get("w_in", l, 0, OFF_CONV + half * 512)
            wg, wgb = self.W.get("w_in", l, 0, OFF_CONV + 1024 + half * 512)
            for cc in range(4):
                c = half * 4 + cc
                g_, gb_ = glu[c % 2]
                s_, sb_ = sig[c % 2]
                q_, qb_ = sq[c % 2]
                pa, pab = self.bank("mm")
                pg, pgb = self.bank("mm")
                for kc in range(8):
                    self.mm(pa[:, 0:W], wa[:, kc, cc * 128:(cc + 1) * 128] if wa is not None else None, self.uT[:, kc, :], kc == 0, kc == 7, [wab, self.uTb], [pab])
                for kc in range(8):
                    self.mm(pg[:, 0:W], wg[:, kc, cc * 128:(cc + 1) * 128] if wg is not None else None, self.uT[:, kc, :], kc == 0, kc == 7, [wgb, self.uTb], [pgb])
                self.act(s_, pg[:, 0:W], AF.Sigmoid, [pgb], [sb_])
                self.act(g_[:, 0:30], self.ctail[:, l, c, :], AF.Identity, [self.ctailb[l][c]], [gb_])
                self.tt(g_[:, 30:30 + W], pa[:, 0:W], s_, ALU.mult, [pab, sb_], [gb_])
                cw = PC_CW + c * 31
                self.ts(convT[:, c, :], g_[:, 0:W], self.pcv(l, cw), self.pcv(l, PC_CB + c), ALU.mult, ALU.add, [gb_, self.pcb], [convb[c]])
                for j in range(1, 31):
                    self.stt(convT[:, c, :], g_[:, j:j + W], self.pcv(l, cw + j), convT[:, c, :], ALU.mult, ALU.add, [gb_, self.pcb, convb[c]], [convb[c]])
                self.act(self.ctail[:, l, c, :], g_[:, W:W + 30], AF.Identity, [gb_], [self.ctailb[l][c]])
                self.act(q_, convT[:, c, :], AF.Square, [convb[c]], [qb_])
                self.mm(ps1[:, 0:W], self.onesf[:], convT[:, c, :], c == 0, c == 7, [convb[c], self.onesfb], [ps1b])
                self.mm(ps2[:, 0:W], self.onesf[:], q_, c == 0, c == 7, [qb_, self.onesfb], [ps2b])
        if l == 0 and ck == 0:
            self.rawdump("glu7", glu[1][0][:, 0:W + 32], glu[1][1], W + 32)
            self.rawdump("sig7", sig[1][0], sig[1][1], W)
            self.rawdump("conv7", convT[:, 7, :], convb[7], W)
            self.rawdump("conv0", convT[:, 0, :], convb[0], W)
        self.act(mean, ps1[:, 0:W], AF.Identity, [ps1b], [meanb], scale=1.0 / D)
        self.tt(m2, mean, mean, ALU.mult, [meanb], [m2b])
        self.stt(rstd, ps2[:, 0:W], 1.0 / D, m2, ALU.mult, ALU.subtract, [ps2b, m2b], [rstdb])
        self.rsqrt(rstd, rstd, 1.0, [rstdb], [rstdb])
        self.stt(nmr, mean, -1.0, rstd, ALU.mult, ALU.mult, [meanb, rstdb], [nmrb])
        for c in range(8):
            t_, tb_ = tmpn[c % 2]
            self.tt(t_, convT[:, c, :], rstd, ALU.mult, [convb[c], rstdb], [tb_])
            self.tt(t_, t_, nmr, ALU.add, [tb_, nmrb], [tb_])
            self.act(actT[:, c, :], t_, AF.Silu, [tb_, self.pcb], [actb], scale=self.pcv(l, PC_LG + c), bias=self.pcv(l, PC_LB + c))
        if l == 0 and ck == 0:
            self.rawdump("mean", mean, meanb, W)
            self.rawdump("rstd", rstd, rstdb, W)
        self.dump("cact", actT, actb, ck) if l == 0 else None
        self.branch_out(l, 0, actT, actb, "w_conv_out", 8)
        self.dump("m0", self.mrg[:], self.mrgb, ck) if l == 0 else None

    def ssd_phase(self, l, ck):
        self.S.tag = 'ssd_phase' + str((l, ck))
        W, NT = self.TC, self.NT
        ar = self.arena
        ar.reset()
        xtok, xtokb = ar.alloc(NT * 1024, BF16, "p (t n) -> p t n", t=NT)
        zs, zsb = ar.alloc(NT * 1024, BF16, "p (t n) -> p t n", t=NT)
        yT, yTb = ar.alloc(8 * W, BF16, "p (c w) -> p c w", c=16)
        bcT, _ = ar.alloc(4 * W, BF16, "p (c w) -> p c w", c=8)
        bcb = ar.bufs(8)
        btok, btokb = ar.alloc(NT * 256, BF16, "p (t n) -> p t n", t=NT)
        xdt, xdtb = ar.alloc(1024, BF16)
        xdte, xdteb = ar.alloc(1024, BF16)
        yw, _ = ar.alloc(2048)
        ywb = ar.bufs(4)
        ytok, ytokb = ar.alloc(1024, BF16)
        stbf, stbfb = ar.alloc(1024, BF16)
        xw = [ar.alloc(W + 8) for _ in range(2)]
        acc = [ar.alloc(W) for _ in range(2)]
        xc = [ar.alloc(W // 2, BF16) for _ in range(2)]
        lt = [ar.alloc(512) for _ in range(2)]
        sc = [ar.alloc(256, BF16, "p (j q) -> p j q", j=4) for _ in range(2)]
        cbm, cbmb = ar.alloc(512, F32, "p (g q) -> p g q", g=4)
        t1 = [ar.alloc(512) for _ in range(2)]
        junk, junkb = ar.alloc(512)
        dtall, dtallb = ar.alloc(NT * 32, F32, "p (t h) -> p t h", t=NT)
        daall, daallb = ar.alloc(NT * 32, F32, "p (t h) -> p t h", t=NT)
        cs, csb = ar.alloc(32)
        dfs, dfsb = ar.alloc(32)
        cd, cdb = ar.alloc(32)
        dte, dteb = ar.alloc(32)
        ss, ssb = ar.alloc(8)
        rs4, rs4b = ar.alloc(8)

        for zt in range(4):
            wz, wzb = self.W.get("w_in", l, 0, OFF_Z + zt * 512)
            for t in range(NT):
                pz, pzb = self.bank("mm")
                for kc in range(8):
                    self.mm(pz[:, 0:512], self.uT[:, kc, t * 128:(t + 1) * 128], wz[:, kc, :] if wz is not None else None, kc == 0, kc == 7, [wzb, self.uTb], [pzb])
                self.act(zs[:, t, zt * 512:(zt + 1) * 512], pz[:, 0:512], AF.Silu, [pzb], [zsb])
        for tile in range(6):
            wx, wxb = self.W.get("w_in", l, 0, OFF_XBC + tile * 512)
            for cc in range(4):
                c = tile * 4 + cc
                xw_, xwb_ = xw[c % 2]
                a_, ab_ = acc[c % 2]
                pb_, pbb_ = self.bank("mm")
                for kc in range(8):
                    self.mm(pb_[:, 0:W], wx[:, kc, cc * 128:(cc + 1) * 128] if wx is not None else None, self.uT[:, kc, :], kc == 0, kc == 7, [wxb, self.uTb], [pbb_])
                self.act(xw_[:, 0:3], self.xtail[:, l, c, :], AF.Identity, [self.xtailb[l][c]], [xwb_])
                self.act(xw_[:, 3:3 + W], pb_[:, 0:W], AF.Identity, [pbb_], [xwb_])
                sw = PC_SW + c * 4
                self.ts(a_, xw_[:, 0:W], self.pcv(l, sw), self.pcv(l, PC_SB + c), ALU.mult, ALU.add, [xwb_, self.pcb], [ab_])
                for j in range(1, 4):
                    self.stt(a_, xw_[:, j:j + W], self.pcv(l, sw + j), a_, ALU.mult, ALU.add, [xwb_, self.pcb, ab_], [ab_])
                self.act(self.xtail[:, l, c, :], xw_[:, W:W + 3], AF.Identity, [xwb_], [self.xtailb[l][c]])
                if c < 20:
                    x_, xb_ = xc[c % 2]
                    if c < 16:
                        self.act(x_, a_, AF.Silu, [ab_], [xb_])
                        src, srcb = x_, xb_
                    else:
                        self.act(bcT[:, c - 16, :], a_, AF.Silu, [ab_], [bcb[c - 16]])
                        src, srcb = bcT[:, c - 16, :], bcb[c - 16]
                    pt, ptb = self.bank("tr")
                    ptv = pt[:].bitcast(BF16)
                    for t in range(NT):
                        self.PE(lambda e, o=ptv[:, t * 128:(t + 1) * 128], i=src[:, t * 128:(t + 1) * 128]: e.transpose(o, i, self.identb[:]), [srcb, self.identbb], [ptb])
                    if c < 16:
                        self.DVE(lambda e, c=c, ptv=ptv: e.tensor_copy(xtok[:, :, c * 128:(c + 1) * 128], ptv[:, 0:NT * 128].rearrange("p (t n) -> p t n", t=NT)), [ptb], [xtokb])
                    else:
                        self.DVE(lambda e, c=c, ptv=ptv: e.tensor_copy(btok[:, :, (c - 16) * 128:(c - 15) * 128], ptv[:, 0:NT * 128].rearrange("p (t n) -> p t n", t=NT)), [ptb], [btokb])
                else:
                    self.act(bcT[:, c - 16, :], a_, AF.Silu, [ab_], [bcb[c - 16]])
        wdt, wdtb = self.W.get("w_in", l, 0, OFF_DT, 8, 32)
        for t in range(NT):
            pd, pdb = self.bank("tr")
            for kc in range(8):
                self.mm(pd[:, 0:32], self.uT[:, kc, t * 128:(t + 1) * 128], wdt[:, kc, 0:32] if wdt is not None else None, kc == 0, kc == 7, [wdtb, self.uTb], [pdb])
            self.tt(dtall[:, t, :], pd[:, 0:32], self.prv(l, PR_DTB, 32), ALU.add, [pdb, self.prb], [dtallb])
        dflat = dtall.rearrange("p t h -> p (t h)")
        self.act(dflat, dflat, AF.Exp, [dtallb], [dtallb])
        self.act(dflat, dflat, AF.Ln, [dtallb], [dtallb], bias=1.0)
        for t in range(NT):
            self.tt(daall[:, t, :], dtall[:, t, :], self.arow[:, l * 32:(l + 1) * 32], ALU.mult, [dtallb, self.arowb], [daallb])

        stl = self.state[:, l, :]
        stb = self.stateb[l]
        for t in range(NT):
            tsl = slice(t * 128, (t + 1) * 128)
            dtv = dtall[:, t, :]
            dav = daall[:, t, :]
            self.act(stbf, stl, AF.Identity, stb, [stbfb])
            pc_, pcb_ = self.bank("tr")
            self.mm(pc_[:, 0:32], self.triu, dav, True, True, [self.cstb, daallb], [pcb_])
            self.mm(pc_[:, 32:64], self.onesf[:], dav, True, True, [self.onesfb, daallb], [pcb_])
            self.act(cs, pc_[:, 0:32], AF.Identity, [pcb_], [csb])
            self.act(dfs, pc_[:, 0:32], AF.Exp, [pcb_], [dfsb])
            self.act(cd, pc_[:, 32:64], AF.Exp, [pcb_], [cdb])
            self.tt(dte, pc_[:, 32:64], cs, ALU.subtract, [pcb_, csb], [dteb])
            self.act(dte, dte, AF.Exp, [dteb], [dteb])
            x3 = xtok[:, t, :].rearrange("p (h d) -> p h d", h=32)
            self.tt(xdt.rearrange("p (h d) -> p h d", h=32), x3, dtv.unsqueeze(2).to_broadcast([128, 32, 64]), ALU.mult, [xtokb, dtallb], [xdtb])
            self.tt(xdte.rearrange("p (h d) -> p h d", h=32), xdt.rearrange("p (h d) -> p h d", h=32), dte.unsqueeze(2).to_broadcast([128, 32, 64]), ALU.mult, [xdtb, dteb], [xdteb])
            pcb2, pcb2b = self.bank("mm")
            for g in range(4):
                self.mm(pcb2[:, g * 128:(g + 1) * 128], bcT[:, g, tsl], bcT[:, 4 + g, tsl], True, True, [bcb[g], bcb[4 + g]], [pcb2b])
            self.tt(cbm, pcb2[:, 0:512].rearrange("p (g q) -> p g q", g=4), self.triu.unsqueeze(1).to_broadcast([128, 4, 128]), ALU.mult, [pcb2b, self.cstb], [cbmb])
            self.DVE(lambda e: e.memset(ss, 0.0), [], [ssb])
            for g in range(4):
                py, pyb = self.bank("mm")
                for hb in range(2):
                    h0 = g * 8 + hb * 4
                    lt_, ltb_ = lt[hb]
                    sc_, scb_ = sc[hb]
                    pq, pqb = self.bank("st")
                    for j in range(4):
                        self.mm(pq[:, j * 128:(j + 1) * 128], dav[:, h0 + j:h0 + j + 1].to_broadcast([128, 128]), self.triu, True, True, [daallb, self.cstb], [pqb])
                    lt3 = lt_.rearrange("p (j q) -> p j q", j=4)
                    self.tt(lt3, pq[:, 0:512].rearrange("p (j q) -> p j q", j=4), cs[:, h0:h0 + 4].unsqueeze(2).to_broadcast([128, 4, 128]), ALU.subtract, [pqb, csb], [ltb_])
                    self.act(lt_, lt_, AF.Relu, [ltb_], [ltb_], scale=-1.0)
                    self.act(lt_, lt_, AF.Exp, [ltb_], [ltb_], scale=-1.0)
                    self.tt(sc_, lt3, cbm[:, g, :].unsqueeze(1).to_broadcast([128, 4, 128]), ALU.mult, [ltb_, cbmb], [scb_])
                    for j in range(4):
                        h = h0 + j
                        self.mm(py[:, (hb * 4 + j) * 64:(hb * 4 + j + 1) * 64], sc_[:, j, :], xdt[:, h * 64:(h + 1) * 64], True, True, [scb_, xdtb], [pyb])
                po, pob = self.bank("mm")
                self.mm(po[:, 0:512], bcT[:, 4 + g, tsl], stbf[:, g * 512:(g + 1) * 512], True, True, [bcb[4 + g], stbfb], [pob])
                ta, tab = t1[0]
                tb, tbb = t1[1]
                ywg = yw[:, g * 512:(g + 1) * 512]
                self.tt(ta.rearrange("p (h d) -> p h d", h=8), po[:, 0:512].rearrange("p (h d) -> p h d", h=8), dfs[:, g * 8:(g + 1) * 8].unsqueeze(2).to_broadcast([128, 8, 64]), ALU.mult, [pob, dfsb], [tab])
                self.tt(ywg, py[:, 0:512], ta, ALU.add, [pyb, tab], [ywb[g]])
                self.tt(tb.rearrange("p (h d) -> p h d", h=8), xtok[:, t, g * 512:(g + 1) * 512].rearrange("p (h d) -> p h d", h=8),
                        self.prv(l, PR_D + g * 8, 8).unsqueeze(2).to_broadcast([128, 8, 64]), ALU.mult, [xtokb, self.prb], [tbb])
                self.tt(ywg, ywg, tb, ALU.add, [ywb[g], tbb], [ywb[g]])
                self.tt(ywg, ywg, zs[:, t, g * 512:(g + 1) * 512], ALU.mult, [ywb[g], zsb], [ywb[g]])
                self.act(junk, ywg, AF.Square, [ywb[g]], [junkb, ssb], accum_out=ss[:, g:g + 1])
                pn, pnb = self.bank("mm")
                self.mm(pn[:, 0:512], btok[:, t, g * 128:(g + 1) * 128], xdte[:, g * 512:(g + 1) * 512], True, True, [btokb, xdteb], [pnb])
                sg = stl[:, g * 512:(g + 1) * 512]
                self.tt(sg.rearrange("p (h d) -> p h d", h=8), sg.rearrange("p (h d) -> p h d", h=8), cd[:, g * 8:(g + 1) * 8].unsqueeze(2).to_broadcast([128, 8, 64]), ALU.mult, [stb[g], cdb, stbfb], [stb[g]])
                self.tt(sg, sg, pn[:, 0:512], ALU.add, [stb[g], pnb], [stb[g]])
            if l == 0 and ck == 0 and t == 0:
                self.rawdump("dt", dtall.rearrange("p t h -> p (t h)"), dtallb, NT * 32)
                self.rawdump("da", daall.rearrange("p t h -> p (t h)"), daallb, NT * 32)
                self.rawdump("cs", cs, csb, 32)
                self.rawdump("dfs", dfs, dfsb, 32)
                self.rawdump("cd", cd, cdb, 32)
                self.rawdump("dte", dte, dteb, 32)
                self.rawdump("xtok", xtok[:, 0, :], xtokb, 2048)
                self.rawdump("btok", btok[:, 0, :], btokb, 512)
                self.rawdump("xdt", xdt, xdtb, 2048)
                self.rawdump("zs", zs[:, 0, :], zsb, 2048)
                self.rawdump("cbm", cbm.rearrange("p g q -> p (g q)"), cbmb, 512)
                self.rawdump("lt", lt[1][0], lt[1][1], 512)
                self.rawdump("yw", yw, ywb, 2048)
                self.rawdump("ss", ss, ssb, 8)
                self.rawdump("bc", bcT.rearrange("p c w -> p (c w)"), bcb, 8 * W)
            self.rsqrt(rs4[:, 0:4], ss[:, 0:4], 1.0 / 512, [ssb], [rs4b])
            self.tt(ytok.rearrange("p (g n) -> p g n", g=4), yw.rearrange("p (g n) -> p g n", g=4), rs4[:, 0:4].unsqueeze(2).to_broadcast([128, 4, 512]), ALU.mult, ywb + [rs4b], [ytokb])
            for half in range(2):
                pt, ptb = self.bank("tr")
                ptv = pt[:].bitcast(BF16)
                for j in range(8):
                    c = half * 8 + j
                    self.PE(lambda e, o=ptv[:, j * 128:(j + 1) * 128], i=ytok[:, c * 128:(c + 1) * 128]: e.transpose(o, i, self.identb[:]), [ytokb, self.identbb], [ptb])
                self.tt(yT[:, half * 8:(half + 1) * 8, tsl], ptv[:, 0:1024].rearrange("p (c q) -> p c q", c=8),
                        self.pcv(l, PC_NG + half * 8, 8).unsqueeze(2).to_broadcast([128, 8, 128]), ALU.mult, [ptb, self.pcb], [yTb])
        self.dump("ybin", yT, yTb, ck, 16) if l == 0 else None
        WW = 512 // 2
        scr = [(lt[0][0][:, 0:W], lt[0][1]), (lt[0][0][:, WW:WW + W], lt[0][1]), (lt[1][0][:, 0:W], lt[1][1]), (lt[1][0][:, WW:WW + W], lt[1][1]),
               (t1[0][0][:, 0:W], t1[0][1]), (t1[1][0][:, 0:W], t1[1][1])] if W <= 256 else None
        self.branch_out(l, 1, yT, yTb, "w_ssd_out", 16, scratch=scr)
        self.dump("m1", self.mrg[:], self.mrgb, ck) if l == 0 else None

    def qknorm_chunk(self, pq, pqb, hd_scale, stat_lhsT, stat_b, sq, rs):
        pass

    def swa_phase(self, l, ck):
        self.S.tag = 'swa_phase' + str((l, ck))
        W, NT = self.TC, self.NT
        ar = self.arena
        ar.reset()
        qT, qTb = ar.alloc(4 * W, BF16, "p (c w) -> p c w", c=8)
        oT, oTb = ar.alloc(4 * W, BF16, "p (c w) -> p c w", c=8)
        KW = 128 + W
        kT, kTb = ar.alloc(2 * KW, BF16, "p (g k) -> p g k", g=4)
        vt, vtb = ar.alloc((1 + NT) * 256, BF16, "p (t n) -> p t n", t=1 + NT)
        sq = [ar.alloc(W) for _ in range(2)]
        rs = [ar.alloc(W) for _ in range(2)]
        pts = [[ar.alloc(256, BF16) for _ in range(2)] for _ in range(2)]
        tbs = [ar.alloc(512) for _ in range(2)]
        dtot, dtotb = ar.alloc(512)
        rd, rdb = ar.alloc(512)
        self.DVE(lambda e: e.tensor_copy(kT[:, :, 0:128], self.kprev[:, l]), [self.kprevb[l]], [kTb])
        self.DVE(lambda e: e.tensor_copy(vt[:, 0, :], self.vprev[:, l, :]), [self.vprevb[l]], [vtb])
        kT2 = kT.rearrange("p (m e) k -> p m e k", e=2)
        self.DVE(lambda e: e.memset(kT2[64:128, :, 0, 128:KW], 0.0), [], [kTb])
        self.DVE(lambda e: e.memset(kT2[0:64, :, 1, 128:KW], 0.0), [], [kTb])
        gq = self.pcv(l, PC_QN)
        gk = self.pcv(l, PC_KN)
        for tile in range(2):
            wq, wqb = self.W.get("w_in", l, 0, OFF_Q + tile * 512, kind="qperm")
            for cc in range(4):
                c = tile * 4 + cc
                s_, sb_ = sq[c % 2]
                r_, rb_ = rs[c % 2]
                pq, pqb = self.bank("mm")
                for kc in range(8):
                    self.mm(pq[:, 0:W], wq[:, kc, cc * 128:(cc + 1) * 128], self.uT[:, kc, :], kc == 0, kc == 7, [wqb, self.uTb], [pqb])
                self.act(s_, pq[:, 0:W], AF.Square, [pqb], [sb_])
                pst, pstb = self.bank("st")
                self.mm(pst[:, 0:W], self.blk64, s_, True, True, [self.cstb, sb_], [pstb])
                self.rsqrt(r_, pst[:, 0:W], 1.0 / 64, [pstb], [rb_])
                self.stt(qT[:, c, :], pq[:, 0:W], gq, r_, ALU.mult, ALU.mult, [pqb, rb_, self.pcb], [qTb])
        wkv, wkvb = self.W.get("w_in", l, 0, OFF_K)
        for m in range(2):
            s_, sb_ = sq[m % 2]
            r_, rb_ = rs[m % 2]
            pk, pkb = self.bank("mm")
            for kc in range(8):
                self.mm(pk[:, 0:W], wkv[:, kc, m * 128:(m + 1) * 128], self.uT[:, kc, :], kc == 0, kc == 7, [wkvb, self.uTb], [pkb])
            self.act(s_, pk[:, 0:W], AF.Square, [pkb], [sb_])
            pst, pstb = self.bank("st")
            self.mm(pst[:, 0:W], self.blk64, s_, True, True, [self.cstb, sb_], [pstb])
            self.rsqrt(r_, pst[:, 0:W], 1.0 / 64, [pstb], [rb_])
            self.stt(kT[0:64, 2 * m, 128:KW], pk[0:64, 0:W], gk[0:64], r_[0:64], ALU.mult, ALU.mult, [pkb, rb_, self.pcb], [kTb])
            self.stt(kT[64:128, 2 * m + 1, 128:KW], pk[64:128, 0:W], gk[64:128], r_[64:128], ALU.mult, ALU.mult, [pkb, rb_, self.pcb], [kTb])
        for t in range(NT):
            pv, pvb = self.bank("mm")
            for kc in range(8):
                self.mm(pv[:, 0:256], self.uT[:, kc, t * 128:(t + 1) * 128], wkv[:, kc, 256:512], kc == 0, kc == 7, [wkvb, self.uTb], [pvb])
            self.DVE(lambda e, t=t, pv=pv: e.tensor_copy(vt[:, 1 + t, :].rearrange("p (g e d) -> p g e d", g=4, e=2),
                                                       pv[:, 0:256].rearrange("p (g d) -> p g d", g=4).unsqueeze(2).to_broadcast([128, 4, 2, 64])), [pvb], [vtb])
        for t in range(NT):
            gblk = ck * NT + t
            tsl = slice(t * 128, (t + 1) * 128)
            sides = ([0] if gblk > 0 else []) + [1]
            for g in range(4):
                for side in sides:
                    ko = t * 128 + side * 128
                    pS, pSb = self.bank("mm")
                    for i in range(4):
                        self.mm(pS[:, i * 128:(i + 1) * 128], kT[:, g, ko:ko + 128], qT[:, 4 * (g // 2) + i, tsl], True, True, [kTb, qTb], [pSb])
                    tb_, tbb_ = tbs[side]
                    p_, pb_ = pts[g % 2][side]
                    self.stt(tb_, pS[:, 0:512], 0.125, self.bias[:, side, g * 4:(g + 1) * 4, :].rearrange("p h q -> p (h q)"), ALU.mult, ALU.add, [pSb, self.biasb], [tbb_])
                    self.act(p_, tb_, AF.Exp, [tbb_], [pb_])
                pO, pOb = self.bank("mm")
                pD, pDb = self.bank("st")
                for i, side in enumerate(sides):
                    p_, pb_ = pts[g % 2][side]
                    self.mm(pO[:, 0:512], vt[:, t + side, g * 128:(g + 1) * 128], p_, i == 0, i == len(sides) - 1, [vtb, pb_], [pOb])
                for i, side in enumerate(sides):
                    p_, pb_ = pts[g % 2][side]
                    self.mm(pD[:, 0:512], self.onesb[:], p_, i == 0, i == len(sides) - 1, [self.onesbb, pb_], [pDb])
                self.tt(dtot.rearrange("p (h q) -> p h q", h=4), pD[:, 0:512].rearrange("p (h q) -> p h q", h=4),
                        self.esink[:, l * 16 + g * 4:l * 16 + g * 4 + 4].unsqueeze(2).to_broadcast([128, 4, 128]), ALU.add, [pDb, self.esinkb], [dtotb])
                self.DVE(lambda e: e.reciprocal(rd, dtot), [dtotb], [rdb])
                for e_ in range(2):
                    rows = slice(64 * e_, 64 * e_ + 64)
                    self.tt(oT[rows, 2 * g:2 * g + 2, tsl], pO[rows, 0:512].rearrange("p (c e q) -> p c e q", c=2, e=2)[:, :, e_, :],
                            rd[rows, 0:512].rearrange("p (c e q) -> p c e q", c=2, e=2)[:, :, e_, :], ALU.mult, [pOb, rdb], [oTb])
        self.DVE(lambda e: e.tensor_copy(self.kprev[:, l], kT[:, :, W:W + 128]), [kTb], [self.kprevb[l]])
        self.DVE(lambda e: e.tensor_copy(self.vprev[:, l, :], vt[:, NT, :]), [vtb], [self.vprevb[l]])
        self.dump("ycin", oT, oTb, ck) if l == 0 else None
        self.branch_out(l, 2, oT, oTb, "w_attn_out", 8)
        self.dump("m2", self.mrg[:], self.mrgb, ck) if l == 0 else None

    def proj_add(self, l, wname, srcT, srcb):
        W = self.TC
        for half in range(2):
            wt, wtb = self.W.get(wname, l, 0, half * 512)
            for cc in range(4):
                oc = half * 4 + cc
                py, pyb = self.bank("mm")
                for kc in range(8):
                    self.mm(py[:, 0:W], wt[:, kc, cc * 128:(cc + 1) * 128] if wt is not None else None, srcT[:, kc, :], kc == 0, kc == 7, [wtb, srcb], [pyb])
                self.tt(self.hT[:, oc, :], self.hT[:, oc, :], py[:, 0:W], ALU.add, [pyb, self.hTb[oc]], [self.hTb[oc]])

    def mix_phase(self, l, ck):
        self.S.tag = 'mix_phase' + str((l, ck))
        W = self.TC
        ar = self.arena
        ar.reset()
        mb, mbb = ar.alloc(4 * W, BF16, "p (c w) -> p c w", c=8)
        for c in range(8):
            self.act(mb[:, c, :], self.mrg[:, c, :], AF.Identity, [self.mrgb[c]], [mbb])
        self.proj_add(l, "w_mix_out", mb, mbb)
        self.dump("h1", self.hT[:], self.hTb, ck) if l == 0 else None

    def kv_precompute(self, l):
        self.S.tag = 'kv_precompute' + str((l))
        ar = self.arena
        ar.reset()
        memx, memxb = ar.alloc(2048, F32, "p (t d) -> p t d", t=2)
        memn, memnb = ar.alloc(1024, BF16, "p (t d) -> p t d", t=2)
        memT, memTb = ar.alloc(1024, BF16, "p (c k) -> p c k", c=8)
        kvo, kvob = ar.alloc(2048, BF16)
        junk, junkb = ar.alloc(1024)
        ms, msb = ar.alloc(8)
        sq = [ar.alloc(256) for _ in range(2)]
        rs, rsb = ar.alloc(256)
        src = self.mem.rearrange("(t p) d -> p t d", p=128)
        self.S.dma(self.kvltrack, lambda e: e.dma_start(out=memx, in_=src), writes=[memxb])
        self.DVE(lambda e: e.memset(ms, 0.0), [], [msb])
        for kt in range(2):
            self.act(junk, memx[:, kt, :], AF.Square, [memxb], [junkb, msb], accum_out=ms[:, kt:kt + 1])
        self.rsqrt(ms[:, 0:2], ms[:, 0:2], 1.0 / D, [msb], [msb])
        for kt in range(2):
            self.ts(memn[:, kt, :], memx[:, kt, :], ms[:, kt:kt + 1], None, ALU.mult, None, [memxb, msb], [memnb])
        for kt in range(2):
            pt, ptb = self.bank("tr")
            ptv = pt[:].bitcast(BF16)
            for c in range(8):
                self.PE(lambda e, o=ptv[:, c * 128:(c + 1) * 128], i=memn[:, kt, c * 128:(c + 1) * 128]: e.transpose(o, i, self.identb[:]), [memnb, self.identbb], [ptb])
            self.tt(memT[:, :, kt * 128:(kt + 1) * 128], ptv[:, 0:1024].rearrange("p (c k) -> p c k", c=8),
                    self.pcv(l, PC_NM, 8).unsqueeze(2).to_broadcast([128, 8, 128]), ALU.mult, [ptb, self.pcb], [memTb])
        kT = kvo[:, 0:2048].rearrange("p (c k) -> p c k", c=8)
        vv = kvo[:, 2048:4096].rearrange("p (t d) -> p t d", t=2)
        for tile in range(2):
            wk, wkb = self.W.get("w_xkv", l, 0, tile * 512)
            for hh2 in range(2):
                hh = tile * 2 + hh2
                pk = [self.bank("mm") for _ in range(2)]
                pst, pstb = self.bank("st")
                for j in range(2):
                    cc = hh2 * 2 + j
                    for kc in range(8):
                        self.mm(pk[j][0][:, 0:256], wk[:, kc, cc * 128:(cc + 1) * 128] if wk is not None else None, memT[:, kc, :], kc == 0, kc == 7, [wkb, memTb], [pk[j][1]])
                    self.act(sq[j][0], pk[j][0][:, 0:256], AF.Square, [pk[j][1]], [sq[j][1]])
                    self.mm(pst[:, 0:256], self.onesf[:], sq[j][0], j == 0, j == 1, [self.onesfb, sq[j][1]], [pstb])
                self.rsqrt(rs, pst[:, 0:256], 1.0 / 256, [pstb], [rsb])
                for j in range(2):
                    self.stt(kT[:, 2 * hh + j, :], pk[j][0][:, 0:256], self.pcv(l, PC_XK + j), rs, ALU.mult, ALU.mult, [pk[j][1], rsb, self.pcb], [kvob])
        for tile in range(2):
            wv, wvb = self.W.get("w_xkv", l, 0, 1024 + tile * 512)
            for kt in range(2):
                pv, pvb = self.bank("mm")
                for kc in range(8):
                    self.mm(pv[:, 0:512], memT[:, kc, kt * 128:(kt + 1) * 128], wv[:, kc, :] if wv is not None else None, kc == 0, kc == 7, [wvb, memTb], [pvb])
                self.act(vv[:, kt, tile * 512:(tile + 1) * 512], pv[:, 0:512], AF.Identity, [pvb], [kvob])
        self.S.dma(self.kvtrack, lambda e: e.dma_start(out=self.kvs[l], in_=kvo), reads=[kvob], writes=[self.kvs_b[l]])

    def xattn_phase(self, l, ck):
        self.S.tag = 'xattn_phase' + str((l, ck))
        W = self.TC
        self.rmsnorm(l, PC_NX)
        ar = self.arena
        ar.reset()
        kvb, kvbb = ar.alloc(2048, BF16)
        qT, qTb = ar.alloc(4 * W, BF16, "p (c w) -> p c w", c=8)
        oT, oTb = ar.alloc(4 * W, BF16, "p (c w) -> p c w", c=8)
        sq = [ar.alloc(W) for _ in range(2)]
        rs, rsb = ar.alloc(W)
        rd, rdb = ar.alloc(W)
        pts = [[ar.alloc(W // 2, BF16) for _ in range(2)] for _ in range(2)]
        self.S.dma(self.kvltrack, lambda e: e.dma_start(out=kvb, in_=self.kvs[l]), reads=[self.kvs_b[l]], writes=[kvbb])
        kT = kvb[:, 0:2048].rearrange("p (c k) -> p c k", c=8)
        vv = kvb[:, 2048:4096].rearrange("p (t d) -> p t d", t=2)
        for tile in range(2):
            wq, wqb = self.W.get("w_xq", l, 0, tile * 512)
            for hh2 in range(2):
                hh = tile * 2 + hh2
                pq = [self.bank("mm") for _ in range(2)]
                pst, pstb = self.bank("st")
                for j in range(2):
                    cc = hh2 * 2 + j
                    for kc in range(8):
                        self.mm(pq[j][0][:, 0:W], wq[:, kc, cc * 128:(cc + 1) * 128] if wq is not None else None, self.uT[:, kc, :], kc == 0, kc == 7, [wqb, self.uTb], [pq[j][1]])
                    self.act(sq[j][0], pq[j][0][:, 0:W], AF.Square, [pq[j][1]], [sq[j][1]])
                    self.mm(pst[:, 0:W], self.onesf[:], sq[j][0], j == 0, j == 1, [self.onesfb, sq[j][1]], [pstb])
                self.rsqrt(rs, pst[:, 0:W], 1.0 / 256, [pstb], [rsb])
                for j in range(2):
                    self.stt(qT[:, 2 * hh + j, :], pq[j][0][:, 0:W], self.pcv(l, PC_XQ + j), rs, ALU.mult, ALU.mult, [pq[j][1], rsb, self.pcb], [qTb])
        for hh in range(4):
            pp = pts[hh % 2]
            for kt in range(2):
                pS, pSb = self.bank("mm")
                for j in range(2):
                    self.mm(pS[:, 0:W], kT[:, 2 * hh + j, kt * 128:(kt + 1) * 128], qT[:, 2 * hh + j, :], j == 0, j == 1, [kvbb, qTb], [pSb])
                self.act(pp[kt][0], pS[:, 0:W], AF.Exp, [pSb], [pp[kt][1]], scale=1.0 / 16)
            pD, pDb = self.bank("st")
            for kt in range(2):
                self.mm(pD[:, 0:W], self.onesb[:], pp[kt][0], kt == 0, kt == 1, [self.onesbb, pp[kt][1]], [pDb])
            self.DVE(lambda e, pD=pD: e.reciprocal(rd, pD[:, 0:W]), [pDb], [rdb])
            for dc in range(2):
                c = 2 * hh + dc
                pO, pOb = self.bank("mm")
                for kt in range(2):
                    self.mm(pO[:, 0:W], vv[:, kt, c * 128:(c + 1) * 128], pp[kt][0], kt == 0, kt == 1, [kvbb, pp[kt][1]], [pOb])
                self.tt(oT[:, c, :], pO[:, 0:W], rd, ALU.mult, [pOb, rdb], [oTb])
        self.proj_add(l, "w_xo", oT, oTb)
        self.dump("h2", self.hT[:], self.hTb, ck) if l == 0 else None

    def mlp_phase(self, l, ck):
        self.S.tag = 'mlp_phase' + str((l, ck))
        W = self.TC
        self.rmsnorm(l, PC_NMLP)
        ar = self.arena
        ar.reset()
        actT, _ = ar.alloc(16 * W, BF16, "p (c w) -> p c w", c=32)
        actb = ar.bufs(4)
        rl = [ar.alloc(W) for _ in range(2)]
        for tile in range(8):
            wu, wub = self.W.get("w_mlp_up", l, 0, tile * 512)
            for cc in range(4):
                hc = tile * 4 + cc
                r_, rb_ = rl[hc % 2]
                pu, pub = self.bank("mm")
                for kc in range(8):
                    self.mm(pu[:, 0:W], wu[:, kc, cc * 128:(cc + 1) * 128] if wu is not None else None, self.uT[:, kc, :], kc == 0, kc == 7, [wub, self.uTb], [pub])
                self.act(r_, pu[:, 0:W], AF.Relu, [pub], [rb_])
                self.tt(actT[:, hc, :], r_, r_, ALU.mult, [rb_], [actb[hc // 8]])
        for half in range(2):
            banks = [self.bank("mm") for _ in range(4)]
            for j in range(4):
                wd_, wdb_ = self.W.get("w_mlp_down", l, j * 8, half * 512)
                for cc in range(4):
                    py, pyb = banks[cc]
                    for kk in range(8):
                        self.mm(py[:, 0:W], wd_[:, kk, cc * 128:(cc + 1) * 128] if wd_ is not None else None, actT[:, j * 8 + kk, :], j == 0 and kk == 0, j == 3 and kk == 7, [wdb_, actb[j]], [pyb])
            for cc in range(4):
                oc = half * 4 + cc
                py, pyb = banks[cc]
                self.tt(self.hT[:, oc, :], self.hT[:, oc, :], py[:, 0:W], ALU.add, [pyb, self.hTb[oc]], [self.hTb[oc]])
        self.dump("h3", self.hT[:], self.hTb, ck) if l == 0 else None


HPERM = list(range(16))


def host_consts():
    c = np.zeros((128, 512), np.float32)
    c[:, 0:128] = np.eye(128, dtype=np.float32)
    c[:, 128:256] = np.triu(np.ones((128, 128), np.float32))
    c[:, 256:384] = np.tril(np.ones((128, 128), np.float32), -1) * NEG
    blk = np.zeros((128, 128), np.float32)
    blk[0:64, 0:64] = 1.0
    blk[64:128, 64:128] = 1.0
    c[:, 384:512] = blk
    return c


def host_swab(rel_table):
    qi = np.arange(128)[:, None] + 128
    kj = np.arange(256)[None, :]
    dist = qi - kj
    max_exact = 16
    d = np.maximum(dist, 1).astype(np.float32)
    large = max_exact + (np.log(d / np.float32(max_exact)) / np.float32(math.log(128 / max_exact)) * np.float32(32 - max_exact)).astype(np.int32)
    large = np.minimum(large, 31)
    bucket = np.where(dist < max_exact, np.maximum(dist, 0), large)
    valid = (dist >= 0) & (dist < 128)
    bias = rel_table[bucket]
    bias = np.where(valid[:, :, None], bias, np.float32(NEG)).astype(np.float32)
    out = np.zeros((128, 2, 16, 128), np.float32)
    for side in range(2):
        blkb = bias[:, side * 128:(side + 1) * 128, :]
        out[:, side] = np.transpose(blkb[:, :, HPERM], (1, 2, 0))
    return out.reshape(128, 2 * 16 * 128)


def colv(v):
    return np.ascontiguousarray(np.asarray(v, np.float32).reshape(-1, 128).T)


def host_params(inp, L):
    pcol = np.zeros((128, L, NPC), np.float32)
    prow = np.zeros((128, L, NPR), np.float32)
    for l in range(L):
        pc = pcol[:, l]
        pc[:, PC_NMIX:PC_NMIX + 8] = colv(inp["norm_mix"][l])
        for k in range(3):
            pc[:, PC_GB + k * 8:PC_GB + k * 8 + 8] = colv(inp["gate_bias"][l, k])
        cw = inp["conv_dw_w"][l]
        pc[:, PC_CW:PC_CW + 248] = np.transpose(cw.reshape(31, 8, 128), (2, 1, 0)).reshape(128, 248)
        pc[:, PC_CB:PC_CB + 8] = colv(inp["conv_dw_b"][l])
        pc[:, PC_LG:PC_LG + 8] = colv(inp["conv_ln_g"][l])
        pc[:, PC_LB:PC_LB + 8] = colv(inp["conv_ln_b"][l])
        sw = inp["ssd_conv_w"][l]
        pc[:, PC_SW:PC_SW + 96] = np.transpose(sw.reshape(4, 24, 128), (2, 1, 0)).reshape(128, 96)
        pc[:, PC_SB:PC_SB + 24] = colv(inp["ssd_conv_b"][l])
        pc[:, PC_NG:PC_NG + 16] = colv(inp["ssd_norm_g"][l])
        pc[:, PC_QN] = np.tile(inp["attn_q_norm"][l], 2)
        pc[:, PC_KN] = np.tile(inp["attn_k_norm"][l], 2)
        pc[:, PC_NX:PC_NX + 8] = colv(inp["norm_xattn"][l])
        pc[:, PC_NM:PC_NM + 8] = colv(inp["norm_mem"][l])
        pc[:, PC_XQ:PC_XQ + 2] = colv(inp["xattn_q_norm"][l])
        pc[:, PC_XK:PC_XK + 2] = colv(inp["xattn_k_norm"][l])
        pc[:, PC_NMLP:PC_NMLP + 8] = colv(inp["norm_mlp"][l])
        pr = prow[:, l]
        pr[:, PR_DTB:PR_DTB + 32] = inp["ssd_dt_bias"][l][None, :]
        pr[:, PR_ALOG:PR_ALOG + 32] = inp["ssd_A_log"][l][None, :]
        pr[:, PR_D:PR_D + 32] = inp["ssd_D"][l][None, :]
        pr[:, PR_SINK:PR_SINK + 16] = inp["attn_sinks"][l][HPERM][None, :]
    return pcol.reshape(128, L * NPC), prow.reshape(128, L * NPR)


def make_in_maps(inp, T, L, batches):
    consts = host_consts()
    swab = host_swab(np.asarray(inp["rel_table"], np.float32))
    pcol, prow = host_params(inp, L)
    shared = {"consts": consts, "swab": swab, "pcol": pcol, "prow": prow}
    for n in WSHAPES:
        shared[n] = np.ascontiguousarray(np.asarray(inp[n], np.float32)[:L])
    maps = []
    for b in batches:
        m = dict(shared)
        m["x"] = np.ascontiguousarray(np.asarray(inp["x"], np.float32)[b, :T])
        m["mem"] = np.ascontiguousarray(np.asarray(inp["mem"], np.float32)[b])
        maps.append(m)
    return maps


_NC_CACHE = {}


def kernel(**inputs):
    T, L, TC = 4096, 4, 256
    key = (T, L, TC)
    if key not in _NC_CACHE:
        _NC_CACHE[key] = KB(T, L, TC).build()
    nc = _NC_CACHE[key]
    maps = make_in_maps(inputs, T, L, list(range(8)))
    res = run_bass_kernel_spmd(nc, maps, core_ids=list(range(8)))
    return np.stack([np.asarray(r["y"], np.float32) for r in res.results], axis=0)
```

```python
import math
from contextlib import ExitStack

import numpy as np
import concourse.bass as bass
import concourse.mybir as mybir
from concourse.bass_utils import run_bass_kernel_spmd

F32 = mybir.dt.float32
BF16 = mybir.dt.bfloat16
ALU = mybir.AluOpType
AF = mybir.ActivationFunctionType

D = 1024
EPS = 1e-6
OFF_CONV, OFF_Z, OFF_XBC, OFF_DT, OFF_Q, OFF_K, OFF_V, OFF_GATE, IN_COLS = 0, 2048, 4096, 7168, 7200, 8224, 8480, 8736, 11808
NEG = -30000.0

WSHAPES = {
    "w_in": (1024, IN_COLS), "w_conv_out": (1024, 1024), "w_ssd_out": (2048, 1024), "w_attn_out": (1024, 1024),
    "w_mix_out": (1024, 1024), "w_xq": (1024, 1024), "w_xkv": (1024, 2048), "w_xo": (1024, 1024),
    "w_mlp_up": (1024, 4096), "w_mlp_down": (4096, 1024),
}
WORDER = ["w_in", "w_conv_out", "w_ssd_out", "w_attn_out", "w_mix_out", "w_xkv", "w_xq", "w_xo", "w_mlp_up", "w_mlp_down"]

PC_NMIX, PC_GB, PC_CW, PC_CB, PC_LG, PC_LB, PC_SW, PC_SB, PC_NG, PC_QN, PC_KN, PC_NX, PC_NM, PC_XQ, PC_XK, PC_NMLP, NPC = (
    0, 8, 32, 280, 288, 296, 304, 400, 424, 440, 441, 442, 450, 458, 460, 462, 470)
PR_DTB, PR_ALOG, PR_D, PR_SINK, NPR = 0, 32, 64, 96, 112

EPOCH = 6000


class Track:
    def __init__(self, name, is_dma=False):
        self.name = name
        self.is_dma = is_dma
        self.n = 0
        self.nsig = 0
        self.sems = []


class Buf:
    __slots__ = ("name", "w", "r", "excl")

    def __init__(self, name="b", excl=False):
        self.name = name
        self.w = None
        self.r = {}
        self.excl = excl


class Op:
    __slots__ = ("eng", "track", "seq", "eseq", "fn", "waits", "signal", "sem", "val", "tag")


class Sched:
    def __init__(self, nc, stack, dry=False):
        self.nc = nc
        self.stack = stack
        self.dry = dry
        self.ops = []
        self.engs = {"pe": nc.tensor, "act": nc.scalar, "dve": nc.vector, "pool": nc.gpsimd, "sp": nc.sync}
        self.etrack = {k: Track(k) for k in self.engs}
        self.ecount = {k: 0 for k in self.engs}
        self.seen = {k: {} for k in self.engs}
        self.nsem = 0
        self.epdone = set()

    def new_sem(self, name):
        self.nsem += 1
        return self.stack.enter_context(self.nc.semaphore(f"{name}_{self.nsem}"))

    def _record(self, eng, track, fn, reads, writes):
        if self.dry:
            return None
        op = Op()
        op.eng = eng
        op.track = track
        track.n += 1
        op.seq = track.n
        self.ecount[eng] += 1
        op.eseq = self.ecount[eng]
        op.fn = fn
        op.tag = getattr(self, 'tag', '')
        op.signal = track.is_dma
        op.sem = None
        op.val = 0
        deps = []
        xreads = [b for b in reads if b.excl and b not in writes]
        for b in reads:
            if b.w is not None:
                deps.append((b.w, False))
            if b.excl:
                for o in b.r.values():
                    deps.append((o, True))
        for b in writes:
            if b.w is not None:
                deps.append((b.w, False))
            for o in b.r.values():
                deps.append((o, False))
        waits = []
        seen = self.seen[eng]
        own = self.etrack[eng]
        for d, soft in deps:
            if d.track is own:
                if soft or eng == "pe" or eng == "sp":
                    continue
                if d.eseq < op.eseq - 2:
                    continue
            if d.track is track and track.is_dma:
                continue
            if seen.get(d.track, 0) >= d.seq:
                continue
            seen[d.track] = d.seq
            d.signal = True
            waits.append(d)
        op.waits = waits
        for b in reads:
            b.r[track] = op
        for b in writes:
            b.w = op
            b.r = {}
        self.ops.append(op)
        return op

    def op(self, eng, fn, reads=(), writes=()):
        return self._record(eng, self.etrack[eng], fn, reads, writes)

    def dma(self, track, fn, reads=(), writes=(), eng="sp"):
        return self._record(eng, track, fn, reads, writes)

    def wait_all(self, eng, bufs):
        return self._record(eng, self.etrack[eng], None, bufs, ())

    def emit(self):
        import os
        maxops = int(os.environ.get("MAXOPS", "0"))
        ops = self.ops[:maxops] if maxops else self.ops
        for op in ops:
            e = self.engs[op.eng]
            for d in op.waits:
                t = d.track
                if t.is_dma:
                    epn = EPOCH // 16
                    ep = t.sems.index(d.sem)
                    if ep > 0 and (op.eng, id(t), ep - 1) not in self.epdone:
                        self.epdone.add((op.eng, id(t), ep - 1))
                        e.wait_ge(t.sems[ep - 1], epn * 16)
                e.wait_ge(d.sem, d.val)
            if op.fn is None:
                continue
            inst = op.fn(e)
            if op.signal:
                t = op.track
                k = t.nsig
                t.nsig += 1
                inc = 16 if t.is_dma else 1
                epn = EPOCH // inc
                ep = k // epn
                if ep >= len(t.sems):
                    t.sems.append(self.new_sem(t.name))
                op.sem = t.sems[ep]
                op.val = (k % epn + 1) * inc
                inst.then_inc(op.sem, inc)


class Arena:
    def __init__(self, tensor, nwords):
        self.t = tensor
        self.n = nwords
        self.pos = 0
        self.live = []
        self.pending = {}

    def reset(self):
        for b in self.live:
            if b.w is not None:
                o = self.pending.get(b.w.track)
                if o is None or o.seq < b.w.seq:
                    self.pending[b.w.track] = b.w
            for tr, op in b.r.items():
                o = self.pending.get(tr)
                if o is None or o.seq < op.seq:
                    self.pending[tr] = op
        self.live = []
        self.pos = 0
        self.rawstg = None

    def bufs(self, n):
        out = []
        for _ in range(n):
            b = Buf("arb")
            b.r = dict(self.pending)
            self.live.append(b)
            out.append(b)
        return out

    def alloc(self, words, dtype=F32, pat=None, **kw):
        words = (words + 7) // 8 * 8
        assert self.pos + words <= self.n, f"arena overflow {self.pos}+{words}>{self.n}"
        ap = self.t[:, self.pos:self.pos + words]
        self.pos += words
        if dtype == BF16:
            ap = ap.bitcast(BF16)
        if pat is not None:
            ap = ap.rearrange(pat, **kw)
        b = Buf("ar")
        b.r = dict(self.pending)
        self.live.append(b)
        return ap, b


class WStream:
    NSLOT = 4

    def __init__(self, K):
        self.K = K
        self.plan = []
        self.run = False
        self.i = 0
        self.issued = 0
        nc = self.K.nc
        self.slots = [nc.alloc_sbuf_tensor(f"wslot{i}", [128, 8, 512], BF16) for i in range(self.NSLOT)]
        self.bufs = [Buf(f"wslot{i}") for i in range(self.NSLOT)]
        self.tracks = [Track(f"wsl{i}", True) for i in range(self.NSLOT)]

    def start(self):
        self.run = True
        self.i = 0
        self.issued = 0

    def _issue(self, j):
        name, l, kc0, n0, nk, ncols, kind = self.plan[j]
        s = j % self.NSLOT
        src = self.K.wb[(l, name)][:, kc0:kc0 + nk, n0:n0 + ncols]
        dst = self.slots[s][:, 0:nk, 0:ncols]
        if kind == "qperm":
            for e_ in range(2):
                for i_ in range(4):
                    sp = src[:, :, (e_ * 4 + i_) * 64:(e_ * 4 + i_ + 1) * 64]
                    dp = dst[:, :, (i_ * 2 + e_) * 64:(i_ * 2 + e_ + 1) * 64]
                    self.K.S.dma(self.tracks[s], lambda e, dp=dp, sp=sp: e.dma_start(out=dp, in_=sp),
                                 reads=[self.K.wbuf[(l, name)]], writes=[self.bufs[s]])
            return
        self.K.S.dma(self.tracks[s], lambda e, dst=dst, src=src: e.dma_start(out=dst, in_=src),
                     reads=[self.K.wbuf[(l, name)]], writes=[self.bufs[s]])

    def get(self, name, l, kc0, n0, nk=8, ncols=512, kind=None):
        key = (name, l, kc0, n0, nk, ncols, kind)
        if not self.run:
            self.plan.append(key)
            return self.slots[0], Buf("dummy")
        assert self.plan[self.i] == key, (self.plan[self.i], key)
        while self.issued < min(len(self.plan), self.i + self.NSLOT - 1):
            self._issue(self.issued)
            self.issued += 1
        s = self.i % self.NSLOT
        self.i += 1
        return self.slots[s], self.bufs[s]


class KB:
    def __init__(self, T, L, TC=256, dbg=()):
        self.T, self.L, self.TC = T, L, TC
        self.NT = TC // 128
        self.NCH = T // TC
        self.dbg = set(dbg)

    def PE(self, fn, reads, writes):
        self.S.op("pe", fn, reads, writes)

    def ACT(self, fn, reads, writes):
        self.S.op("act", fn, reads, writes)

    def DVE(self, fn, reads, writes):
        self.S.op("dve", fn, reads, writes)

    def mm(self, out, lhsT, rhs, start, stop, reads, writes):
        self.S.op("pe", lambda e: e.matmul(out, lhsT, rhs, start=start, stop=stop), reads, writes)

    def act(self, out, in_, func, reads, writes, **kw):
        self.S.op("act", lambda e: e.activation(out, in_, func, **kw), reads, writes)

    def tt(self, out, in0, in1, op, reads, writes):
        self.S.op("dve", lambda e: e.tensor_tensor(out=out, in0=in0, in1=in1, op=op), reads, writes)

    def stt(self, out, in0, scalar, in1, op0, op1, reads, writes):
        self.S.op("dve", lambda e: e.scalar_tensor_tensor(out=out, in0=in0, scalar=scalar, in1=in1, op0=op0, op1=op1), reads, writes)

    def ts(self, out, in0, s1, s2, op0, op1, reads, writes):
        if s2 is None:
            self.S.op("dve", lambda e: e.tensor_scalar(out=out, in0=in0, scalar1=s1, scalar2=None, op0=op0), reads, writes)
        else:
            self.S.op("dve", lambda e: e.tensor_scalar(out=out, in0=in0, scalar1=s1, scalar2=s2, op0=op0, op1=op1), reads, writes)

    def bank(self, grp):
        lst = self.bgrp[grp]
        i = self.bpos[grp]
        self.bpos[grp] = (i + 1) % len(lst)
        j = lst[i]
        return self.ps[j], self.psb[j]

    def rsqrt(self, out, in_, scale, reads, writes):
        self.act(out, in_, AF.Ln, reads, writes, scale=scale, bias=EPS)
        self.act(out, out, AF.Exp, writes, writes, scale=-0.5)

    def pcv(self, l, off, n=1):
        return self.pc[:, l * NPC + off: l * NPC + off + n]

    def prv(self, l, off, n):
        return self.pr[:, l * NPR + off: l * NPR + off + n]

    def build(self):
        nc = bass.Bass("TRN2", target_bir_lowering=False)
        self.nc = nc
        T, L, TC = self.T, self.L, self.TC

        def din(n, s, d=F32):
            return nc.dram_tensor(n, s, d, kind="ExternalInput").ap()

        self.x = din("x", [T, D])
        self.mem = din("mem", [256, D])
        self.consts_d = din("consts", [128, 512])
        self.swab_d = din("swab", [128, 2 * 16 * 128])
        self.pcol_d = din("pcol", [128, L * NPC])
        self.prow_d = din("prow", [128, L * NPR])
        self.wd = {n: din(n, [L, k, c]) for n, (k, c) in WSHAPES.items()}
        self.y = nc.dram_tensor("y", [T, D], F32, kind="ExternalOutput").ap()
        self.wb = {}
        self.wbuf = {}
        for l in range(L):
            for n, (k, c) in WSHAPES.items():
                self.wb[(l, n)] = nc.dram_tensor(f"wb_{n}_{l}", [128, k // 128, c], BF16, kind="Internal").ap()
                self.wbuf[(l, n)] = Buf(f"wb_{n}_{l}")
        self.kvs = [nc.dram_tensor(f"kvs_{l}", [128, 4096], BF16, kind="Internal").ap() for l in range(L)]
        self.kvs_b = [Buf(f"kvs{l}") for l in range(L)]
        self.dbg_d = {}
        for name in self.dbg:
            if name == "raw":
                continue
            self.dbg_d[name] = nc.dram_tensor("dbg_" + name, [128, 16 if name in ("ybin",) else 8, T], F32, kind="ExternalOutput").ap()
        self.dbg.discard("raw") if False else None
        self.dbg_b = {n: Buf("dbg" + n) for n in self.dbg}
        self.dbg_t = {n: Track("dbg" + n, True) for n in self.dbg}

        with ExitStack() as st:
            self.W = WStream(self)
            self.S = Sched(nc, st, dry=True)
            self.alloc_all(dry=True)
            self.program()
            self.S = Sched(nc, st, dry=False)
            self.W.start()
            self.program()
            self.S.emit()
        return nc

    def alloc_all(self, dry):
        nc = self.nc
        L, W = self.L, self.TC
        A = nc.alloc_sbuf_tensor

        def R(name, shape, dt=F32):
            return A(name, shape, dt), Buf(name)

        self.hT, _ = R("hT", [128, 8, W])
        self.hTb = [Buf(f"hT{c}") for c in range(8)]
        self.uT, self.uTb = R("uT", [128, 8, W], BF16)
        self.mrg, _ = R("mrg", [128, 8, W])
        self.mrgb = [Buf(f"mrg{c}") for c in range(8)]
        self.state, _ = R("state", [128, L, 2048])
        self.stateb = [[Buf(f"st{l}_{g}") for g in range(4)] for l in range(L)]
        self.bias, self.biasb = R("swabias", [128, 2, 16, 128])
        self.pc, self.pcb = R("pc", [128, L * NPC])
        self.pr, self.prb = R("pr", [128, L * NPR])
        self.arow, self.arowb = R("arow", [128, L * 32])
        self.esink, self.esinkb = R("esink", [128, L * 16])
        self.ctail, _ = R("ctail", [128, L, 8, 30])
        self.ctailb = [[Buf("ct") for c in range(8)] for l in range(L)]
        self.xtail, _ = R("xtail", [128, L, 24, 3])
        self.xtailb = [[Buf("xt") for c in range(24)] for l in range(L)]
        self.kprev, _ = R("kprev", [128, L, 4, 128], BF16)
        self.kprevb = [Buf("kp") for l in range(L)]
        self.vprev, _ = R("vprev", [128, L, 512], BF16)
        self.vprevb = [Buf("vp") for l in range(L)]
        self.cst, self.cstb = R("cst", [128, 512])
        self.ident = self.cst[:, 0:128]
        self.triu = self.cst[:, 128:256]
        self.negm = self.cst[:, 256:384]
        self.blk64 = self.cst[:, 384:512]
        self.identb, self.identbb = R("identb", [128, 128], BF16)
        self.negmb, self.negmbb = R("negmb", [128, 128], BF16)
        self.onesb, self.onesbb = R("onesb", [128, 128], BF16)
        self.onesf, self.onesfb = R("onesf", [128, 128])
        self.ps = [nc.alloc_psum_tensor(f"ps{i}", [128, 512], F32) for i in range(8)]
        self.psb = [Buf(f"ps{i}", excl=True) for i in range(8)]
        self.bgrp = {"mm": [0, 1, 2, 3], "st": [4, 5], "tr": [6, 7]}
        self.bpos = {"mm": 0, "st": 0, "tr": 0}
        nw = (nc.sbuf_bytes_remaining - 2048) // 4
        nw = nw // 8 * 8
        self.ar_t = A("arena", [128, nw], F32)
        self.arena = Arena(self.ar_t, nw)

    def program(self):
        if not self.S.dry:
            self.reset_bufs()
        self.init_phase()
        for ck in range(self.NCH):
            self.load_x(ck)
            import os
            PH = os.environ.get("PH", "ncswmxp")
            for l in range(self.L):
                if "n" in PH:
                    self.rmsnorm(l, PC_NMIX)
                self.dump("u", self.uT[:], self.uTb, ck) if l == 0 else None
                if "c" in PH:
                    self.conv_phase(l, ck)
                if "s" in PH:
                    self.ssd_phase(l, ck)
                if "w" in PH:
                    self.swa_phase(l, ck)
                if "m" in PH:
                    self.mix_phase(l, ck)
                if "x" in PH:
                    if ck == 0:
                        self.kv_precompute(l)
                    self.xattn_phase(l, ck)
                if "p" in PH:
                    self.mlp_phase(l, ck)
            self.store_y(ck)
        self.S.wait_all("sp", [self.yb])

    def reset_bufs(self):
        self.arena = Arena(self.ar_t, self.arena.n)
        self.bpos = {"mm": 0, "st": 0, "tr": 0}

    def dump(self, name, ap, bufs, ck, nchunks=8):
        if name not in self.dbg:
            return
        W = self.TC
        stg, sb = self.arena_dbg(nchunks)
        bl = bufs if isinstance(bufs, list) else [bufs]
        self.ACT(lambda e: e.activation(stg, ap, AF.Identity), bl, [sb])
        dst = self.dbg_d[name][:, :, ck * W:(ck + 1) * W]
        self.S.dma(self.dbg_t[name], lambda e: e.dma_start(out=dst, in_=stg), reads=[sb], writes=[self.dbg_b[name]])

    def rawdump(self, name, ap, bufs, n):
        if "raw" not in self.dbg:
            return
        if not hasattr(self, "raw_d"):
            self.raw_d = {}
        if name not in self.raw_d:
            self.raw_d[name] = self.nc.dram_tensor("raw_" + name, [128, n], F32, kind="ExternalOutput").ap()
        if getattr(self.arena, "rawstg", None) is None:
            self.arena.rawstg = self.arena.alloc(2048)
        stg, sb = self.arena.rawstg
        stg = stg[:, 0:n]
        bl = bufs if isinstance(bufs, list) else [bufs]
        self.ACT(lambda e: e.activation(stg, ap, AF.Identity), bl, [sb])
        dst = self.raw_d[name]
        self.S.dma(Track("raw" + name, True), lambda e: e.dma_start(out=dst, in_=stg), reads=[sb], writes=[Buf("rawo")])

    def arena_dbg(self, nchunks):
        W = self.TC
        return self.arena.alloc(nchunks * W, F32, "p (c w) -> p c w", c=nchunks)

    def init_phase(self):
        S, nc, L = self.S, self.nc, self.L
        self.yb = Buf("y")
        self.ytrack = Track("ytr", True)
        self.xtrack = Track("xtr", True)
        self.kvtrack = Track("kvtr", True)
        self.kvltrack = Track("kvltr", True)
        tr = [Track(f"init{i}", True) for i in range(4)]
        S.dma(tr[0], lambda e: e.dma_start(out=self.cst[:], in_=self.consts_d), writes=[self.cstb])
        S.dma(tr[1], lambda e: e.dma_start(out=self.bias[:].rearrange("p a h q -> p (a h q)"), in_=self.swab_d), writes=[self.biasb])
        S.dma(tr[2], lambda e: e.dma_start(out=self.pc[:], in_=self.pcol_d), writes=[self.pcb])
        S.dma(tr[3], lambda e: e.dma_start(out=self.pr[:], in_=self.prow_d), writes=[self.prb])
        self.cvtrack = {}
        for l in range(L):
            for n in WORDER:
                k, c = WSHAPES[n]
                t = Track(f"cv_{n}_{l}", True)
                for kc in range(k // 128):
                    for n0 in range(0, c, 4096):
                        n1 = min(c, n0 + 4096)
                        dst = self.wb[(l, n)][:, kc, n0:n1]
                        src = self.wd[n][l, kc * 128:(kc + 1) * 128, n0:n1]
                        S.dma(t, lambda e, dst=dst, src=src: e.dma_start(out=dst, in_=src), writes=[self.wbuf[(l, n)]], eng="pool")
        self.DVE(lambda e: e.memset(self.onesb[:], 1.0), [], [self.onesbb])
        self.DVE(lambda e: e.memset(self.onesf[:], 1.0), [], [self.onesfb])
        self.DVE(lambda e: e.tensor_copy(self.identb[:], self.ident), [self.cstb], [self.identbb])
        self.DVE(lambda e: e.tensor_copy(self.negmb[:], self.negm), [self.cstb], [self.negmbb])
        allst = [b for row in self.stateb for b in row]
        self.DVE(lambda e: e.memset(self.state[:].rearrange("p l n -> p (l n)"), 0.0), [], allst)
        self.DVE(lambda e: e.memset(self.ctail[:].rearrange("p l c j -> p (l c j)"), 0.0), [], [b for row in self.ctailb for b in row])
        self.DVE(lambda e: e.memset(self.xtail[:].rearrange("p l c j -> p (l c j)"), 0.0), [], [b for row in self.xtailb for b in row])
        self.DVE(lambda e: e.memset(self.kprev[:].rearrange("p l g k -> p (l g k)"), 0.0), [], self.kprevb)
        self.DVE(lambda e: e.memset(self.vprev[:].rearrange("p l n -> p (l n)"), 0.0), [], self.vprevb)
        for l in range(L):
            a = self.arow[:, l * 32:(l + 1) * 32]
            self.act(a, self.prv(l, PR_ALOG, 32), AF.Exp, [self.prb], [self.arowb])
            self.ts(a, a, -1.0, None, ALU.mult, None, [self.arowb], [self.arowb])
            self.act(self.esink[:, l * 16:(l + 1) * 16], self.prv(l, PR_SINK, 16), AF.Exp, [self.prb], [self.esinkb])

    def load_x(self, ck):
        self.S.tag = 'load_x' + str((ck))
        W, NT = self.TC, self.NT
        ar = self.arena
        ar.reset()
        xin, xb = ar.alloc(NT * 1024, F32, "p (t d) -> p t d", t=NT)
        src = self.x[ck * W:(ck + 1) * W, :].rearrange("(t p) d -> p t d", p=128)
        self.S.dma(self.xtrack, lambda e: e.dma_start(out=xin, in_=src), writes=[xb])
        cpb = 512 // W
        for c0 in range(0, 8, cpb):
            pt, pb = self.bank("tr")
            for cc in range(cpb):
                for t in range(NT):
                    o = pt[:, cc * W + t * 128: cc * W + (t + 1) * 128]
                    i = xin[:, t, (c0 + cc) * 128:(c0 + cc + 1) * 128]
                    self.PE(lambda e, o=o, i=i: e.transpose(o, i, self.ident), [xb, self.cstb], [pb])
            self.DVE(lambda e, c0=c0, pt=pt: e.tensor_copy(self.hT[:, c0:c0 + cpb, :], pt[:, 0:cpb * W].rearrange("p (c w) -> p c w", c=cpb)),
                     [pb], self.hTb[c0:c0 + cpb])

    def store_y(self, ck):
        self.S.tag = 'store_y' + str((ck))
        W, NT = self.TC, self.NT
        ar = self.arena
        ar.reset()
        yo, yob = ar.alloc(NT * 1024, F32, "p (t d) -> p t d", t=NT)
        for t in range(NT):
            for c0 in range(0, 8, 4):
                pt, pb = self.bank("tr")
                for cc in range(4):
                    o = pt[:, cc * 128:(cc + 1) * 128]
                    i = self.hT[:, c0 + cc, t * 128:(t + 1) * 128]
                    self.PE(lambda e, o=o, i=i: e.transpose(o, i, self.ident), [self.hTb[c0 + cc], self.cstb], [pb])
                self.ACT(lambda e, t=t, c0=c0, pt=pt: e.activation(yo[:, t, c0 * 128:(c0 + 4) * 128], pt[:, 0:512], AF.Identity), [pb], [yob])
        dst = self.y[ck * W:(ck + 1) * W, :].rearrange("(t p) d -> p t d", p=128)
        self.S.dma(self.ytrack, lambda e: e.dma_start(out=dst, in_=yo), reads=[yob], writes=[self.yb])

    def rmsnorm(self, l, off):
        self.S.tag = 'rmsnorm' + str((l, off))
        W = self.TC
        ar = self.arena
        ar.reset()
        sq = [ar.alloc(W) for _ in range(2)]
        rs, rsb = ar.alloc(W)
        pst, pstb = self.bank("st")
        for c in range(8):
            s, sb = sq[c % 2]
            self.act(s, self.hT[:, c, :], AF.Square, [self.hTb[c]], [sb])
            self.mm(pst[:, 0:W], self.onesf[:], s, c == 0, c == 7, [sb, self.onesfb], [pstb])
        self.rsqrt(rs, pst[:, 0:W], 1.0 / D, [pstb], [rsb])
        for c in range(8):
            self.stt(self.uT[:, c, :], self.hT[:, c, :], self.pcv(l, off + c), rs, ALU.mult, ALU.mult,
                     [self.hTb[c], rsb, self.pcb], [self.uTb])

    def branch_out(self, l, k, srcT, srcb, wname, KC, scratch=None):
        W = self.TC
        ar = self.arena
        if scratch is None:
            gs = [ar.alloc(W) for _ in range(4)]
            tmp = [ar.alloc(W) for _ in range(2)]
        else:
            gs, tmp = scratch[0:4], scratch[4:6]
        for half in range(2):
            wg, wgb = self.W.get("w_in", l, 0, OFF_GATE + k * 1024 + half * 512)
            for cc in range(4):
                oc = half * 4 + cc
                pg, pgb = self.bank("st")
                for kc in range(8):
                    self.mm(pg[:, 0:W], wg[:, kc, cc * 128:(cc + 1) * 128] if wg is not None else None, self.uT[:, kc, :], kc == 0, kc == 7, [wgb, self.uTb], [pgb])
                self.act(gs[cc][0], pg[:, 0:W], AF.Sigmoid, [pgb, self.pcb], [gs[cc][1]], bias=self.pcv(l, PC_GB + k * 8 + oc))
            banks = [self.bank("mm") for _ in range(4)]
            for j in range(KC // 8):
                wt, wtb = self.W.get(wname, l, j * 8, half * 512)
                for cc in range(4):
                    py, pyb = banks[cc]
                    for kk in range(8):
                        kc = j * 8 + kk
                        self.mm(py[:, 0:W], wt[:, kk, cc * 128:(cc + 1) * 128] if wt is not None else None, srcT[:, kc, :], kc == 0, kc == KC - 1, [wtb, srcb], [pyb])
            for cc in range(4):
                oc = half * 4 + cc
                py, pyb = banks[cc]
                if k == 0:
                    self.tt(self.mrg[:, oc, :], py[:, 0:W], gs[cc][0], ALU.mult, [pyb, gs[cc][1]], [self.mrgb[oc]])
                else:
                    t_, tb_ = tmp[cc % 2]
                    self.tt(t_, py[:, 0:W], gs[cc][0], ALU.mult, [pyb, gs[cc][1]], [tb_])
                    self.tt(self.mrg[:, oc, :], self.mrg[:, oc, :], t_, ALU.add, [tb_, self.mrgb[oc]], [self.mrgb[oc]])

    def conv_phase(self, l, ck):
        self.S.tag = 'conv_phase' + str((l, ck))
        W = self.TC
        ar = self.arena
        ar.reset()
        convT, _ = ar.alloc(8 * W, F32, "p (c w) -> p c w", c=8)
        convb = ar.bufs(8)
        actT, actb = ar.alloc(4 * W, BF16, "p (c w) -> p c w", c=8)
        glu = [ar.alloc(W + 32) for _ in range(4)]
        sig = [ar.alloc(W) for _ in range(2)]
        sq = [ar.alloc(W) for _ in range(2)]
        mean, meanb = ar.alloc(W)
        m2, m2b = ar.alloc(W)
        rstd, rstdb = ar.alloc(W)
        nmr, nmrb = ar.alloc(W)
        tmpn = [ar.alloc(W) for _ in range(2)]
        ps1, ps1b = self.bank("st")
        ps2, ps2b = self.bank("st")
        for half in range(2):
            wa, wab = self.W.get("w_in", l, 0, OFF_CONV + half * 512)
            wg, wgb = self.W.get("w_in", l, 0, OFF_CONV + 1024 + half * 512)
            for cc in range(4):
                c = half * 4 + cc
                g_, gb_ = glu[cc]
                s_, sb_ = sig[c % 2]
                pa, pab = self.bank("mm")
                pg, pgb = self.bank("mm")
                for kc in range(8):
                    self.mm(pa[:, 0:W], wa[:, kc, cc * 128:(cc + 1) * 128], self.uT[:, kc, :], kc == 0, kc == 7, [wab, self.uTb], [pab])
                for kc in range(8):
                    self.mm(pg[:, 0:W], wg[:, kc, cc * 128:(cc + 1) * 128], self.uT[:, kc, :], kc == 0, kc == 7, [wgb, self.uTb], [pgb])
                self.act(s_, pg[:, 0:W], AF.Sigmoid, [pgb], [sb_])
                self.act(g_[:, 0:30], self.ctail[:, l, c, :], AF.Identity, [self.ctailb[l][c]], [gb_])
                self.tt(g_[:, 30:30 + W], pa[:, 0:W], s_, ALU.mult, [pab, sb_], [gb_])
            for j in range(31):
                for cc in range(4):
                    c = half * 4 + cc
                    g_, gb_ = glu[cc]
                    cw = PC_CW + c * 31
                    if j == 0:
                        self.ts(convT[:, c, :], g_[:, 0:W], self.pcv(l, cw), self.pcv(l, PC_CB + c), ALU.mult, ALU.add, [gb_, self.pcb], [convb[c]])
                    else:
                        self.stt(convT[:, c, :], g_[:, j:j + W], self.pcv(l, cw + j), convT[:, c, :], ALU.mult, ALU.add, [gb_, self.pcb, convb[c]], [convb[c]])
            for cc in range(4):
                c = half * 4 + cc
                g_, gb_ = glu[cc]
                q_, qb_ = sq[c % 2]
                self.act(self.ctail[:, l, c, :], g_[:, W:W + 30], AF.Identity, [gb_], [self.ctailb[l][c]])
                self.act(q_, convT[:, c, :], AF.Square, [convb[c]], [qb_])
                self.mm(ps1[:, 0:W], self.onesf[:], convT[:, c, :], c == 0, c == 7, [convb[c], self.onesfb], [ps1b])
                self.mm(ps2[:, 0:W], self.onesf[:], q_, c == 0, c == 7, [qb_, self.onesfb], [ps2b])
        if l == 0 and ck == 0:
            self.rawdump("glu7", glu[3][0][:, 0:W + 32], glu[3][1], W + 32)
            self.rawdump("sig7", sig[1][0], sig[1][1], W)
            self.rawdump("conv7", convT[:, 7, :], convb[7], W)
            self.rawdump("conv0", convT[:, 0, :], convb[0], W)
        self.act(mean, ps1[:, 0:W], AF.Identity, [ps1b], [meanb], scale=1.0 / D)
        self.tt(m2, mean, mean, ALU.mult, [meanb], [m2b])
        self.stt(rstd, ps2[:, 0:W], 1.0 / D, m2, ALU.mult, ALU.subtract, [ps2b, m2b], [rstdb])
        self.rsqrt(rstd, rstd, 1.0, [rstdb], [rstdb])
        self.stt(nmr, mean, -1.0, rstd, ALU.mult, ALU.mult, [meanb, rstdb], [nmrb])
        for c in range(8):
            t_, tb_ = tmpn[c % 2]
            self.tt(t_, convT[:, c, :], rstd, ALU.mult, [convb[c], rstdb], [tb_])
            self.tt(t_, t_, nmr, ALU.add, [tb_, nmrb], [tb_])
            self.act(actT[:, c, :], t_, AF.Silu, [tb_, self.pcb], [actb], scale=self.pcv(l, PC_LG + c), bias=self.pcv(l, PC_LB + c))
        if l == 0 and ck == 0:
            self.rawdump("mean", mean, meanb, W)
            self.rawdump("rstd", rstd, rstdb, W)
        self.dump("cact", actT, actb, ck) if l == 0 else None
        self.branch_out(l, 0, actT, actb, "w_conv_out", 8)
        self.dump("m0", self.mrg[:], self.mrgb, ck) if l == 0 else None

    def ssd_phase(self, l, ck):
        self.S.tag = 'ssd_phase' + str((l, ck))
        W, NT = self.TC, self.NT
        ar = self.arena
        ar.reset()
        xtok, xtokb = ar.alloc(NT * 1024, BF16, "p (t n) -> p t n", t=NT)
        zs, zsb = ar.alloc(NT * 1024, BF16, "p (t n) -> p t n", t=NT)
        yT, yTb = ar.alloc(8 * W, BF16, "p (c w) -> p c w", c=16)
        bcT, _ = ar.alloc(4 * W, BF16, "p (c w) -> p c w", c=8)
        bcb = ar.bufs(8)
        btok, btokb = ar.alloc(NT * 256, BF16, "p (t n) -> p t n", t=NT)
        xdt, xdtb = ar.alloc(1024, BF16)
        xdte, xdteb = ar.alloc(1024, BF16)
        yw, _ = ar.alloc(2048)
        ywb = ar.bufs(4)
        ytok, ytokb = ar.alloc(1024, BF16)
        stbf, stbfb = ar.alloc(1024, BF16)
        xw = [ar.alloc(W + 8) for _ in range(2)]
        acc = [ar.alloc(W) for _ in range(2)]
        xc = [ar.alloc(W // 2, BF16) for _ in range(2)]
        lt = [ar.alloc(512) for _ in range(2)]
        sc = [ar.alloc(256, BF16, "p (j q) -> p j q", j=4) for _ in range(2)]
        cbm, cbmb = ar.alloc(512, F32, "p (g q) -> p g q", g=4)
        t1 = [ar.alloc(512) for _ in range(2)]
        junk, junkb = ar.alloc(512)
        dtall, dtallb = ar.alloc(NT * 32, F32, "p (t h) -> p t h", t=NT)
        daall, daallb = ar.alloc(NT * 32, F32, "p (t h) -> p t h", t=NT)
        cs, csb = ar.alloc(32)
        dfs, dfsb = ar.alloc(32)
        cd, cdb = ar.alloc(32)
        dte, dteb = ar.alloc(32)
        ss, ssb = ar.alloc(8)
        rs4, rs4b = ar.alloc(8)

        for zt in range(4):
            wz, wzb = self.W.get("w_in", l, 0, OFF_Z + zt * 512)
            for t in range(NT):
                pz, pzb = self.bank("mm")
                for kc in range(8):
                    self.mm(pz[:, 0:512], self.uT[:, kc, t * 128:(t + 1) * 128], wz[:, kc, :] if wz is not None else None, kc == 0, kc == 7, [wzb, self.uTb], [pzb])
                self.act(zs[:, t, zt * 512:(zt + 1) * 512], pz[:, 0:512], AF.Silu, [pzb], [zsb])
        for tile in range(6):
            wx, wxb = self.W.get("w_in", l, 0, OFF_XBC + tile * 512)
            for cc in range(4):
                c = tile * 4 + cc
                xw_, xwb_ = xw[c % 2]
                a_, ab_ = acc[c % 2]
                pb_, pbb_ = self.bank("mm")
                for kc in range(8):
                    self.mm(pb_[:, 0:W], wx[:, kc, cc * 128:(cc + 1) * 128] if wx is not None else None, self.uT[:, kc, :], kc == 0, kc == 7, [wxb, self.uTb], [pbb_])
                self.act(xw_[:, 0:3], self.xtail[:, l, c, :], AF.Identity, [self.xtailb[l][c]], [xwb_])
                self.act(xw_[:, 3:3 + W], pb_[:, 0:W], AF.Identity, [pbb_], [xwb_])
                sw = PC_SW + c * 4
                self.ts(a_, xw_[:, 0:W], self.pcv(l, sw), self.pcv(l, PC_SB + c), ALU.mult, ALU.add, [xwb_, self.pcb], [ab_])
                for j in range(1, 4):
                    self.stt(a_, xw_[:, j:j + W], self.pcv(l, sw + j), a_, ALU.mult, ALU.add, [xwb_, self.pcb, ab_], [ab_])
                self.act(self.xtail[:, l, c, :], xw_[:, W:W + 3], AF.Identity, [xwb_], [self.xtailb[l][c]])
                if c < 20:
                    x_, xb_ = xc[c % 2]
                    if c < 16:
                        self.act(x_, a_, AF.Silu, [ab_], [xb_])
                        src, srcb = x_, xb_
                    else:
                        self.act(bcT[:, c - 16, :], a_, AF.Silu, [ab_], [bcb[c - 16]])
                        src, srcb = bcT[:, c - 16, :], bcb[c - 16]
                    pt, ptb = self.bank("tr")
                    ptv = pt[:].bitcast(BF16)
                    for t in range(NT):
                        self.PE(lambda e, o=ptv[:, t * 128:(t + 1) * 128], i=src[:, t * 128:(t + 1) * 128]: e.transpose(o, i, self.identb[:]), [srcb, self.identbb], [ptb])
                    if c < 16:
                        self.DVE(lambda e, c=c, ptv=ptv: e.tensor_copy(xtok[:, :, c * 128:(c + 1) * 128], ptv[:, 0:NT * 128].rearrange("p (t n) -> p t n", t=NT)), [ptb], [xtokb])
                    else:
                        self.DVE(lambda e, c=c, ptv=ptv: e.tensor_copy(btok[:, :, (c - 16) * 128:(c - 15) * 128], ptv[:, 0:NT * 128].rearrange("p (t n) -> p t n", t=NT)), [ptb], [btokb])
                else:
                    self.act(bcT[:, c - 16, :], a_, AF.Silu, [ab_], [bcb[c - 16]])
        wdt, wdtb = self.W.get("w_in", l, 0, OFF_DT, 8, 32)
        for t in range(NT):
            pd, pdb = self.bank("tr")
            for kc in range(8):
                self.mm(pd[:, 0:32], self.uT[:, kc, t * 128:(t + 1) * 128], wdt[:, kc, 0:32] if wdt is not None else None, kc == 0, kc == 7, [wdtb, self.uTb], [pdb])
            self.tt(dtall[:, t, :], pd[:, 0:32], self.prv(l, PR_DTB, 32), ALU.add, [pdb, self.prb], [dtallb])
        dflat = dtall.rearrange("p t h -> p (t h)")
        self.act(dflat, dflat, AF.Exp, [dtallb], [dtallb])
        self.act(dflat, dflat, AF.Ln, [dtallb], [dtallb], bias=1.0)
        for t in range(NT):
            self.tt(daall[:, t, :], dtall[:, t, :], self.arow[:, l * 32:(l + 1) * 32], ALU.mult, [dtallb, self.arowb], [daallb])

        stl = self.state[:, l, :]
        stb = self.stateb[l]
        for t in range(NT):
            tsl = slice(t * 128, (t + 1) * 128)
            dtv = dtall[:, t, :]
            dav = daall[:, t, :]
            self.act(stbf, stl, AF.Identity, stb, [stbfb])
            pc_, pcb_ = self.bank("tr")
            self.mm(pc_[:, 0:32], self.triu, dav, True, True, [self.cstb, daallb], [pcb_])
            self.mm(pc_[:, 32:64], self.onesf[:], dav, True, True, [self.onesfb, daallb], [pcb_])
            self.act(cs, pc_[:, 0:32], AF.Identity, [pcb_], [csb])
            self.act(dfs, pc_[:, 0:32], AF.Exp, [pcb_], [dfsb])
            self.act(cd, pc_[:, 32:64], AF.Exp, [pcb_], [cdb])
            self.tt(dte, pc_[:, 32:64], cs, ALU.subtract, [pcb_, csb], [dteb])
            self.act(dte, dte, AF.Exp, [dteb], [dteb])
            x3 = xtok[:, t, :].rearrange("p (h d) -> p h d", h=32)
            self.tt(xdt.rearrange("p (h d) -> p h d", h=32), x3, dtv.unsqueeze(2).to_broadcast([128, 32, 64]), ALU.mult, [xtokb, dtallb], [xdtb])
            self.tt(xdte.rearrange("p (h d) -> p h d", h=32), xdt.rearrange("p (h d) -> p h d", h=32), dte.unsqueeze(2).to_broadcast([128, 32, 64]), ALU.mult, [xdtb, dteb], [xdteb])
            pcb2, pcb2b = self.bank("mm")
            for g in range(4):
                self.mm(pcb2[:, g * 128:(g + 1) * 128], bcT[:, g, tsl], bcT[:, 4 + g, tsl], True, True, [bcb[g], bcb[4 + g]], [pcb2b])
            self.tt(cbm, pcb2[:, 0:512].rearrange("p (g q) -> p g q", g=4), self.triu.unsqueeze(1).to_broadcast([128, 4, 128]), ALU.mult, [pcb2b, self.cstb], [cbmb])
            self.DVE(lambda e: e.memset(ss, 0.0), [], [ssb])
            for g in range(4):
                py, pyb = self.bank("mm")
                for hb in range(2):
                    h0 = g * 8 + hb * 4
                    lt_, ltb_ = lt[hb]
                    sc_, scb_ = sc[hb]
                    pq, pqb = self.bank("st")
                    for j in range(4):
                        self.mm(pq[:, j * 128:(j + 1) * 128], dav[:, h0 + j:h0 + j + 1].to_broadcast([128, 128]), self.triu, True, True, [daallb, self.cstb], [pqb])
                    lt3 = lt_.rearrange("p (j q) -> p j q", j=4)
                    self.tt(lt3, pq[:, 0:512].rearrange("p (j q) -> p j q", j=4), cs[:, h0:h0 + 4].unsqueeze(2).to_broadcast([128, 4, 128]), ALU.subtract, [pqb, csb], [ltb_])
                    self.act(lt_, lt_, AF.Relu, [ltb_], [ltb_], scale=-1.0)
                    self.act(lt_, lt_, AF.Exp, [ltb_], [ltb_], scale=-1.0)
                    self.tt(sc_, lt3, cbm[:, g, :].unsqueeze(1).to_broadcast([128, 4, 128]), ALU.mult, [ltb_, cbmb], [scb_])
                    for j in range(4):
                        h = h0 + j
                        self.mm(py[:, (hb * 4 + j) * 64:(hb * 4 + j + 1) * 64], sc_[:, j, :], xdt[:, h * 64:(h + 1) * 64], True, True, [scb_, xdtb], [pyb])
                po, pob = self.bank("mm")
                self.mm(po[:, 0:512], bcT[:, 4 + g, tsl], stbf[:, g * 512:(g + 1) * 512], True, True, [bcb[4 + g], stbfb], [pob])
                ta, tab = t1[0]
                tb, tbb = t1[1]
                ywg = yw[:, g * 512:(g + 1) * 512]
                self.tt(ta.rearrange("p (h d) -> p h d", h=8), po[:, 0:512].rearrange("p (h d) -> p h d", h=8), dfs[:, g * 8:(g + 1) * 8].unsqueeze(2).to_broadcast([128, 8, 64]), ALU.mult, [pob, dfsb], [tab])
                self.tt(ywg, py[:, 0:512], ta, ALU.add, [pyb, tab], [ywb[g]])
                self.tt(tb.rearrange("p (h d) -> p h d", h=8), xtok[:, t, g * 512:(g + 1) * 512].rearrange("p (h d) -> p h d", h=8),
                        self.prv(l, PR_D + g * 8, 8).unsqueeze(2).to_broadcast([128, 8, 64]), ALU.mult, [xtokb, self.prb], [tbb])
                self.tt(ywg, ywg, tb, ALU.add, [ywb[g], tbb], [ywb[g]])
                self.tt(ywg, ywg, zs[:, t, g * 512:(g + 1) * 512], ALU.mult, [ywb[g], zsb], [ywb[g]])
                self.act(junk, ywg, AF.Square, [ywb[g]], [junkb, ssb], accum_out=ss[:, g:g + 1])
                pn, pnb = self.bank("mm")
                self.mm(pn[:, 0:512], btok[:, t, g * 128:(g + 1) * 128], xdte[:, g * 512:(g + 1) * 512], True, True, [btokb, xdteb], [pnb])
                sg = stl[:, g * 512:(g + 1) * 512]
                self.tt(sg.rearrange("p (h d) -> p h d", h=8), sg.rearrange("p (h d) -> p h d", h=8), cd[:, g * 8:(g + 1) * 8].unsqueeze(2).to_broadcast([128, 8, 64]), ALU.mult, [stb[g], cdb, stbfb], [stb[g]])
                self.tt(sg, sg, pn[:, 0:512], ALU.add, [stb[g], pnb], [stb[g]])
            if l == 0 and ck == 0 and t == 0:
                self.rawdump("dt", dtall.rearrange("p t h -> p (t h)"), dtallb, NT * 32)
                self.rawdump("da", daall.rearrange("p t h -> p (t h)"), daallb, NT * 32)
                self.rawdump("cs", cs, csb, 32)
                self.rawdump("dfs", dfs, dfsb, 32)
                self.rawdump("cd", cd, cdb, 32)
                self.rawdump("dte", dte, dteb, 32)
                self.rawdump("xtok", xtok[:, 0, :], xtokb, 2048)
                self.rawdump("btok", btok[:, 0, :], btokb, 512)
                self.rawdump("xdt", xdt, xdtb, 2048)
                self.rawdump("zs", zs[:, 0, :], zsb, 2048)
                self.rawdump("cbm", cbm.rearrange("p g q -> p (g q)"), cbmb, 512)
                self.rawdump("lt", lt[1][0], lt[1][1], 512)
                self.rawdump("yw", yw, ywb, 2048)
                self.rawdump("ss", ss, ssb, 8)
                self.rawdump("bc", bcT.rearrange("p c w -> p (c w)"), bcb, 8 * W)
            self.rsqrt(rs4[:, 0:4], ss[:, 0:4], 1.0 / 512, [ssb], [rs4b])
            self.tt(ytok.rearrange("p (g n) -> p g n", g=4), yw.rearrange("p (g n) -> p g n", g=4), rs4[:, 0:4].unsqueeze(2).to_broadcast([128, 4, 512]), ALU.mult, ywb + [rs4b], [ytokb])
            for half in range(2):
                pt, ptb = self.bank("tr")
                ptv = pt[:].bitcast(BF16)
                for j in range(8):
                    c = half * 8 + j
                    self.PE(lambda e, o=ptv[:, j * 128:(j + 1) * 128], i=ytok[:, c * 128:(c + 1) * 128]: e.transpose(o, i, self.identb[:]), [ytokb, self.identbb], [ptb])
                self.tt(yT[:, half * 8:(half + 1) * 8, tsl], ptv[:, 0:1024].rearrange("p (c q) -> p c q", c=8),
                        self.pcv(l, PC_NG + half * 8, 8).unsqueeze(2).to_broadcast([128, 8, 128]), ALU.mult, [ptb, self.pcb], [yTb])
        self.dump("ybin", yT, yTb, ck, 16) if l == 0 else None
        WW = 512 // 2
        scr = [(lt[0][0][:, 0:W], lt[0][1]), (lt[0][0][:, WW:WW + W], lt[0][1]), (lt[1][0][:, 0:W], lt[1][1]), (lt[1][0][:, WW:WW + W], lt[1][1]),
               (t1[0][0][:, 0:W], t1[0][1]), (t1[1][0][:, 0:W], t1[1][1])] if W <= 256 else None
        self.branch_out(l, 1, yT, yTb, "w_ssd_out", 16, scratch=scr)
        self.dump("m1", self.mrg[:], self.mrgb, ck) if l == 0 else None

    def qknorm_chunk(self, pq, pqb, hd_scale, stat_lhsT, stat_b, sq, rs):
        pass

    def swa_phase(self, l, ck):
        self.S.tag = 'swa_phase' + str((l, ck))
        W, NT = self.TC, self.NT
        ar = self.arena
        ar.reset()
        qT, qTb = ar.alloc(4 * W, BF16, "p (c w) -> p c w", c=8)
        oT, oTb = ar.alloc(4 * W, BF16, "p (c w) -> p c w", c=8)
        KW = 128 + W
        kT, kTb = ar.alloc(2 * KW, BF16, "p (g k) -> p g k", g=4)
        vt, vtb = ar.alloc((1 + NT) * 256, BF16, "p (t n) -> p t n", t=1 + NT)
        sq = [ar.alloc(W) for _ in range(2)]
        rs = [ar.alloc(W) for _ in range(2)]
        pts = [[ar.alloc(256, BF16) for _ in range(2)] for _ in range(2)]
        tbs = [ar.alloc(512) for _ in range(2)]
        dtot, dtotb = ar.alloc(512)
        rd, rdb = ar.alloc(512)
        self.DVE(lambda e: e.tensor_copy(kT[:, :, 0:128], self.kprev[:, l]), [self.kprevb[l]], [kTb])
        self.DVE(lambda e: e.tensor_copy(vt[:, 0, :], self.vprev[:, l, :]), [self.vprevb[l]], [vtb])
        kT2 = kT.rearrange("p (m e) k -> p m e k", e=2)
        self.DVE(lambda e: e.memset(kT2[64:128, :, 0, 128:KW], 0.0), [], [kTb])
        self.DVE(lambda e: e.memset(kT2[0:64, :, 1, 128:KW], 0.0), [], [kTb])
        gq = self.pcv(l, PC_QN)
        gk = self.pcv(l, PC_KN)
        for tile in range(2):
            wq, wqb = self.W.get("w_in", l, 0, OFF_Q + tile * 512, kind="qperm")
            for cc in range(4):
                c = tile * 4 + cc
                s_, sb_ = sq[c % 2]
                r_, rb_ = rs[c % 2]
                pq, pqb = self.bank("mm")
                for kc in range(8):
                    self.mm(pq[:, 0:W], wq[:, kc, cc * 128:(cc + 1) * 128], self.uT[:, kc, :], kc == 0, kc == 7, [wqb, self.uTb], [pqb])
                self.act(s_, pq[:, 0:W], AF.Square, [pqb], [sb_])
                pst, pstb = self.bank("st")
                self.mm(pst[:, 0:W], self.blk64, s_, True, True, [self.cstb, sb_], [pstb])
                self.rsqrt(r_, pst[:, 0:W], 1.0 / 64, [pstb], [rb_])
                self.stt(qT[:, c, :], pq[:, 0:W], gq, r_, ALU.mult, ALU.mult, [pqb, rb_, self.pcb], [qTb])
        wkv, wkvb = self.W.get("w_in", l, 0, OFF_K)
        for m in range(2):
            s_, sb_ = sq[m % 2]
            r_, rb_ = rs[m % 2]
            pk, pkb = self.bank("mm")
            for kc in range(8):
                self.mm(pk[:, 0:W], wkv[:, kc, m * 128:(m + 1) * 128], self.uT[:, kc, :], kc == 0, kc == 7, [wkvb, self.uTb], [pkb])
            self.act(s_, pk[:, 0:W], AF.Square, [pkb], [sb_])
            pst, pstb = self.bank("st")
            self.mm(pst[:, 0:W], self.blk64, s_, True, True, [self.cstb, sb_], [pstb])
            self.rsqrt(r_, pst[:, 0:W], 1.0 / 64, [pstb], [rb_])
            self.stt(kT[0:64, 2 * m, 128:KW], pk[0:64, 0:W], gk[0:64], r_[0:64], ALU.mult, ALU.mult, [pkb, rb_, self.pcb], [kTb])
            self.stt(kT[64:128, 2 * m + 1, 128:KW], pk[64:128, 0:W], gk[64:128], r_[64:128], ALU.mult, ALU.mult, [pkb, rb_, self.pcb], [kTb])
        for t in range(NT):
            pv, pvb = self.bank("mm")
            for kc in range(8):
                self.mm(pv[:, 0:256], self.uT[:, kc, t * 128:(t + 1) * 128], wkv[:, kc, 256:512], kc == 0, kc == 7, [wkvb, self.uTb], [pvb])
            self.DVE(lambda e, t=t, pv=pv: e.tensor_copy(vt[:, 1 + t, :].rearrange("p (g e d) -> p g e d", g=4, e=2),
                                                       pv[:, 0:256].rearrange("p (g d) -> p g d", g=4).unsqueeze(2).to_broadcast([128, 4, 2, 64])), [pvb], [vtb])
        for t in range(NT):
            gblk = ck * NT + t
            tsl = slice(t * 128, (t + 1) * 128)
            sides = ([0] if gblk > 0 else []) + [1]
            for g in range(4):
                for side in sides:
                    ko = t * 128 + side * 128
                    pS, pSb = self.bank("mm")
                    for i in range(4):
                        self.mm(pS[:, i * 128:(i + 1) * 128], kT[:, g, ko:ko + 128], qT[:, 4 * (g // 2) + i, tsl], True, True, [kTb, qTb], [pSb])
                    tb_, tbb_ = tbs[side]
                    p_, pb_ = pts[g % 2][side]
                    self.stt(tb_, pS[:, 0:512], 0.125, self.bias[:, side, g * 4:(g + 1) * 4, :].rearrange("p h q -> p (h q)"), ALU.mult, ALU.add, [pSb, self.biasb], [tbb_])
                    self.act(p_, tb_, AF.Exp, [tbb_], [pb_])
                pO, pOb = self.bank("mm")
                pD, pDb = self.bank("st")
                for i, side in enumerate(sides):
                    p_, pb_ = pts[g % 2][side]
                    self.mm(pO[:, 0:512], vt[:, t + side, g * 128:(g + 1) * 128], p_, i == 0, i == len(sides) - 1, [vtb, pb_], [pOb])
                for i, side in enumerate(sides):
                    p_, pb_ = pts[g % 2][side]
                    self.mm(pD[:, 0:512], self.onesb[:], p_, i == 0, i == len(sides) - 1, [self.onesbb, pb_], [pDb])
                self.tt(dtot.rearrange("p (h q) -> p h q", h=4), pD[:, 0:512].rearrange("p (h q) -> p h q", h=4),
                        self.esink[:, l * 16 + g * 4:l * 16 + g * 4 + 4].unsqueeze(2).to_broadcast([128, 4, 128]), ALU.add, [pDb, self.esinkb], [dtotb])
                self.DVE(lambda e: e.reciprocal(rd, dtot), [dtotb], [rdb])
                for e_ in range(2):
                    rows = slice(64 * e_, 64 * e_ + 64)
                    self.tt(oT[rows, 2 * g:2 * g + 2, tsl], pO[rows, 0:512].rearrange("p (c e q) -> p c e q", c=2, e=2)[:, :, e_, :],
                            rd[rows, 0:512].rearrange("p (c e q) -> p c e q", c=2, e=2)[:, :, e_, :], ALU.mult, [pOb, rdb], [oTb])
        self.DVE(lambda e: e.tensor_copy(self.kprev[:, l], kT[:, :, W:W + 128]), [kTb], [self.kprevb[l]])
        self.DVE(lambda e: e.tensor_copy(self.vprev[:, l, :], vt[:, NT, :]), [vtb], [self.vprevb[l]])
        self.dump("ycin", oT, oTb, ck) if l == 0 else None
        self.branch_out(l, 2, oT, oTb, "w_attn_out", 8)
        self.dump("m2", self.mrg[:], self.mrgb, ck) if l == 0 else None

    def proj_add(self, l, wname, srcT, srcb):
        W = self.TC
        for half in range(2):
            wt, wtb = self.W.get(wname, l, 0, half * 512)
            for cc in range(4):
                oc = half * 4 + cc
                py, pyb = self.bank("mm")
                for kc in range(8):
                    self.mm(py[:, 0:W], wt[:, kc, cc * 128:(cc + 1) * 128] if wt is not None else None, srcT[:, kc, :], kc == 0, kc == 7, [wtb, srcb], [pyb])
                self.tt(self.hT[:, oc, :], self.hT[:, oc, :], py[:, 0:W], ALU.add, [pyb, self.hTb[oc]], [self.hTb[oc]])

    def mix_phase(self, l, ck):
        self.S.tag = 'mix_phase' + str((l, ck))
        W = self.TC
        ar = self.arena
        ar.reset()
        mb, mbb = ar.alloc(4 * W, BF16, "p (c w) -> p c w", c=8)
        for c in range(8):
            self.act(mb[:, c, :], self.mrg[:, c, :], AF.Identity, [self.mrgb[c]], [mbb])
        self.proj_add(l, "w_mix_out", mb, mbb)
        self.dump("h1", self.hT[:], self.hTb, ck) if l == 0 else None

    def kv_precompute(self, l):
        self.S.tag = 'kv_precompute' + str((l))
        ar = self.arena
        ar.reset()
        memx, memxb = ar.alloc(2048, F32, "p (t d) -> p t d", t=2)
        memn, memnb = ar.alloc(1024, BF16, "p (t d) -> p t d", t=2)
        memT, memTb = ar.alloc(1024, BF16, "p (c k) -> p c k", c=8)
        kvo, kvob = ar.alloc(2048, BF16)
        junk, junkb = ar.alloc(1024)
        ms, msb = ar.alloc(8)
        sq = [ar.alloc(256) for _ in range(2)]
        rs, rsb = ar.alloc(256)
        src = self.mem.rearrange("(t p) d -> p t d", p=128)
        self.S.dma(self.kvltrack, lambda e: e.dma_start(out=memx, in_=src), writes=[memxb])
        self.DVE(lambda e: e.memset(ms, 0.0), [], [msb])
        for kt in range(2):
            self.act(junk, memx[:, kt, :], AF.Square, [memxb], [junkb, msb], accum_out=ms[:, kt:kt + 1])
        self.rsqrt(ms[:, 0:2], ms[:, 0:2], 1.0 / D, [msb], [msb])
        for kt in range(2):
            self.ts(memn[:, kt, :], memx[:, kt, :], ms[:, kt:kt + 1], None, ALU.mult, None, [memxb, msb], [memnb])
        for kt in range(2):
            pt, ptb = self.bank("tr")
            ptv = pt[:].bitcast(BF16)
            for c in range(8):
                self.PE(lambda e, o=ptv[:, c * 128:(c + 1) * 128], i=memn[:, kt, c * 128:(c + 1) * 128]: e.transpose(o, i, self.identb[:]), [memnb, self.identbb], [ptb])
            self.tt(memT[:, :, kt * 128:(kt + 1) * 128], ptv[:, 0:1024].rearrange("p (c k) -> p c k", c=8),
                    self.pcv(l, PC_NM, 8).unsqueeze(2).to_broadcast([128, 8, 128]), ALU.mult, [ptb, self.pcb], [memTb])
        kT = kvo[:, 0:2048].rearrange("p (c k) -> p c k", c=8)
        vv = kvo[:, 2048:4096].rearrange("p (t d) -> p t d", t=2)
        for tile in range(2):
            wk, wkb = self.W.get("w_xkv", l, 0, tile * 512)
            for hh2 in range(2):
                hh = tile * 2 + hh2
                pk = [self.bank("mm") for _ in range(2)]
                pst, pstb = self.bank("st")
                for j in range(2):
                    cc = hh2 * 2 + j
                    for kc in range(8):
                        self.mm(pk[j][0][:, 0:256], wk[:, kc, cc * 128:(cc + 1) * 128] if wk is not None else None, memT[:, kc, :], kc == 0, kc == 7, [wkb, memTb], [pk[j][1]])
                    self.act(sq[j][0], pk[j][0][:, 0:256], AF.Square, [pk[j][1]], [sq[j][1]])
                    self.mm(pst[:, 0:256], self.onesf[:], sq[j][0], j == 0, j == 1, [self.onesfb, sq[j][1]], [pstb])
                self.rsqrt(rs, pst[:, 0:256], 1.0 / 256, [pstb], [rsb])
                for j in range(2):
                    self.stt(kT[:, 2 * hh + j, :], pk[j][0][:, 0:256], self.pcv(l, PC_XK + j), rs, ALU.mult, ALU.mult, [pk[j][1], rsb, self.pcb], [kvob])
        for tile in range(2):
            wv, wvb = self.W.get("w_xkv", l, 0, 1024 + tile * 512)
            for kt in range(2):
                pv, pvb = self.bank("mm")
                for kc in range(8):
                    self.mm(pv[:, 0:512], memT[:, kc, kt * 128:(kt + 1) * 128], wv[:, kc, :] if wv is not None else None, kc == 0, kc == 7, [wvb, memTb], [pvb])
                self.act(vv[:, kt, tile * 512:(tile + 1) * 512], pv[:, 0:512], AF.Identity, [pvb], [kvob])
        self.S.dma(self.kvtrack, lambda e: e.dma_start(out=self.kvs[l], in_=kvo), reads=[kvob], writes=[self.kvs_b[l]])

    def xattn_phase(self, l, ck):
        self.S.tag = 'xattn_phase' + str((l, ck))
        W = self.TC
        self.rmsnorm(l, PC_NX)
        ar = self.arena
        ar.reset()
        kvb, kvbb = ar.alloc(2048, BF16)
        qT, qTb = ar.alloc(4 * W, BF16, "p (c w) -> p c w", c=8)
        oT, oTb = ar.alloc(4 * W, BF16, "p (c w) -> p c w", c=8)
        sq = [ar.alloc(W) for _ in range(2)]
        rs, rsb = ar.alloc(W)
        rd, rdb = ar.alloc(W)
        pts = [[ar.alloc(W // 2, BF16) for _ in range(2)] for _ in range(2)]
        self.S.dma(self.kvltrack, lambda e: e.dma_start(out=kvb, in_=self.kvs[l]), reads=[self.kvs_b[l]], writes=[kvbb])
        kT = kvb[:, 0:2048].rearrange("p (c k) -> p c k", c=8)
        vv = kvb[:, 2048:4096].rearrange("p (t d) -> p t d", t=2)
        for tile in range(2):
            wq, wqb = self.W.get("w_xq", l, 0, tile * 512)
            for hh2 in range(2):
                hh = tile * 2 + hh2
                pq = [self.bank("mm") for _ in range(2)]
                pst, pstb = self.bank("st")
                for j in range(2):
                    cc = hh2 * 2 + j
                    for kc in range(8):
                        self.mm(pq[j][0][:, 0:W], wq[:, kc, cc * 128:(cc + 1) * 128] if wq is not None else None, self.uT[:, kc, :], kc == 0, kc == 7, [wqb, self.uTb], [pq[j][1]])
                    self.act(sq[j][0], pq[j][0][:, 0:W], AF.Square, [pq[j][1]], [sq[j][1]])
                    self.mm(pst[:, 0:W], self.onesf[:], sq[j][0], j == 0, j == 1, [self.onesfb, sq[j][1]], [pstb])
                self.rsqrt(rs, pst[:, 0:W], 1.0 / 256, [pstb], [rsb])
                for j in range(2):
                    self.stt(qT[:, 2 * hh + j, :], pq[j][0][:, 0:W], self.pcv(l, PC_XQ + j), rs, ALU.mult, ALU.mult, [pq[j][1], rsb, self.pcb], [qTb])
        for hh in range(4):
            pp = pts[hh % 2]
            for kt in range(2):
                pS, pSb = self.bank("mm")
                for j in range(2):
                    self.mm(pS[:, 0:W], kT[:, 2 * hh + j, kt * 128:(kt + 1) * 128], qT[:, 2 * hh + j, :], j == 0, j == 1, [kvbb, qTb], [pSb])
                self.act(pp[kt][0], pS[:, 0:W], AF.Exp, [pSb], [pp[kt][1]], scale=1.0 / 16)
            pD, pDb = self.bank("st")
            for kt in range(2):
                self.mm(pD[:, 0:W], self.onesb[:], pp[kt][0], kt == 0, kt == 1, [self.onesbb, pp[kt][1]], [pDb])
            self.DVE(lambda e, pD=pD: e.reciprocal(rd, pD[:, 0:W]), [pDb], [rdb])
            for dc in range(2):
                c = 2 * hh + dc
                pO, pOb = self.bank("mm")
                for kt in range(2):
                    self.mm(pO[:, 0:W], vv[:, kt, c * 128:(c + 1) * 128], pp[kt][0], kt == 0, kt == 1, [kvbb, pp[kt][1]], [pOb])
                self.tt(oT[:, c, :], pO[:, 0:W], rd, ALU.mult, [pOb, rdb], [oTb])
        self.proj_add(l, "w_xo", oT, oTb)
        self.dump("h2", self.hT[:], self.hTb, ck) if l == 0 else None

    def mlp_phase(self, l, ck):
        self.S.tag = 'mlp_phase' + str((l, ck))
        W = self.TC
        self.rmsnorm(l, PC_NMLP)
        ar = self.arena
        ar.reset()
        actT, _ = ar.alloc(16 * W, BF16, "p (c w) -> p c w", c=32)
        actb = ar.bufs(4)
        rl = [ar.alloc(W) for _ in range(2)]
        for tile in range(8):
            wu, wub = self.W.get("w_mlp_up", l, 0, tile * 512)
            for cc in range(4):
                hc = tile * 4 + cc
                r_, rb_ = rl[hc % 2]
                pu, pub = self.bank("mm")
                for kc in range(8):
                    self.mm(pu[:, 0:W], wu[:, kc, cc * 128:(cc + 1) * 128] if wu is not None else None, self.uT[:, kc, :], kc == 0, kc == 7, [wub, self.uTb], [pub])
                self.act(r_, pu[:, 0:W], AF.Relu, [pub], [rb_])
                self.tt(actT[:, hc, :], r_, r_, ALU.mult, [rb_], [actb[hc // 8]])
        for half in range(2):
            banks = [self.bank("mm") for _ in range(4)]
            for j in range(4):
                wd_, wdb_ = self.W.get("w_mlp_down", l, j * 8, half * 512)
                for cc in range(4):
                    py, pyb = banks[cc]
                    for kk in range(8):
                        self.mm(py[:, 0:W], wd_[:, kk, cc * 128:(cc + 1) * 128] if wd_ is not None else None, actT[:, j * 8 + kk, :], j == 0 and kk == 0, j == 3 and kk == 7, [wdb_, actb[j]], [pyb])
            for cc in range(4):
                oc = half * 4 + cc
                py, pyb = banks[cc]
                self.tt(self.hT[:, oc, :], self.hT[:, oc, :], py[:, 0:W], ALU.add, [pyb, self.hTb[oc]], [self.hTb[oc]])
        self.dump("h3", self.hT[:], self.hTb, ck) if l == 0 else None


HPERM = list(range(16))


def host_consts():
    c = np.zeros((128, 512), np.float32)
    c[:, 0:128] = np.eye(128, dtype=np.float32)
    c[:, 128:256] = np.triu(np.ones((128, 128), np.float32))
    c[:, 256:384] = np.tril(np.ones((128, 128), np.float32), -1) * NEG
    blk = np.zeros((128, 128), np.float32)
    blk[0:64, 0:64] = 1.0
    blk[64:128, 64:128] = 1.0
    c[:, 384:512] = blk
    return c


def host_swab(rel_table):
    qi = np.arange(128)[:, None] + 128
    kj = np.arange(256)[None, :]
    dist = qi - kj
    max_exact = 16
    d = np.maximum(dist, 1).astype(np.float32)
    large = max_exact + (np.log(d / np.float32(max_exact)) / np.float32(math.log(128 / max_exact)) * np.float32(32 - max_exact)).astype(np.int32)
    large = np.minimum(large, 31)
    bucket = np.where(dist < max_exact, np.maximum(dist, 0), large)
    valid = (dist >= 0) & (dist < 128)
    bias = rel_table[bucket]
    bias = np.where(valid[:, :, None], bias, np.float32(NEG)).astype(np.float32)
    out = np.zeros((128, 2, 16, 128), np.float32)
    for side in range(2):
        blkb = bias[:, side * 128:(side + 1) * 128, :]
        out[:, side] = np.transpose(blkb[:, :, HPERM], (1, 2, 0))
    return out.reshape(128, 2 * 16 * 128)


def colv(v):
    return np.ascontiguousarray(np.asarray(v, np.float32).reshape(-1, 128).T)


def host_params(inp, L):
    pcol = np.zeros((128, L, NPC), np.float32)
    prow = np.zeros((128, L, NPR), np.float32)
    for l in range(L):
        pc = pcol[:, l]
        pc[:, PC_NMIX:PC_NMIX + 8] = colv(inp["norm_mix"][l])
        for k in range(3):
            pc[:, PC_GB + k * 8:PC_GB + k * 8 + 8] = colv(inp["gate_bias"][l, k])
        cw = inp["conv_dw_w"][l]
        pc[:, PC_CW:PC_CW + 248] = np.transpose(cw.reshape(31, 8, 128), (2, 1, 0)).reshape(128, 248)
        pc[:, PC_CB:PC_CB + 8] = colv(inp["conv_dw_b"][l])
        pc[:, PC_LG:PC_LG + 8] = colv(inp["conv_ln_g"][l])
        pc[:, PC_LB:PC_LB + 8] = colv(inp["conv_ln_b"][l])
        sw = inp["ssd_conv_w"][l]
        pc[:, PC_SW:PC_SW + 96] = np.transpose(sw.reshape(4, 24, 128), (2, 1, 0)).reshape(128, 96)
        pc[:, PC_SB:PC_SB + 24] = colv(inp["ssd_conv_b"][l])
        pc[:, PC_NG:PC_NG + 16] = colv(inp["ssd_norm_g"][l])
        pc[:, PC_QN] = np.tile(inp["attn_q_norm"][l], 2)
        pc[:, PC_KN] = np.tile(inp["attn_k_norm"][l], 2)
        pc[:, PC_NX:PC_NX + 8] = colv(inp["norm_xattn"][l])
        pc[:, PC_NM:PC_NM + 8] = colv(inp["norm_mem"][l])
        pc[:, PC_XQ:PC_XQ + 2] = colv(inp["xattn_q_norm"][l])
        pc[:, PC_XK:PC_XK + 2] = colv(inp["xattn_k_norm"][l])
        pc[:, PC_NMLP:PC_NMLP + 8] = colv(inp["norm_mlp"][l])
        pr = prow[:, l]
        pr[:, PR_DTB:PR_DTB + 32] = inp["ssd_dt_bias"][l][None, :]
        pr[:, PR_ALOG:PR_ALOG + 32] = inp["ssd_A_log"][l][None, :]
        pr[:, PR_D:PR_D + 32] = inp["ssd_D"][l][None, :]
        pr[:, PR_SINK:PR_SINK + 16] = inp["attn_sinks"][l][HPERM][None, :]
    return pcol.reshape(128, L * NPC), prow.reshape(128, L * NPR)


def make_in_maps(inp, T, L, batches):
    consts = host_consts()
    swab = host_swab(np.asarray(inp["rel_table"], np.float32))
    pcol, prow = host_params(inp, L)
    shared = {"consts": consts, "swab": swab, "pcol": pcol, "prow": prow}
    for n in WSHAPES:
        shared[n] = np.ascontiguousarray(np.asarray(inp[n], np.float32)[:L])
    maps = []
    for b in batches:
        m = dict(shared)
        m["x"] = np.ascontiguousarray(np.asarray(inp["x"], np.float32)[b, :T])
        m["mem"] = np.ascontiguousarray(np.asarray(inp["mem"], np.float32)[b])
        maps.append(m)
    return maps


_NC_CACHE = {}


def kernel(**inputs):
    T, L, TC = 4096, 4, 256
    key = (T, L, TC)
    if key not in _NC_CACHE:
        _NC_CACHE[key] = KB(T, L, TC).build()
    nc = _NC_CACHE[key]
    maps = make_in_maps(inputs, T, L, list(range(8)))
    res = run_bass_kernel_spmd(nc, maps, core_ids=list(range(8)))
    return np.stack([np.asarray(r["y"], np.float32) for r in res.results], axis=0)
```

```python
import math
from contextlib import ExitStack

import numpy as np
import concourse.bass as bass
import concourse.mybir as mybir
from concourse.bass_utils import run_bass_kernel_spmd

F32 = mybir.dt.float32
BF16 = mybir.dt.bfloat16
ALU = mybir.AluOpType
AF = mybir.ActivationFunctionType

D = 1024
EPS = 1e-6
OFF_CONV, OFF_Z, OFF_XBC, OFF_DT, OFF_Q, OFF_K, OFF_V, OFF_GATE, IN_COLS = 0, 2048, 4096, 7168, 7200, 8224, 8480, 8736, 11808
NEG = -30000.0

WSHAPES = {
    "w_in": (1024, IN_COLS), "w_conv_out": (1024, 1024), "w_ssd_out": (2048, 1024), "w_attn_out": (1024, 1024),
    "w_mix_out": (1024, 1024), "w_xq": (1024, 1024), "w_xkv": (1024, 2048), "w_xo": (1024, 1024),
    "w_mlp_up": (1024, 4096), "w_mlp_down": (4096, 1024),
}
WORDER = ["w_in", "w_conv_out", "w_ssd_out", "w_attn_out", "w_mix_out", "w_xkv", "w_xq", "w_xo", "w_mlp_up", "w_mlp_down"]

PC_NMIX, PC_GB, PC_CW, PC_CB, PC_LG, PC_LB, PC_SW, PC_SB, PC_NG, PC_QN, PC_KN, PC_NX, PC_NM, PC_XQ, PC_XK, PC_NMLP, NPC = (
    0, 8, 32, 280, 288, 296, 304, 400, 424, 440, 441, 442, 450, 458, 460, 462, 470)
PR_DTB, PR_ALOG, PR_D, PR_SINK, NPR = 0, 32, 64, 96, 112

EPOCH = 6000


class Track:
    def __init__(self, name, is_dma=False):
        self.name = name
        self.is_dma = is_dma
        self.n = 0
        self.nsig = 0
        self.sems = []


class Buf:
    __slots__ = ("name", "w", "r", "excl")

    def __init__(self, name="b", excl=False):
        self.name = name
        self.w = None
        self.r = {}
        self.excl = excl


class Op:
    __slots__ = ("eng", "track", "seq", "eseq", "fn", "waits", "signal", "sem", "val", "tag")


class Sched:
    def __init__(self, nc, stack, dry=False):
        self.nc = nc
        self.stack = stack
        self.dry = dry
        self.ops = []
        self.engs = {"pe": nc.tensor, "act": nc.scalar, "dve": nc.vector, "pool": nc.gpsimd, "sp": nc.sync}
        self.etrack = {k: Track(k) for k in self.engs}
        self.ecount = {k: 0 for k in self.engs}
        self.seen = {k: {} for k in self.engs}
        self.nsem = 0
        self.epdone = set()

    def new_sem(self, name):
        self.nsem += 1
        return self.stack.enter_context(self.nc.semaphore(f"{name}_{self.nsem}"))

    def _record(self, eng, track, fn, reads, writes):
        if self.dry:
            return None
        op = Op()
        op.eng = eng
        op.track = track
        track.n += 1
        op.seq = track.n
        self.ecount[eng] += 1
        op.eseq = self.ecount[eng]
        op.fn = fn
        op.tag = getattr(self, 'tag', '')
        op.signal = track.is_dma
        op.sem = None
        op.val = 0
        deps = []
        xreads = [b for b in reads if b.excl and b not in writes]
        for b in reads:
            if b.w is not None:
                deps.append((b.w, False))
            if b.excl:
                for o in b.r.values():
                    deps.append((o, True))
        for b in writes:
            if b.w is not None:
                deps.append((b.w, False))
            for o in b.r.values():
                deps.append((o, False))
        waits = []
        seen = self.seen[eng]
        own = self.etrack[eng]
        for d, soft in deps:
            if d.track is own:
                if soft or eng == "pe" or eng == "sp":
                    continue
                if d.eseq < op.eseq - 2:
                    continue
            if d.track is track and track.is_dma:
                continue
            if seen.get(d.track, 0) >= d.seq:
                continue
            seen[d.track] = d.seq
            d.signal = True
            waits.append(d)
        op.waits = waits
        for b in reads:
            b.r[track] = op
        for b in writes:
            b.w = op
            b.r = {}
        self.ops.append(op)
        return op

    def op(self, eng, fn, reads=(), writes=()):
        return self._record(eng, self.etrack[eng], fn, reads, writes)

    def dma(self, track, fn, reads=(), writes=(), eng="sp"):
        return self._record(eng, track, fn, reads, writes)

    def wait_all(self, eng, bufs):
        return self._record(eng, self.etrack[eng], None, bufs, ())

    def emit(self):
        import os
        maxops = int(os.environ.get("MAXOPS", "0"))
        ops = self.ops[:maxops] if maxops else self.ops
        for op in ops:
            e = self.engs[op.eng]
            for d in op.waits:
                t = d.track
                if t.is_dma:
                    epn = EPOCH // 16
                    ep = t.sems.index(d.sem)
                    if ep > 0 and (op.eng, id(t), ep - 1) not in self.epdone:
                        self.epdone.add((op.eng, id(t), ep - 1))
                        e.wait_ge(t.sems[ep - 1], epn * 16)
                e.wait_ge(d.sem, d.val)
            if op.fn is None:
                continue
            inst = op.fn(e)
            if op.signal:
                t = op.track
                k = t.nsig
                t.nsig += 1
                inc = 16 if t.is_dma else 1
                epn = EPOCH // inc
                ep = k // epn
                if ep >= len(t.sems):
                    t.sems.append(self.new_sem(t.name))
                op.sem = t.sems[ep]
                op.val = (k % epn + 1) * inc
                inst.then_inc(op.sem, inc)


class Arena:
    def __init__(self, tensor, nwords):
        self.t = tensor
        self.n = nwords
        self.pos = 0
        self.live = []
        self.pending = {}

    def reset(self):
        for b in self.live:
            if b.w is not None:
                o = self.pending.get(b.w.track)
                if o is None or o.seq < b.w.seq:
                    self.pending[b.w.track] = b.w
            for tr, op in b.r.items():
                o = self.pending.get(tr)
                if o is None or o.seq < op.seq:
                    self.pending[tr] = op
        self.live = []
        self.pos = 0
        self.rawstg = None

    def bufs(self, n):
        out = []
        for _ in range(n):
            b = Buf("arb")
            b.r = dict(self.pending)
            self.live.append(b)
            out.append(b)
        return out

    def alloc(self, words, dtype=F32, pat=None, **kw):
        words = (words + 7) // 8 * 8
        assert self.pos + words <= self.n, f"arena overflow {self.pos}+{words}>{self.n}"
        ap = self.t[:, self.pos:self.pos + words]
        self.pos += words
        if dtype == BF16:
            ap = ap.bitcast(BF16)
        if pat is not None:
            ap = ap.rearrange(pat, **kw)
        b = Buf("ar")
        b.r = dict(self.pending)
        self.live.append(b)
        return ap, b


class WStream:
    NSLOT = 4

    def __init__(self, K):
        self.K = K
        self.plan = []
        self.run = False
        self.i = 0
        self.issued = 0
        nc = self.K.nc
        self.slots = [nc.alloc_sbuf_tensor(f"wslot{i}", [128, 8, 512], BF16) for i in range(self.NSLOT)]
        self.bufs = [Buf(f"wslot{i}") for i in range(self.NSLOT)]
        self.tracks = [Track(f"wsl{i}", True) for i in range(self.NSLOT)]

    def start(self):
        self.run = True
        self.i = 0
        self.issued = 0

    def _issue(self, j):
        name, l, kc0, n0, nk, ncols, kind = self.plan[j]
        s = j % self.NSLOT
        src = self.K.wb[(l, name)][:, kc0:kc0 + nk, n0:n0 + ncols]
        dst = self.slots[s][:, 0:nk, 0:ncols]
        if kind == "qperm":
            for e_ in range(2):
                for i_ in range(4):
                    sp = src[:, :, (e_ * 4 + i_) * 64:(e_ * 4 + i_ + 1) * 64]
                    dp = dst[:, :, (i_ * 2 + e_) * 64:(i_ * 2 + e_ + 1) * 64]
                    self.K.S.dma(self.tracks[s], lambda e, dp=dp, sp=sp: e.dma_start(out=dp, in_=sp),
                                 reads=[self.K.wbuf[(l, name)]], writes=[self.bufs[s]])
            return
        self.K.S.dma(self.tracks[s], lambda e, dst=dst, src=src: e.dma_start(out=dst, in_=src),
                     reads=[self.K.wbuf[(l, name)]], writes=[self.bufs[s]])

    def get(self, name, l, kc0, n0, nk=8, ncols=512, kind=None):
        key = (name, l, kc0, n0, nk, ncols, kind)
        if not self.run:
            self.plan.append(key)
            return self.slots[0], Buf("dummy")
        assert self.plan[self.i] == key, (self.plan[self.i], key)
        while self.issued < min(len(self.plan), self.i + self.NSLOT - 1):
            self._issue(self.issued)
            self.issued += 1
        s = self.i % self.NSLOT
        self.i += 1
        return self.slots[s], self.bufs[s]


class KB:
    def __init__(self, T, L, TC=256, dbg=()):
        self.T, self.L, self.TC = T, L, TC
        self.NT = TC // 128
        self.NCH = T // TC
        self.dbg = set(dbg)

    def PE(self, fn, reads, writes):
        self.S.op("pe", fn, reads, writes)

    def ACT(self, fn, reads, writes):
        self.S.op("act", fn, reads, writes)

    def DVE(self, fn, reads, writes):
        self.S.op("dve", fn, reads, writes)

    def mm(self, out, lhsT, rhs, start, stop, reads, writes):
        self.S.op("pe", lambda e: e.matmul(out, lhsT, rhs, start=start, stop=stop), reads, writes)

    def act(self, out, in_, func, reads, writes, **kw):
        self.S.op("act", lambda e: e.activation(out, in_, func, **kw), reads, writes)

    def tt(self, out, in0, in1, op, reads, writes):
        self.S.op("dve", lambda e: e.tensor_tensor(out=out, in0=in0, in1=in1, op=op), reads, writes)

    def stt(self, out, in0, scalar, in1, op0, op1, reads, writes):
        self.S.op("dve", lambda e: e.scalar_tensor_tensor(out=out, in0=in0, scalar=scalar, in1=in1, op0=op0, op1=op1), reads, writes)

    def ts(self, out, in0, s1, s2, op0, op1, reads, writes):
        if s2 is None:
            self.S.op("dve", lambda e: e.tensor_scalar(out=out, in0=in0, scalar1=s1, scalar2=None, op0=op0), reads, writes)
        else:
            self.S.op("dve", lambda e: e.tensor_scalar(out=out, in0=in0, scalar1=s1, scalar2=s2, op0=op0, op1=op1), reads, writes)

    def bank(self, grp):
        lst = self.bgrp[grp]
        i = self.bpos[grp]
        self.bpos[grp] = (i + 1) % len(lst)
        j = lst[i]
        return self.ps[j], self.psb[j]

    def rsqrt(self, out, in_, scale, reads, writes):
        self.act(out, in_, AF.Ln, reads, writes, scale=scale, bias=EPS)
        self.act(out, out, AF.Exp, writes, writes, scale=-0.5)

    def pcv(self, l, off, n=1):
        return self.pc[:, l * NPC + off: l * NPC + off + n]

    def prv(self, l, off, n):
        return self.pr[:, l * NPR + off: l * NPR + off + n]

    def build(self):
        nc = bass.Bass("TRN2", target_bir_lowering=False)
        self.nc = nc
        T, L, TC = self.T, self.L, self.TC

        def din(n, s, d=F32):
            return nc.dram_tensor(n, s, d, kind="ExternalInput").ap()

        self.x = din("x", [T, D])
        self.mem = din("mem", [256, D])
        self.consts_d = din("consts", [128, 512])
        self.swab_d = din("swab", [128, 2 * 16 * 128])
        self.pcol_d = din("pcol", [128, L * NPC])
        self.prow_d = din("prow", [128, L * NPR])
        self.wd = {n: din(n, [L, k, c]) for n, (k, c) in WSHAPES.items()}
        self.y = nc.dram_tensor("y", [T, D], F32, kind="ExternalOutput").ap()
        self.wb = {}
        self.wbuf = {}
        for l in range(L):
            for n, (k, c) in WSHAPES.items():
                self.wb[(l, n)] = nc.dram_tensor(f"wb_{n}_{l}", [128, k // 128, c], BF16, kind="Internal").ap()
                self.wbuf[(l, n)] = Buf(f"wb_{n}_{l}")
        self.kvs = [nc.dram_tensor(f"kvs_{l}", [128, 4096], BF16, kind="Internal").ap() for l in range(L)]
        self.kvs_b = [Buf(f"kvs{l}") for l in range(L)]
        self.dbg_d = {}
        for name in self.dbg:
            if name == "raw":
                continue
            self.dbg_d[name] = nc.dram_tensor("dbg_" + name, [128, 16 if name in ("ybin",) else 8, T], F32, kind="ExternalOutput").ap()
        self.dbg.discard("raw") if False else None
        self.dbg_b = {n: Buf("dbg" + n) for n in self.dbg}
        self.dbg_t = {n: Track("dbg" + n, True) for n in self.dbg}

        with ExitStack() as st:
            self.W = WStream(self)
            self.S = Sched(nc, st, dry=True)
            self.alloc_all(dry=True)
            self.program()
            self.S = Sched(nc, st, dry=False)
            self.W.start()
            self.program()
            self.S.emit()
        return nc

    def alloc_all(self, dry):
        nc = self.nc
        L, W = self.L, self.TC
        A = nc.alloc_sbuf_tensor

        def R(name, shape, dt=F32):
            return A(name, shape, dt), Buf(name)

        self.hT, _ = R("hT", [128, 8, W])
        self.hTb = [Buf(f"hT{c}") for c in range(8)]
        self.uT, self.uTb = R("uT", [128, 8, W], BF16)
        self.mrg, _ = R("mrg", [128, 8, W])
        self.mrgb = [Buf(f"mrg{c}") for c in range(8)]
        self.state, _ = R("state", [128, L, 2048])
        self.stateb = [[Buf(f"st{l}_{g}") for g in range(4)] for l in range(L)]
        self.bias, self.biasb = R("swabias", [128, 2, 16, 128])
        self.pc, self.pcb = R("pc", [128, L * NPC])
        self.pr, self.prb = R("pr", [128, L * NPR])
        self.arow, self.arowb = R("arow", [128, L * 32])
        self.esink, self.esinkb = R("esink", [128, L * 16])
        self.ctail, _ = R("ctail", [128, L, 8, 30])
        self.ctailb = [[Buf("ct") for c in range(8)] for l in range(L)]
        self.xtail, _ = R("xtail", [128, L, 24, 3])
        self.xtailb = [[Buf("xt") for c in range(24)] for l in range(L)]
        self.kprev, _ = R("kprev", [128, L, 4, 128], BF16)
        self.kprevb = [Buf("kp") for l in range(L)]
        self.vprev, _ = R("vprev", [128, L, 512], BF16)
        self.vprevb = [Buf("vp") for l in range(L)]
        self.cst, self.cstb = R("cst", [128, 512])
        self.ident = self.cst[:, 0:128]
        self.triu = self.cst[:, 128:256]
        self.negm = self.cst[:, 256:384]
        self.blk64 = self.cst[:, 384:512]
        self.identb, self.identbb = R("identb", [128, 128], BF16)
        self.negmb, self.negmbb = R("negmb", [128, 128], BF16)
        self.onesb, self.onesbb = R("onesb", [128, 128], BF16)
        self.onesf, self.onesfb = R("onesf", [128, 128])
        self.ps = [nc.alloc_psum_tensor(f"ps{i}", [128, 512], F32) for i in range(8)]
        self.psb = [Buf(f"ps{i}", excl=True) for i in range(8)]
        self.bgrp = {"mm": [0, 1, 2, 3], "st": [4, 5], "tr": [6, 7]}
        self.bpos = {"mm": 0, "st": 0, "tr": 0}
        nw = (nc.sbuf_bytes_remaining - 2048) // 4
        nw = nw // 8 * 8
        self.ar_t = A("arena", [128, nw], F32)
        self.arena = Arena(self.ar_t, nw)

    def program(self):
        if not self.S.dry:
            self.reset_bufs()
        self.init_phase()
        for ck in range(self.NCH):
            self.load_x(ck)
            import os
            PH = os.environ.get("PH", "ncswmxp")
            for l in range(self.L):
                if "n" in PH:
                    self.rmsnorm(l, PC_NMIX)
                self.dump("u", self.uT[:], self.uTb, ck) if l == 0 else None
                if "c" in PH:
                    self.conv_phase(l, ck)
                if "s" in PH:
                    self.ssd_phase(l, ck)
                if "w" in PH:
                    self.swa_phase(l, ck)
                if "m" in PH:
                    self.mix_phase(l, ck)
                if "x" in PH:
                    if ck == 0:
                        self.kv_precompute(l)
                    self.xattn_phase(l, ck)
                if "p" in PH:
                    self.mlp_phase(l, ck)
            self.store_y(ck)
        self.S.wait_all("sp", [self.yb])

    def reset_bufs(self):
        self.arena = Arena(self.ar_t, self.arena.n)
        self.bpos = {"mm": 0, "st": 0, "tr": 0}

    def dump(self, name, ap, bufs, ck, nchunks=8):
        if name not in self.dbg:
            return
        W = self.TC
        stg, sb = self.arena_dbg(nchunks)
        bl = bufs if isinstance(bufs, list) else [bufs]
        self.ACT(lambda e: e.activation(stg, ap, AF.Identity), bl, [sb])
        dst = self.dbg_d[name][:, :, ck * W:(ck + 1) * W]
        self.S.dma(self.dbg_t[name], lambda e: e.dma_start(out=dst, in_=stg), reads=[sb], writes=[self.dbg_b[name]])

    def rawdump(self, name, ap, bufs, n):
        if "raw" not in self.dbg:
            return
        if not hasattr(self, "raw_d"):
            self.raw_d = {}
        if name not in self.raw_d:
            self.raw_d[name] = self.nc.dram_tensor("raw_" + name, [128, n], F32, kind="ExternalOutput").ap()
        if getattr(self.arena, "rawstg", None) is None:
            self.arena.rawstg = self.arena.alloc(2048)
        stg, sb = self.arena.rawstg
        stg = stg[:, 0:n]
        bl = bufs if isinstance(bufs, list) else [bufs]
        self.ACT(lambda e: e.activation(stg, ap, AF.Identity), bl, [sb])
        dst = self.raw_d[name]
        self.S.dma(Track("raw" + name, True), lambda e: e.dma_start(out=dst, in_=stg), reads=[sb], writes=[Buf("rawo")])

    def arena_dbg(self, nchunks):
        W = self.TC
        return self.arena.alloc(nchunks * W, F32, "p (c w) -> p c w", c=nchunks)

    def init_phase(self):
        S, nc, L = self.S, self.nc, self.L
        self.yb = Buf("y")
        self.ytrack = Track("ytr", True)
        self.xtrack = Track("xtr", True)
        self.kvtrack = Track("kvtr", True)
        self.kvltrack = Track("kvltr", True)
        tr = [Track(f"init{i}", True) for i in range(4)]
        S.dma(tr[0], lambda e: e.dma_start(out=self.cst[:], in_=self.consts_d), writes=[self.cstb])
        S.dma(tr[1], lambda e: e.dma_start(out=self.bias[:].rearrange("p a h q -> p (a h q)"), in_=self.swab_d), writes=[self.biasb])
        S.dma(tr[2], lambda e: e.dma_start(out=self.pc[:], in_=self.pcol_d), writes=[self.pcb])
        S.dma(tr[3], lambda e: e.dma_start(out=self.pr[:], in_=self.prow_d), writes=[self.prb])
        self.cvtrack = {}
        for l in range(L):
            for n in WORDER:
                k, c = WSHAPES[n]
                t = Track(f"cv_{n}_{l}", True)
                for kc in range(k // 128):
                    for n0 in range(0, c, 4096):
                        n1 = min(c, n0 + 4096)
                        dst = self.wb[(l, n)][:, kc, n0:n1]
                        src = self.wd[n][l, kc * 128:(kc + 1) * 128, n0:n1]
                        S.dma(t, lambda e, dst=dst, src=src: e.dma_start(out=dst, in_=src), writes=[self.wbuf[(l, n)]], eng="pool")
        self.DVE(lambda e: e.memset(self.onesb[:], 1.0), [], [self.onesbb])
        self.DVE(lambda e: e.memset(self.onesf[:], 1.0), [], [self.onesfb])
        self.DVE(lambda e: e.tensor_copy(self.identb[:], self.ident), [self.cstb], [self.identbb])
        self.DVE(lambda e: e.tensor_copy(self.negmb[:], self.negm), [self.cstb], [self.negmbb])
        allst = [b for row in self.stateb for b in row]
        self.DVE(lambda e: e.memset(self.state[:].rearrange("p l n -> p (l n)"), 0.0), [], allst)
        self.DVE(lambda e: e.memset(self.ctail[:].rearrange("p l c j -> p (l c j)"), 0.0), [], [b for row in self.ctailb for b in row])
        self.DVE(lambda e: e.memset(self.xtail[:].rearrange("p l c j -> p (l c j)"), 0.0), [], [b for row in self.xtailb for b in row])
        self.DVE(lambda e: e.memset(self.kprev[:].rearrange("p l g k -> p (l g k)"), 0.0), [], self.kprevb)
        self.DVE(lambda e: e.memset(self.vprev[:].rearrange("p l n -> p (l n)"), 0.0), [], self.vprevb)
        for l in range(L):
            a = self.arow[:, l * 32:(l + 1) * 32]
            self.act(a, self.prv(l, PR_ALOG, 32), AF.Exp, [self.prb], [self.arowb])
            self.ts(a, a, -1.0, None, ALU.mult, None, [self.arowb], [self.arowb])
            self.act(self.esink[:, l * 16:(l + 1) * 16], self.prv(l, PR_SINK, 16), AF.Exp, [self.prb], [self.esinkb])

    def load_x(self, ck):
        self.S.tag = 'load_x' + str((ck))
        W, NT = self.TC, self.NT
        ar = self.arena
        ar.reset()
        xin, xb = ar.alloc(NT * 1024, F32, "p (t d) -> p t d", t=NT)
        src = self.x[ck * W:(ck + 1) * W, :].rearrange("(t p) d -> p t d", p=128)
        self.S.dma(self.xtrack, lambda e: e.dma_start(out=xin, in_=src), writes=[xb])
        cpb = 512 // W
        for c0 in range(0, 8, cpb):
            pt, pb = self.bank("tr")
            for cc in range(cpb):
                for t in range(NT):
                    o = pt[:, cc * W + t * 128: cc * W + (t + 1) * 128]
                    i = xin[:, t, (c0 + cc) * 128:(c0 + cc + 1) * 128]
                    self.PE(lambda e, o=o, i=i: e.transpose(o, i, self.ident), [xb, self.cstb], [pb])
            self.DVE(lambda e, c0=c0, pt=pt: e.tensor_copy(self.hT[:, c0:c0 + cpb, :], pt[:, 0:cpb * W].rearrange("p (c w) -> p c w", c=cpb)),
                     [pb], self.hTb[c0:c0 + cpb])

    def store_y(self, ck):
        self.S.tag = 'store_y' + str((ck))
        W, NT = self.TC, self.NT
        ar = self.arena
        ar.reset()
        yo, yob = ar.alloc(NT * 1024, F32, "p (t d) -> p t d", t=NT)
        for t in range(NT):
            for c0 in range(0, 8, 4):
                pt, pb = self.bank("tr")
                for cc in range(4):
                    o = pt[:, cc * 128:(cc + 1) * 128]
                    i = self.hT[:, c0 + cc, t * 128:(t + 1) * 128]
                    self.PE(lambda e, o=o, i=i: e.transpose(o, i, self.ident), [self.hTb[c0 + cc], self.cstb], [pb])
                self.ACT(lambda e, t=t, c0=c0, pt=pt: e.activation(yo[:, t, c0 * 128:(c0 + 4) * 128], pt[:, 0:512], AF.Identity), [pb], [yob])
        dst = self.y[ck * W:(ck + 1) * W, :].rearrange("(t p) d -> p t d", p=128)
        self.S.dma(self.ytrack, lambda e: e.dma_start(out=dst, in_=yo), reads=[yob], writes=[self.yb])

    def rmsnorm(self, l, off):
        self.S.tag = 'rmsnorm' + str((l, off))
        W = self.TC
        ar = self.arena
        ar.reset()
        sq = [ar.alloc(W) for _ in range(2)]
        rs, rsb = ar.alloc(W)
        pst, pstb = self.bank("st")
        for c in range(8):
            s, sb = sq[c % 2]
            self.act(s, self.hT[:, c, :], AF.Square, [self.hTb[c]], [sb])
            self.mm(pst[:, 0:W], self.onesf[:], s, c == 0, c == 7, [sb, self.onesfb], [pstb])
        self.rsqrt(rs, pst[:, 0:W], 1.0 / D, [pstb], [rsb])
        for c in range(8):
            self.stt(self.uT[:, c, :], self.hT[:, c, :], self.pcv(l, off + c), rs, ALU.mult, ALU.mult,
                     [self.hTb[c], rsb, self.pcb], [self.uTb])

    def branch_out(self, l, k, srcT, srcb, wname, KC, scratch=None):
        W = self.TC
        ar = self.arena
        if scratch is None:
            gs = [ar.alloc(W) for _ in range(4)]
            tmp = [ar.alloc(W) for _ in range(2)]
        else:
            gs, tmp = scratch[0:4], scratch[4:6]
        for half in range(2):
            wg, wgb = self.W.get("w_in", l, 0, OFF_GATE + k * 1024 + half * 512)
            for cc in range(4):
                oc = half * 4 + cc
                pg, pgb = self.bank("st")
                for kc in range(8):
                    self.mm(pg[:, 0:W], wg[:, kc, cc * 128:(cc + 1) * 128] if wg is not None else None, self.uT[:, kc, :], kc == 0, kc == 7, [wgb, self.uTb], [pgb])
                self.act(gs[cc][0], pg[:, 0:W], AF.Sigmoid, [pgb, self.pcb], [gs[cc][1]], bias=self.pcv(l, PC_GB + k * 8 + oc))
            banks = [self.bank("mm") for _ in range(4)]
            for j in range(KC // 8):
                wt, wtb = self.W.get(wname, l, j * 8, half * 512)
                for cc in range(4):
                    py, pyb = banks[cc]
                    for kk in range(8):
                        kc = j * 8 + kk
                        self.mm(py[:, 0:W], wt[:, kk, cc * 128:(cc + 1) * 128] if wt is not None else None, srcT[:, kc, :], kc == 0, kc == KC - 1, [wtb, srcb], [pyb])
            for cc in range(4):
                oc = half * 4 + cc
                py, pyb = banks[cc]
                if k == 0:
                    self.tt(self.mrg[:, oc, :], py[:, 0:W], gs[cc][0], ALU.mult, [pyb, gs[cc][1]], [self.mrgb[oc]])
                else:
                    t_, tb_ = tmp[cc % 2]
                    self.tt(t_, py[:, 0:W], gs[cc][0], ALU.mult, [pyb, gs[cc][1]], [tb_])
                    self.tt(self.mrg[:, oc, :], self.mrg[:, oc, :], t_, ALU.add, [tb_, self.mrgb[oc]], [self.mrgb[oc]])

    def conv_phase(self, l, ck):
        self.S.tag = 'conv_phase' + str((l, ck))
        W = self.TC
        ar = self.arena
        ar.reset()
        convT, _ = ar.alloc(8 * W, F32, "p (c w) -> p c w", c=8)
        convb = ar.bufs(8)
        actT, actb = ar.alloc(4 * W, BF16, "p (c w) -> p c w", c=8)
        glu = [ar.alloc(W + 32) for _ in range(4)]
        sig = [ar.alloc(W) for _ in range(2)]
        sq = [ar.alloc(W) for _ in range(2)]
        mean, meanb = ar.alloc(W)
        m2, m2b = ar.alloc(W)
        rstd, rstdb = ar.alloc(W)
        nmr, nmrb = ar.alloc(W)
        tmpn = [ar.alloc(W) for _ in range(2)]
        ps1, ps1b = self.bank("st")
        ps2, ps2b = self.bank("st")
        for half in range(2):
            wa, wab = self.W.get("w_in", l, 0, OFF_CONV + half * 512)
            wg, wgb = self.W.get("w_in", l, 0, OFF_CONV + 1024 + half * 512)
            for cc in range(4):
                c = half * 4 + cc
                g_, gb_ = glu[cc]
                s_, sb_ = sig[c % 2]
                pa, pab = self.bank("mm")
                pg, pgb = self.bank("mm")
                for kc in range(8):
                    self.mm(pa[:, 0:W], wa[:, kc, cc * 128:(cc + 1) * 128], self.uT[:, kc, :], kc == 0, kc == 7, [wab, self.uTb], [pab])
                for kc in range(8):
                    self.mm(pg[:, 0:W], wg[:, kc, cc * 128:(cc + 1) * 128], self.uT[:, kc, :], kc == 0, kc == 7, [wgb, self.uTb], [pgb])
                self.act(s_, pg[:, 0:W], AF.Sigmoid, [pgb], [sb_])
                self.act(g_[:, 0:30], self.ctail[:, l, c, :], AF.Identity, [self.ctailb[l][c]], [gb_])
                self.tt(g_[:, 30:30 + W], pa[:, 0:W], s_, ALU.mult, [pab, sb_], [gb_])
            for j in range(31):
                for cc in range(4):
                    c = half * 4 + cc
                    g_, gb_ = glu[cc]
                    cw = PC_CW + c * 31
                    if j == 0:
                        self.ts(convT[:, c, :], g_[:, 0:W], self.pcv(l, cw), self.pcv(l, PC_CB + c), ALU.mult, ALU.add, [gb_, self.pcb], [convb[c]])
                    else:
                        self.stt(convT[:, c, :], g_[:, j:j + W], self.pcv(l, cw + j), convT[:, c, :], ALU.mult, ALU.add, [gb_, self.pcb, convb[c]], [convb[c]])
            for cc in range(4):
                c = half * 4 + cc
                g_, gb_ = glu[cc]
                q_, qb_ = sq[c % 2]
                self.act(self.ctail[:, l, c, :], g_[:, W:W + 30], AF.Identity, [gb_], [self.ctailb[l][c]])
                self.act(q_, convT[:, c, :], AF.Square, [convb[c]], [qb_])
                self.mm(ps1[:, 0:W], self.onesf[:], convT[:, c, :], c == 0, c == 7, [convb[c], self.onesfb], [ps1b])
                self.mm(ps2[:, 0:W], self.onesf[:], q_, c == 0, c == 7, [qb_, self.onesfb], [ps2b])
        if l == 0 and ck == 0:
            self.rawdump("glu7", glu[3][0][:, 0:W + 32], glu[3][1], W + 32)
            self.rawdump("sig7", sig[1][0], sig[1][1], W)
            self.rawdump("conv7", convT[:, 7, :], convb[7], W)
            self.rawdump("conv0", convT[:, 0, :], convb[0], W)
        self.act(mean, ps1[:, 0:W], AF.Identity, [ps1b], [meanb], scale=1.0 / D)
        self.tt(m2, mean, mean, ALU.mult, [meanb], [m2b])
        self.stt(rstd, ps2[:, 0:W], 1.0 / D, m2, ALU.mult, ALU.subtract, [ps2b, m2b], [rstdb])
        self.rsqrt(rstd, rstd, 1.0, [rstdb], [rstdb])
        self.stt(nmr, mean, -1.0, rstd, ALU.mult, ALU.mult, [meanb, rstdb], [nmrb])
        for c in range(8):
            t_, tb_ = tmpn[c % 2]
            self.tt(t_, convT[:, c, :], rstd, ALU.mult, [convb[c], rstdb], [tb_])
            self.tt(t_, t_, nmr, ALU.add, [tb_, nmrb], [tb_])
            self.act(actT[:, c, :], t_, AF.Silu, [tb_, self.pcb], [actb], scale=self.pcv(l, PC_LG + c), bias=self.pcv(l, PC_LB + c))
        if l == 0 and ck == 0:
            self.rawdump("mean", mean, meanb, W)
            self.rawdump("rstd", rstd, rstdb, W)
        self.dump("cact", actT, actb, ck) if l == 0 else None
        self.branch_out(l, 0, actT, actb, "w_conv_out", 8)
        self.dump("m0", self.mrg[:], self.mrgb, ck) if l == 0 else None

    def ssd_phase(self, l, ck):
        self.S.tag = 'ssd_phase' + str((l, ck))
        W, NT = self.TC, self.NT
        ar = self.arena
        ar.reset()
        xtok, xtokb = ar.alloc(NT * 1024, BF16, "p (t n) -> p t n", t=NT)
        zs, zsb = ar.alloc(NT * 1024, BF16, "p (t n) -> p t n", t=NT)
        yT, yTb = ar.alloc(8 * W, BF16, "p (c w) -> p c w", c=16)
        bcT, _ = ar.alloc(4 * W, BF16, "p (c w) -> p c w", c=8)
        bcb = ar.bufs(8)
        btok, btokb = ar.alloc(NT * 256, BF16, "p (t n) -> p t n", t=NT)
        xdt, xdtb = ar.alloc(1024, BF16)
        xdte, xdteb = ar.alloc(1024, BF16)
        yw, _ = ar.alloc(2048)
        ywb = ar.bufs(4)
        ytok, ytokb = ar.alloc(1024, BF16)
        stbf, stbfb = ar.alloc(1024, BF16)
        xw = [ar.alloc(W + 8) for _ in range(3)]
        acc = [ar.alloc(W) for _ in range(3)]
        xc = [ar.alloc(W // 2, BF16) for _ in range(3)]
        lt = [ar.alloc(512) for _ in range(2)]
        sc = [ar.alloc(256, BF16, "p (j q) -> p j q", j=4) for _ in range(2)]
        cbm, cbmb = ar.alloc(512, F32, "p (g q) -> p g q", g=4)
        t1 = [ar.alloc(512) for _ in range(2)]
        junk, junkb = ytok.bitcast(F32)[:, 0:512], ytokb
        dtall, dtallb = ar.alloc(NT * 32, F32, "p (t h) -> p t h", t=NT)
        daall, daallb = ar.alloc(NT * 32, F32, "p (t h) -> p t h", t=NT)
        cs, csb = ar.alloc(32)
        dfs, dfsb = ar.alloc(32)
        cd, cdb = ar.alloc(32)
        dte, dteb = ar.alloc(32)
        ss, ssb = ar.alloc(8)
        rs4, rs4b = ar.alloc(8)

        for zt in range(4):
            wz, wzb = self.W.get("w_in", l, 0, OFF_Z + zt * 512)
            for t in range(NT):
                pz, pzb = self.bank("mm")
                for kc in range(8):
                    self.mm(pz[:, 0:512], self.uT[:, kc, t * 128:(t + 1) * 128], wz[:, kc, :] if wz is not None else None, kc == 0, kc == 7, [wzb, self.uTb], [pzb])
                self.act(zs[:, t, zt * 512:(zt + 1) * 512], pz[:, 0:512], AF.Silu, [pzb], [zsb])
        st = {}

        def stage_a(c):
            if c % 4 == 0:
                st["w"] = self.W.get("w_in", l, 0, OFF_XBC + (c // 4) * 512)
            wx, wxb = st["w"]
            cc = c % 4
            xw_, xwb_ = xw[c % 3]
            pb_, pbb_ = self.bank("mm")
            for kc in range(8):
                self.mm(pb_[:, 0:W], wx[:, kc, cc * 128:(cc + 1) * 128], self.uT[:, kc, :], kc == 0, kc == 7, [wxb, self.uTb], [pbb_])
            self.act(xw_[:, 0:3], self.xtail[:, l, c, :], AF.Identity, [self.xtailb[l][c]], [xwb_])
            self.act(xw_[:, 3:3 + W], pb_[:, 0:W], AF.Identity, [pbb_], [xwb_])

        def stage_b(c):
            xw_, xwb_ = xw[c % 3]
            a_, ab_ = acc[c % 3]
            sw = PC_SW + c * 4
            self.ts(a_, xw_[:, 0:W], self.pcv(l, sw), self.pcv(l, PC_SB + c), ALU.mult, ALU.add, [xwb_, self.pcb], [ab_])
            for j in range(1, 4):
                self.stt(a_, xw_[:, j:j + W], self.pcv(l, sw + j), a_, ALU.mult, ALU.add, [xwb_, self.pcb, ab_], [ab_])
            self.act(self.xtail[:, l, c, :], xw_[:, W:W + 3], AF.Identity, [xwb_], [self.xtailb[l][c]])
            if c < 16:
                self.act(xc[c % 3][0], a_, AF.Silu, [ab_], [xc[c % 3][1]])
            else:
                self.act(bcT[:, c - 16, :], a_, AF.Silu, [ab_], [bcb[c - 16]])

        def stage_c(c):
            if c >= 20:
                return
            if c < 16:
                src, srcb = xc[c % 3]
            else:
                src, srcb = bcT[:, c - 16, :], bcb[c - 16]
            pt, ptb = self.bank("tr")
            ptv = pt[:].bitcast(BF16)
            for t in range(NT):
                self.PE(lambda e, o=ptv[:, t * 128:(t + 1) * 128], i=src[:, t * 128:(t + 1) * 128]: e.transpose(o, i, self.identb[:]), [srcb, self.identbb], [ptb])
            if c < 16:
                self.DVE(lambda e, c=c, ptv=ptv: e.tensor_copy(xtok[:, :, c * 128:(c + 1) * 128], ptv[:, 0:NT * 128].rearrange("p (t n) -> p t n", t=NT)), [ptb], [xtokb])
            else:
                self.DVE(lambda e, c=c, ptv=ptv: e.tensor_copy(btok[:, :, (c - 16) * 128:(c - 15) * 128], ptv[:, 0:NT * 128].rearrange("p (t n) -> p t n", t=NT)), [ptb], [btokb])

        for step in range(26):
            if step < 24:
                stage_a(step)
            if 0 <= step - 1 < 24:
                stage_b(step - 1)
            if 0 <= step - 2 < 24:
                stage_c(step - 2)
        wdt, wdtb = self.W.get("w_in", l, 0, OFF_DT, 8, 32)
        for t in range(NT):
            pd, pdb = self.bank("tr")
            for kc in range(8):
                self.mm(pd[:, 0:32], self.uT[:, kc, t * 128:(t + 1) * 128], wdt[:, kc, 0:32] if wdt is not None else None, kc == 0, kc == 7, [wdtb, self.uTb], [pdb])
            self.tt(dtall[:, t, :], pd[:, 0:32], self.prv(l, PR_DTB, 32), ALU.add, [pdb, self.prb], [dtallb])
        dflat = dtall.rearrange("p t h -> p (t h)")
        self.act(dflat, dflat, AF.Exp, [dtallb], [dtallb])
        self.act(dflat, dflat, AF.Ln, [dtallb], [dtallb], bias=1.0)
        for t in range(NT):
            self.tt(daall[:, t, :], dtall[:, t, :], self.arow[:, l * 32:(l + 1) * 32], ALU.mult, [dtallb, self.arowb], [daallb])

        stl = self.state[:, l, :]
        stb = self.stateb[l]
        import os
        for t in (range(NT) if 'core' not in os.environ.get('SSDSKIP', '') else []):
            tsl = slice(t * 128, (t + 1) * 128)
            dtv = dtall[:, t, :]
            dav = daall[:, t, :]
            self.act(stbf, stl, AF.Identity, stb, [stbfb])
            pc_, pcb_ = self.bank("tr")
            self.mm(pc_[:, 0:32], self.triu, dav, True, True, [self.cstb, daallb], [pcb_])
            self.mm(pc_[:, 32:64], self.onesf[:], dav, True, True, [self.onesfb, daallb], [pcb_])
            self.act(cs, pc_[:, 0:32], AF.Identity, [pcb_], [csb])
            self.act(dfs, pc_[:, 0:32], AF.Exp, [pcb_], [dfsb])
            self.act(cd, pc_[:, 32:64], AF.Exp, [pcb_], [cdb])
            self.tt(dte, pc_[:, 32:64], cs, ALU.subtract, [pcb_, csb], [dteb])
            self.act(dte, dte, AF.Exp, [dteb], [dteb])
            x3 = xtok[:, t, :].rearrange("p (h d) -> p h d", h=32)
            self.tt(xdt.rearrange("p (h d) -> p h d", h=32), x3, dtv.unsqueeze(2).to_broadcast([128, 32, 64]), ALU.mult, [xtokb, dtallb], [xdtb])
            self.tt(xdte.rearrange("p (h d) -> p h d", h=32), xdt.rearrange("p (h d) -> p h d", h=32), dte.unsqueeze(2).to_broadcast([128, 32, 64]), ALU.mult, [xdtb, dteb], [xdteb])
            pcb2, pcb2b = self.bank("mm")
            for g in range(4):
                self.mm(pcb2[:, g * 128:(g + 1) * 128], bcT[:, g, tsl], bcT[:, 4 + g, tsl], True, True, [bcb[g], bcb[4 + g]], [pcb2b])
            self.tt(cbm, pcb2[:, 0:512].rearrange("p (g q) -> p g q", g=4), self.triu.unsqueeze(1).to_broadcast([128, 4, 128]), ALU.mult, [pcb2b, self.cstb], [cbmb])
            self.DVE(lambda e: e.memset(ss, 0.0), [], [ssb])
            mmb = [(self.ps[i], self.psb[i]) for i in self.bgrp["mm"]]
            stbk = [(self.ps[i], self.psb[i]) for i in self.bgrp["st"]]

            def csq(b):
                h0 = b * 4
                pq, pqb = stbk[b % 2]
                for j in range(4):
                    self.mm(pq[:, j * 128:(j + 1) * 128], dav[:, h0 + j:h0 + j + 1].to_broadcast([128, 128]), self.triu, True, True, [daallb, self.cstb], [pqb])

            csq(0)
            for g in range(4):
                py, pyb = mmb[g % 2]
                po, pob = mmb[2]
                pn, pnb = mmb[3]
                self.mm(po[:, 0:512], bcT[:, 4 + g, tsl], stbf[:, g * 512:(g + 1) * 512], True, True, [bcb[4 + g], stbfb], [pob])
                self.mm(pn[:, 0:512], btok[:, t, g * 128:(g + 1) * 128], xdte[:, g * 512:(g + 1) * 512], True, True, [btokb, xdteb], [pnb])
                ta, tab = t1[0]
                tb, tbb = t1[1]
                self.tt(ta.rearrange("p (h d) -> p h d", h=8), po[:, 0:512].rearrange("p (h d) -> p h d", h=8), dfs[:, g * 8:(g + 1) * 8].unsqueeze(2).to_broadcast([128, 8, 64]), ALU.mult, [pob, dfsb], [tab])
                sg = stl[:, g * 512:(g + 1) * 512]
                self.tt(sg.rearrange("p (h d) -> p h d", h=8), sg.rearrange("p (h d) -> p h d", h=8), cd[:, g * 8:(g + 1) * 8].unsqueeze(2).to_broadcast([128, 8, 64]), ALU.mult, [stb[g], cdb, stbfb], [stb[g]])
                self.tt(sg, sg, pn[:, 0:512], ALU.add, [stb[g], pnb], [stb[g]])
                for hb in range(2):
                    b = g * 2 + hb
                    h0 = b * 4
                    lt_, ltb_ = lt[hb]
                    sc_, scb_ = sc[hb]
                    pq, pqb = stbk[b % 2]
                    if b + 1 < 8:
                        csq(b + 1)
                    lt3 = lt_.rearrange("p (j q) -> p j q", j=4)
                    self.tt(lt3, pq[:, 0:512].rearrange("p (j q) -> p j q", j=4), cs[:, h0:h0 + 4].unsqueeze(2).to_broadcast([128, 4, 128]), ALU.subtract, [pqb, csb], [ltb_])
                    self.act(lt_, lt_, AF.Relu, [ltb_], [ltb_], scale=-1.0)
                    self.act(lt_, lt_, AF.Exp, [ltb_], [ltb_], scale=-1.0)
                    self.tt(sc_, lt3, cbm[:, g, :].unsqueeze(1).to_broadcast([128, 4, 128]), ALU.mult, [ltb_, cbmb], [scb_])
                    for j in range(4):
                        h = h0 + j
                        self.mm(py[:, (hb * 4 + j) * 64:(hb * 4 + j + 1) * 64], sc_[:, j, :], xdt[:, h * 64:(h + 1) * 64], True, True, [scb_, xdtb], [pyb])
                ywg = yw[:, g * 512:(g + 1) * 512]
                self.tt(ywg, py[:, 0:512], ta, ALU.add, [pyb, tab], [ywb[g]])
                self.tt(tb.rearrange("p (h d) -> p h d", h=8), xtok[:, t, g * 512:(g + 1) * 512].rearrange("p (h d) -> p h d", h=8),
                        self.prv(l, PR_D + g * 8, 8).unsqueeze(2).to_broadcast([128, 8, 64]), ALU.mult, [xtokb, self.prb], [tbb])
                self.tt(ywg, ywg, tb, ALU.add, [ywb[g], tbb], [ywb[g]])
                self.tt(ywg, ywg, zs[:, t, g * 512:(g + 1) * 512], ALU.mult, [ywb[g], zsb], [ywb[g]])
                self.act(junk, ywg, AF.Square, [ywb[g]], [junkb, ssb], accum_out=ss[:, g:g + 1])
            if l == 0 and ck == 0 and t == 0:
                self.rawdump("dt", dtall.rearrange("p t h -> p (t h)"), dtallb, NT * 32)
                self.rawdump("da", daall.rearrange("p t h -> p (t h)"), daallb, NT * 32)
                self.rawdump("cs", cs, csb, 32)
                self.rawdump("dfs", dfs, dfsb, 32)
                self.rawdump("cd", cd, cdb, 32)
                self.rawdump("dte", dte, dteb, 32)
                self.rawdump("xtok", xtok[:, 0, :], xtokb, 2048)
                self.rawdump("btok", btok[:, 0, :], btokb, 512)
                self.rawdump("xdt", xdt, xdtb, 2048)
                self.rawdump("zs", zs[:, 0, :], zsb, 2048)
                self.rawdump("cbm", cbm.rearrange("p g q -> p (g q)"), cbmb, 512)
                self.rawdump("lt", lt[1][0], lt[1][1], 512)
                self.rawdump("yw", yw, ywb, 2048)
                self.rawdump("ss", ss, ssb, 8)
                self.rawdump("bc", bcT.rearrange("p c w -> p (c w)"), bcb, 8 * W)
            self.rsqrt(rs4[:, 0:4], ss[:, 0:4], 1.0 / 512, [ssb], [rs4b])
            self.tt(ytok.rearrange("p (g n) -> p g n", g=4), yw.rearrange("p (g n) -> p g n", g=4), rs4[:, 0:4].unsqueeze(2).to_broadcast([128, 4, 512]), ALU.mult, ywb + [rs4b], [ytokb])
            for half in range(2):
                pt, ptb = self.bank("tr")
                ptv = pt[:].bitcast(BF16)
                for j in range(8):
                    c = half * 8 + j
                    self.PE(lambda e, o=ptv[:, j * 128:(j + 1) * 128], i=ytok[:, c * 128:(c + 1) * 128]: e.transpose(o, i, self.identb[:]), [ytokb, self.identbb], [ptb])
                self.tt(yT[:, half * 8:(half + 1) * 8, tsl], ptv[:, 0:1024].rearrange("p (c q) -> p c q", c=8),
                        self.pcv(l, PC_NG + half * 8, 8).unsqueeze(2).to_broadcast([128, 8, 128]), ALU.mult, [ptb, self.pcb], [yTb])
        self.dump("ybin", yT, yTb, ck, 16) if l == 0 else None
        WW = 512 // 2
        scr = [(lt[0][0][:, 0:W], lt[0][1]), (lt[0][0][:, WW:WW + W], lt[0][1]), (lt[1][0][:, 0:W], lt[1][1]), (lt[1][0][:, WW:WW + W], lt[1][1]),
               (t1[0][0][:, 0:W], t1[0][1]), (t1[1][0][:, 0:W], t1[1][1])] if W <= 256 else None
        self.branch_out(l, 1, yT, yTb, "w_ssd_out", 16, scratch=scr)
        self.dump("m1", self.mrg[:], self.mrgb, ck) if l == 0 else None

    def qknorm_chunk(self, pq, pqb, hd_scale, stat_lhsT, stat_b, sq, rs):
        pass

    def swa_phase(self, l, ck):
        self.S.tag = 'swa_phase' + str((l, ck))
        W, NT = self.TC, self.NT
        ar = self.arena
        ar.reset()
        qT, qTb = ar.alloc(4 * W, BF16, "p (c w) -> p c w", c=8)
        oT, oTb = ar.alloc(4 * W, BF16, "p (c w) -> p c w", c=8)
        KW = 128 + W
        kT, kTb = ar.alloc(2 * KW, BF16, "p (g k) -> p g k", g=4)
        vt, vtb = ar.alloc((1 + NT) * 256, BF16, "p (t n) -> p t n", t=1 + NT)
        sq = [ar.alloc(W) for _ in range(2)]
        rs = [ar.alloc(W) for _ in range(2)]
        pts = [[ar.alloc(256, BF16) for _ in range(2)] for _ in range(2)]
        tbs = [[ar.alloc(512) for _ in range(2)] for _ in range(2)]
        dtot, dtotb = ar.alloc(512)
        rd, rdb = ar.alloc(512)
        self.DVE(lambda e: e.tensor_copy(kT[:, :, 0:128], self.kprev[:, l]), [self.kprevb[l]], [kTb])
        self.DVE(lambda e: e.tensor_copy(vt[:, 0, :], self.vprev[:, l, :]), [self.vprevb[l]], [vtb])
        kT2 = kT.rearrange("p (m e) k -> p m e k", e=2)
        self.DVE(lambda e: e.memset(kT2[64:128, :, 0, 128:KW], 0.0), [], [kTb])
        self.DVE(lambda e: e.memset(kT2[0:64, :, 1, 128:KW], 0.0), [], [kTb])
        gq = self.pcv(l, PC_QN)
        gk = self.pcv(l, PC_KN)
        for tile in range(2):
            wq, wqb = self.W.get("w_in", l, 0, OFF_Q + tile * 512, kind="qperm")
            for cc in range(4):
                c = tile * 4 + cc
                s_, sb_ = sq[c % 2]
                r_, rb_ = rs[c % 2]
                pq, pqb = self.bank("mm")
                for kc in range(8):
                    self.mm(pq[:, 0:W], wq[:, kc, cc * 128:(cc + 1) * 128], self.uT[:, kc, :], kc == 0, kc == 7, [wqb, self.uTb], [pqb])
                self.act(s_, pq[:, 0:W], AF.Square, [pqb], [sb_])
                pst, pstb = self.bank("st")
                self.mm(pst[:, 0:W], self.blk64, s_, True, True, [self.cstb, sb_], [pstb])
                self.rsqrt(r_, pst[:, 0:W], 1.0 / 64, [pstb], [rb_])
                self.stt(qT[:, c, :], pq[:, 0:W], gq, r_, ALU.mult, ALU.mult, [pqb, rb_, self.pcb], [qTb])
        wkv, wkvb = self.W.get("w_in", l, 0, OFF_K)
        for m in range(2):
            s_, sb_ = sq[m % 2]
            r_, rb_ = rs[m % 2]
            pk, pkb = self.bank("mm")
            for kc in range(8):
                self.mm(pk[:, 0:W], wkv[:, kc, m * 128:(m + 1) * 128], self.uT[:, kc, :], kc == 0, kc == 7, [wkvb, self.uTb], [pkb])
            self.act(s_, pk[:, 0:W], AF.Square, [pkb], [sb_])
            pst, pstb = self.bank("st")
            self.mm(pst[:, 0:W], self.blk64, s_, True, True, [self.cstb, sb_], [pstb])
            self.rsqrt(r_, pst[:, 0:W], 1.0 / 64, [pstb], [rb_])
            self.stt(kT[0:64, 2 * m, 128:KW], pk[0:64, 0:W], gk[0:64], r_[0:64], ALU.mult, ALU.mult, [pkb, rb_, self.pcb], [kTb])
            self.stt(kT[64:128, 2 * m + 1, 128:KW], pk[64:128, 0:W], gk[64:128], r_[64:128], ALU.mult, ALU.mult, [pkb, rb_, self.pcb], [kTb])
        for t in range(NT):
            pv, pvb = self.bank("mm")
            for kc in range(8):
                self.mm(pv[:, 0:256], self.uT[:, kc, t * 128:(t + 1) * 128], wkv[:, kc, 256:512], kc == 0, kc == 7, [wkvb, self.uTb], [pvb])
            self.DVE(lambda e, t=t, pv=pv: e.tensor_copy(vt[:, 1 + t, :].rearrange("p (g e d) -> p g e d", g=4, e=2),
                                                       pv[:, 0:256].rearrange("p (g d) -> p g d", g=4).unsqueeze(2).to_broadcast([128, 4, 2, 64])), [pvb], [vtb])
        steps = [(t, g) for t in range(NT) for g in range(4)]

        def sides_of(t):
            return ([0] if ck * NT + t > 0 else []) + [1]

        def score_stage(i):
            t, g = steps[i]
            tsl = slice(t * 128, (t + 1) * 128)
            for side in sides_of(t):
                ko = t * 128 + side * 128
                pS, pSb = self.bank("mm")
                for ii in range(4):
                    self.mm(pS[:, ii * 128:(ii + 1) * 128], kT[:, g, ko:ko + 128], qT[:, 4 * (g // 2) + ii, tsl], True, True, [kTb, qTb], [pSb])
                tb_, tbb_ = tbs[i % 2][side]
                p_, pb_ = pts[i % 2][side]
                self.stt(tb_, pS[:, 0:512], 0.125, self.bias[:, side, g * 4:(g + 1) * 4, :].rearrange("p h q -> p (h q)"), ALU.mult, ALU.add, [pSb, self.biasb], [tbb_])
                self.act(p_, tb_, AF.Exp, [tbb_], [pb_])

        def pv_stage(i):
            t, g = steps[i]
            tsl = slice(t * 128, (t + 1) * 128)
            sides = sides_of(t)
            pO, pOb = self.bank("mm")
            pD, pDb = self.bank("st")
            for k_, side in enumerate(sides):
                p_, pb_ = pts[i % 2][side]
                self.mm(pO[:, 0:512], vt[:, t + side, g * 128:(g + 1) * 128], p_, k_ == 0, k_ == len(sides) - 1, [vtb, pb_], [pOb])
            for k_, side in enumerate(sides):
                p_, pb_ = pts[i % 2][side]
                self.mm(pD[:, 0:512], self.onesb[:], p_, k_ == 0, k_ == len(sides) - 1, [self.onesbb, pb_], [pDb])
            self.tt(dtot.rearrange("p (h q) -> p h q", h=4), pD[:, 0:512].rearrange("p (h q) -> p h q", h=4),
                    self.esink[:, l * 16 + g * 4:l * 16 + g * 4 + 4].unsqueeze(2).to_broadcast([128, 4, 128]), ALU.add, [pDb, self.esinkb], [dtotb])
            self.DVE(lambda e: e.reciprocal(rd, dtot), [dtotb], [rdb])
            for e_ in range(2):
                rows = slice(64 * e_, 64 * e_ + 64)
                self.tt(oT[rows, 2 * g:2 * g + 2, tsl], pO[rows, 0:512].rearrange("p (c e q) -> p c e q", c=2, e=2)[:, :, e_, :],
                        rd[rows, 0:512].rearrange("p (c e q) -> p c e q", c=2, e=2)[:, :, e_, :], ALU.mult, [pOb, rdb], [oTb])

        score_stage(0)
        for i in range(len(steps)):
            if i + 1 < len(steps):
                score_stage(i + 1)
            pv_stage(i)
        self.DVE(lambda e: e.tensor_copy(self.kprev[:, l], kT[:, :, W:W + 128]), [kTb], [self.kprevb[l]])
        self.DVE(lambda e: e.tensor_copy(self.vprev[:, l, :], vt[:, NT, :]), [vtb], [self.vprevb[l]])
        self.dump("ycin", oT, oTb, ck) if l == 0 else None
        self.branch_out(l, 2, oT, oTb, "w_attn_out", 8)
        self.dump("m2", self.mrg[:], self.mrgb, ck) if l == 0 else None

    def proj_add(self, l, wname, srcT, srcb):
        W = self.TC
        for half in range(2):
            wt, wtb = self.W.get(wname, l, 0, half * 512)
            for cc in range(4):
                oc = half * 4 + cc
                py, pyb = self.bank("mm")
                for kc in range(8):
                    self.mm(py[:, 0:W], wt[:, kc, cc * 128:(cc + 1) * 128] if wt is not None else None, srcT[:, kc, :], kc == 0, kc == 7, [wtb, srcb], [pyb])
                self.tt(self.hT[:, oc, :], self.hT[:, oc, :], py[:, 0:W], ALU.add, [pyb, self.hTb[oc]], [self.hTb[oc]])

    def mix_phase(self, l, ck):
        self.S.tag = 'mix_phase' + str((l, ck))
        W = self.TC
        ar = self.arena
        ar.reset()
        mb, mbb = ar.alloc(4 * W, BF16, "p (c w) -> p c w", c=8)
        for c in range(8):
            self.act(mb[:, c, :], self.mrg[:, c, :], AF.Identity, [self.mrgb[c]], [mbb])
        self.proj_add(l, "w_mix_out", mb, mbb)
        self.dump("h1", self.hT[:], self.hTb, ck) if l == 0 else None

    def kv_precompute(self, l):
        self.S.tag = 'kv_precompute' + str((l))
        ar = self.arena
        ar.reset()
        memx, memxb = ar.alloc(2048, F32, "p (t d) -> p t d", t=2)
        memn, memnb = ar.alloc(1024, BF16, "p (t d) -> p t d", t=2)
        memT, memTb = ar.alloc(1024, BF16, "p (c k) -> p c k", c=8)
        kvo, kvob = ar.alloc(2048, BF16)
        junk, junkb = ar.alloc(1024)
        ms, msb = ar.alloc(8)
        sq = [ar.alloc(256) for _ in range(2)]
        rs, rsb = ar.alloc(256)
        src = self.mem.rearrange("(t p) d -> p t d", p=128)
        self.S.dma(self.kvltrack, lambda e: e.dma_start(out=memx, in_=src), writes=[memxb])
        self.DVE(lambda e: e.memset(ms, 0.0), [], [msb])
        for kt in range(2):
            self.act(junk, memx[:, kt, :], AF.Square, [memxb], [junkb, msb], accum_out=ms[:, kt:kt + 1])
        self.rsqrt(ms[:, 0:2], ms[:, 0:2], 1.0 / D, [msb], [msb])
        for kt in range(2):
            self.ts(memn[:, kt, :], memx[:, kt, :], ms[:, kt:kt + 1], None, ALU.mult, None, [memxb, msb], [memnb])
        for kt in range(2):
            pt, ptb = self.bank("tr")
            ptv = pt[:].bitcast(BF16)
            for c in range(8):
                self.PE(lambda e, o=ptv[:, c * 128:(c + 1) * 128], i=memn[:, kt, c * 128:(c + 1) * 128]: e.transpose(o, i, self.identb[:]), [memnb, self.identbb], [ptb])
            self.tt(memT[:, :, kt * 128:(kt + 1) * 128], ptv[:, 0:1024].rearrange("p (c k) -> p c k", c=8),
                    self.pcv(l, PC_NM, 8).unsqueeze(2).to_broadcast([128, 8, 128]), ALU.mult, [ptb, self.pcb], [memTb])
        kT = kvo[:, 0:2048].rearrange("p (c k) -> p c k", c=8)
        vv = kvo[:, 2048:4096].rearrange("p (t d) -> p t d", t=2)
        for tile in range(2):
            wk, wkb = self.W.get("w_xkv", l, 0, tile * 512)
            for hh2 in range(2):
                hh = tile * 2 + hh2
                pk = [self.bank("mm") for _ in range(2)]
                pst, pstb = self.bank("st")
                for j in range(2):
                    cc = hh2 * 2 + j
                    for kc in range(8):
                        self.mm(pk[j][0][:, 0:256], wk[:, kc, cc * 128:(cc + 1) * 128] if wk is not None else None, memT[:, kc, :], kc == 0, kc == 7, [wkb, memTb], [pk[j][1]])
                    self.act(sq[j][0], pk[j][0][:, 0:256], AF.Square, [pk[j][1]], [sq[j][1]])
                    self.mm(pst[:, 0:256], self.onesf[:], sq[j][0], j == 0, j == 1, [self.onesfb, sq[j][1]], [pstb])
                self.rsqrt(rs, pst[:, 0:256], 1.0 / 256, [pstb], [rsb])
                for j in range(2):
                    self.stt(kT[:, 2 * hh + j, :], pk[j][0][:, 0:256], self.pcv(l, PC_XK + j), rs, ALU.mult, ALU.mult, [pk[j][1], rsb, self.pcb], [kvob])
        for tile in range(2):
            wv, wvb = self.W.get("w_xkv", l, 0, 1024 + tile * 512)
            for kt in range(2):
                pv, pvb = self.bank("mm")
                for kc in range(8):
                    self.mm(pv[:, 0:512], memT[:, kc, kt * 128:(kt + 1) * 128], wv[:, kc, :] if wv is not None else None, kc == 0, kc == 7, [wvb, memTb], [pvb])
                self.act(vv[:, kt, tile * 512:(tile + 1) * 512], pv[:, 0:512], AF.Identity, [pvb], [kvob])
        self.S.dma(self.kvtrack, lambda e: e.dma_start(out=self.kvs[l], in_=kvo), reads=[kvob], writes=[self.kvs_b[l]])

    def xattn_phase(self, l, ck):
        self.S.tag = 'xattn_phase' + str((l, ck))
        W = self.TC
        self.rmsnorm(l, PC_NX)
        ar = self.arena
        ar.reset()
        kvb, kvbb = ar.alloc(2048, BF16)
        qT, qTb = ar.alloc(4 * W, BF16, "p (c w) -> p c w", c=8)
        oT, oTb = ar.alloc(4 * W, BF16, "p (c w) -> p c w", c=8)
        sq = [ar.alloc(W) for _ in range(2)]
        rs, rsb = ar.alloc(W)
        rd, rdb = ar.alloc(W)
        pts = [[ar.alloc(W // 2, BF16) for _ in range(2)] for _ in range(2)]
        self.S.dma(self.kvltrack, lambda e: e.dma_start(out=kvb, in_=self.kvs[l]), reads=[self.kvs_b[l]], writes=[kvbb])
        kT = kvb[:, 0:2048].rearrange("p (c k) -> p c k", c=8)
        vv = kvb[:, 2048:4096].rearrange("p (t d) -> p t d", t=2)
        for tile in range(2):
            wq, wqb = self.W.get("w_xq", l, 0, tile * 512)
            for hh2 in range(2):
                hh = tile * 2 + hh2
                pq = [self.bank("mm") for _ in range(2)]
                pst, pstb = self.bank("st")
                for j in range(2):
                    cc = hh2 * 2 + j
                    for kc in range(8):
                        self.mm(pq[j][0][:, 0:W], wq[:, kc, cc * 128:(cc + 1) * 128] if wq is not None else None, self.uT[:, kc, :], kc == 0, kc == 7, [wqb, self.uTb], [pq[j][1]])
                    self.act(sq[j][0], pq[j][0][:, 0:W], AF.Square, [pq[j][1]], [sq[j][1]])
                    self.mm(pst[:, 0:W], self.onesf[:], sq[j][0], j == 0, j == 1, [self.onesfb, sq[j][1]], [pstb])
                self.rsqrt(rs, pst[:, 0:W], 1.0 / 256, [pstb], [rsb])
                for j in range(2):
                    self.stt(qT[:, 2 * hh + j, :], pq[j][0][:, 0:W], self.pcv(l, PC_XQ + j), rs, ALU.mult, ALU.mult, [pq[j][1], rsb, self.pcb], [qTb])
        def xscore(hh):
            pp = pts[hh % 2]
            for kt in range(2):
                pS, pSb = self.bank("mm")
                for j in range(2):
                    self.mm(pS[:, 0:W], kT[:, 2 * hh + j, kt * 128:(kt + 1) * 128], qT[:, 2 * hh + j, :], j == 0, j == 1, [kvbb, qTb], [pSb])
                self.act(pp[kt][0], pS[:, 0:W], AF.Exp, [pSb], [pp[kt][1]], scale=1.0 / 16)

        def xpv(hh):
            pp = pts[hh % 2]
            pD, pDb = self.bank("st")
            for kt in range(2):
                self.mm(pD[:, 0:W], self.onesb[:], pp[kt][0], kt == 0, kt == 1, [self.onesbb, pp[kt][1]], [pDb])
            self.DVE(lambda e, pD=pD: e.reciprocal(rd, pD[:, 0:W]), [pDb], [rdb])
            for dc in range(2):
                c = 2 * hh + dc
                pO, pOb = self.bank("mm")
                for kt in range(2):
                    self.mm(pO[:, 0:W], vv[:, kt, c * 128:(c + 1) * 128], pp[kt][0], kt == 0, kt == 1, [kvbb, pp[kt][1]], [pOb])
                self.tt(oT[:, c, :], pO[:, 0:W], rd, ALU.mult, [pOb, rdb], [oTb])

        xscore(0)
        for hh in range(4):
            if hh + 1 < 4:
                xscore(hh + 1)
            xpv(hh)
        self.proj_add(l, "w_xo", oT, oTb)
        self.dump("h2", self.hT[:], self.hTb, ck) if l == 0 else None

    def mlp_phase(self, l, ck):
        self.S.tag = 'mlp_phase' + str((l, ck))
        W = self.TC
        self.rmsnorm(l, PC_NMLP)
        ar = self.arena
        ar.reset()
        actT, _ = ar.alloc(16 * W, BF16, "p (c w) -> p c w", c=32)
        actb = ar.bufs(4)
        rl = [ar.alloc(W) for _ in range(2)]
        for tile in range(8):
            wu, wub = self.W.get("w_mlp_up", l, 0, tile * 512)
            for cc in range(4):
                hc = tile * 4 + cc
                r_, rb_ = rl[hc % 2]
                pu, pub = self.bank("mm")
                for kc in range(8):
                    self.mm(pu[:, 0:W], wu[:, kc, cc * 128:(cc + 1) * 128] if wu is not None else None, self.uT[:, kc, :], kc == 0, kc == 7, [wub, self.uTb], [pub])
                self.act(r_, pu[:, 0:W], AF.Relu, [pub], [rb_])
                self.tt(actT[:, hc, :], r_, r_, ALU.mult, [rb_], [actb[hc // 8]])
        for half in range(2):
            banks = [self.bank("mm") for _ in range(4)]
            for j in range(4):
                wd_, wdb_ = self.W.get("w_mlp_down", l, j * 8, half * 512)
                for cc in range(4):
                    py, pyb = banks[cc]
                    for kk in range(8):
                        self.mm(py[:, 0:W], wd_[:, kk, cc * 128:(cc + 1) * 128] if wd_ is not None else None, actT[:, j * 8 + kk, :], j == 0 and kk == 0, j == 3 and kk == 7, [wdb_, actb[j]], [pyb])
            for cc in range(4):
                oc = half * 4 + cc
                py, pyb = banks[cc]
                self.tt(self.hT[:, oc, :], self.hT[:, oc, :], py[:, 0:W], ALU.add, [pyb, self.hTb[oc]], [self.hTb[oc]])
        self.dump("h3", self.hT[:], self.hTb, ck) if l == 0 else None


HPERM = list(range(16))


def host_consts():
    c = np.zeros((128, 512), np.float32)
    c[:, 0:128] = np.eye(128, dtype=np.float32)
    c[:, 128:256] = np.triu(np.ones((128, 128), np.float32))
    c[:, 256:384] = np.tril(np.ones((128, 128), np.float32), -1) * NEG
    blk = np.zeros((128, 128), np.float32)
    blk[0:64, 0:64] = 1.0
    blk[64:128, 64:128] = 1.0
    c[:, 384:512] = blk
    return c


def host_swab(rel_table):
    qi = np.arange(128)[:, None] + 128
    kj = np.arange(256)[None, :]
    dist = qi - kj
    max_exact = 16
    d = np.maximum(dist, 1).astype(np.float32)
    large = max_exact + (np.log(d / np.float32(max_exact)) / np.float32(math.log(128 / max_exact)) * np.float32(32 - max_exact)).astype(np.int32)
    large = np.minimum(large, 31)
    bucket = np.where(dist < max_exact, np.maximum(dist, 0), large)
    valid = (dist >= 0) & (dist < 128)
    bias = rel_table[bucket]
    bias = np.where(valid[:, :, None], bias, np.float32(NEG)).astype(np.float32)
    out = np.zeros((128, 2, 16, 128), np.float32)
    for side in range(2):
        blkb = bias[:, side * 128:(side + 1) * 128, :]
        out[:, side] = np.transpose(blkb[:, :, HPERM], (1, 2, 0))
    return out.reshape(128, 2 * 16 * 128)


def colv(v):
    return np.ascontiguousarray(np.asarray(v, np.float32).reshape(-1, 128).T)


def host_params(inp, L):
    pcol = np.zeros((128, L, NPC), np.float32)
    prow = np.zeros((128, L, NPR), np.float32)
    for l in range(L):
        pc = pcol[:, l]
        pc[:, PC_NMIX:PC_NMIX + 8] = colv(inp["norm_mix"][l])
        for k in range(3):
            pc[:, PC_GB + k * 8:PC_GB + k * 8 + 8] = colv(inp["gate_bias"][l, k])
        cw = inp["conv_dw_w"][l]
        pc[:, PC_CW:PC_CW + 248] = np.transpose(cw.reshape(31, 8, 128), (2, 1, 0)).reshape(128, 248)
        pc[:, PC_CB:PC_CB + 8] = colv(inp["conv_dw_b"][l])
        pc[:, PC_LG:PC_LG + 8] = colv(inp["conv_ln_g"][l])
        pc[:, PC_LB:PC_LB + 8] = colv(inp["conv_ln_b"][l])
        sw = inp["ssd_conv_w"][l]
        pc[:, PC_SW:PC_SW + 96] = np.transpose(sw.reshape(4, 24, 128), (2, 1, 0)).reshape(128, 96)
        pc[:, PC_SB:PC_SB + 24] = colv(inp["ssd_conv_b"][l])
        pc[:, PC_NG:PC_NG + 16] = colv(inp["ssd_norm_g"][l])
        pc[:, PC_QN] = np.tile(inp["attn_q_norm"][l], 2)
        pc[:, PC_KN] = np.tile(inp["attn_k_norm"][l], 2)
        pc[:, PC_NX:PC_NX + 8] = colv(inp["norm_xattn"][l])
        pc[:, PC_NM:PC_NM + 8] = colv(inp["norm_mem"][l])
        pc[:, PC_XQ:PC_XQ + 2] = colv(inp["xattn_q_norm"][l])
        pc[:, PC_XK:PC_XK + 2] = colv(inp["xattn_k_norm"][l])
        pc[:, PC_NMLP:PC_NMLP + 8] = colv(inp["norm_mlp"][l])
        pr = prow[:, l]
        pr[:, PR_DTB:PR_DTB + 32] = inp["ssd_dt_bias"][l][None, :]
        pr[:, PR_ALOG:PR_ALOG + 32] = inp["ssd_A_log"][l][None, :]
        pr[:, PR_D:PR_D + 32] = inp["ssd_D"][l][None, :]
        pr[:, PR_SINK:PR_SINK + 16] = inp["attn_sinks"][l][HPERM][None, :]
    return pcol.reshape(128, L * NPC), prow.reshape(128, L * NPR)


def make_in_maps(inp, T, L, batches):
    consts = host_consts()
    swab = host_swab(np.asarray(inp["rel_table"], np.float32))
    pcol, prow = host_params(inp, L)
    shared = {"consts": consts, "swab": swab, "pcol": pcol, "prow": prow}
    for n in WSHAPES:
        shared[n] = np.ascontiguousarray(np.asarray(inp[n], np.float32)[:L])
    maps = []
    for b in batches:
        m = dict(shared)
        m["x"] = np.ascontiguousarray(np.asarray(inp["x"], np.float32)[b, :T])
        m["mem"] = np.ascontiguousarray(np.asarray(inp["mem"], np.float32)[b])
        maps.append(m)
    return maps


_NC_CACHE = {}


def kernel(**inputs):
    T, L, TC = 4096, 4, 256
    key = (T, L, TC)
    if key not in _NC_CACHE:
        _NC_CACHE[key] = KB(T, L, TC).build()
    nc = _NC_CACHE[key]
    maps = make_in_maps(inputs, T, L, list(range(8)))
    res = run_bass_kernel_spmd(nc, maps, core_ids=list(range(8)))
    return np.stack([np.asarray(r["y"], np.float32) for r in res.results], axis=0)
```

```python
import math
from contextlib import ExitStack

import numpy as np
import concourse.bass as bass
import concourse.mybir as mybir
from concourse.bass_utils import run_bass_kernel_spmd

F32 = mybir.dt.float32
BF16 = mybir.dt.bfloat16
ALU = mybir.AluOpType
AF = mybir.ActivationFunctionType

D = 1024
EPS = 1e-6
OFF_CONV, OFF_Z, OFF_XBC, OFF_DT, OFF_Q, OFF_K, OFF_V, OFF_GATE, IN_COLS = 0, 2048, 4096, 7168, 7200, 8224, 8480, 8736, 11808
NEG = -30000.0

WSHAPES = {
    "w_in": (1024, IN_COLS), "w_conv_out": (1024, 1024), "w_ssd_out": (2048, 1024), "w_attn_out": (1024, 1024),
    "w_mix_out": (1024, 1024), "w_xq": (1024, 1024), "w_xkv": (1024, 2048), "w_xo": (1024, 1024),
    "w_mlp_up": (1024, 4096), "w_mlp_down": (4096, 1024),
}
WORDER = ["w_in", "w_conv_out", "w_ssd_out", "w_attn_out", "w_mix_out", "w_xkv", "w_xq", "w_xo", "w_mlp_up", "w_mlp_down"]

PC_NMIX, PC_GB, PC_CW, PC_CB, PC_LG, PC_LB, PC_SW, PC_SB, PC_NG, PC_QN, PC_KN, PC_NX, PC_NM, PC_XQ, PC_XK, PC_NMLP, NPC = (
    0, 8, 32, 280, 288, 296, 304, 400, 424, 440, 441, 442, 450, 458, 460, 462, 470)
PR_DTB, PR_ALOG, PR_D, PR_SINK, NPR = 0, 32, 64, 96, 112

EPOCH = 6000


class Track:
    def __init__(self, name, is_dma=False):
        self.name = name
        self.is_dma = is_dma
        self.n = 0
        self.nsig = 0
        self.sems = []


class Buf:
    __slots__ = ("name", "w", "r", "excl")

    def __init__(self, name="b", excl=False):
        self.name = name
        self.w = None
        self.r = {}
        self.excl = excl


class Op:
    __slots__ = ("eng", "track", "seq", "eseq", "fn", "waits", "signal", "sem", "val", "tag")


class Sched:
    def __init__(self, nc, stack, dry=False):
        self.nc = nc
        self.stack = stack
        self.dry = dry
        self.ops = []
        self.engs = {"pe": nc.tensor, "act": nc.scalar, "dve": nc.vector, "pool": nc.gpsimd, "sp": nc.sync}
        self.etrack = {k: Track(k) for k in self.engs}
        self.ecount = {k: 0 for k in self.engs}
        self.seen = {k: {} for k in self.engs}
        self.nsem = 0
        self.epdone = set()

    def new_sem(self, name):
        self.nsem += 1
        return self.stack.enter_context(self.nc.semaphore(f"{name}_{self.nsem}"))

    def _record(self, eng, track, fn, reads, writes):
        if self.dry:
            return None
        op = Op()
        op.eng = eng
        op.track = track
        track.n += 1
        op.seq = track.n
        self.ecount[eng] += 1
        op.eseq = self.ecount[eng]
        op.fn = fn
        op.tag = getattr(self, 'tag', '')
        op.signal = track.is_dma
        op.sem = None
        op.val = 0
        deps = []
        xreads = [b for b in reads if b.excl and b not in writes]
        for b in reads:
            if b.w is not None:
                deps.append((b.w, False))
            if b.excl:
                for o in b.r.values():
                    deps.append((o, True))
        for b in writes:
            if b.w is not None:
                deps.append((b.w, False))
            for o in b.r.values():
                deps.append((o, False))
        waits = []
        seen = self.seen[eng]
        own = self.etrack[eng]
        for d, soft in deps:
            if d.track is own:
                if soft or eng == "pe" or eng == "sp":
                    continue
                if d.eseq < op.eseq - 2:
                    continue
            if d.track is track and track.is_dma:
                continue
            if seen.get(d.track, 0) >= d.seq:
                continue
            seen[d.track] = d.seq
            d.signal = True
            waits.append(d)
        op.waits = waits
        for b in reads:
            b.r[track] = op
        for b in writes:
            b.w = op
            b.r = {}
        self.ops.append(op)
        return op

    def op(self, eng, fn, reads=(), writes=()):
        return self._record(eng, self.etrack[eng], fn, reads, writes)

    def dma(self, track, fn, reads=(), writes=(), eng="sp"):
        return self._record(eng, track, fn, reads, writes)

    def wait_all(self, eng, bufs):
        return self._record(eng, self.etrack[eng], None, bufs, ())

    def emit(self):
        import os
        maxops = int(os.environ.get("MAXOPS", "0"))
        ops = self.ops[:maxops] if maxops else self.ops
        for op in ops:
            e = self.engs[op.eng]
            for d in op.waits:
                t = d.track
                if t.is_dma:
                    epn = EPOCH // 16
                    ep = t.sems.index(d.sem)
                    if ep > 0 and (op.eng, id(t), ep - 1) not in self.epdone:
                        self.epdone.add((op.eng, id(t), ep - 1))
                        e.wait_ge(t.sems[ep - 1], epn * 16)
                e.wait_ge(d.sem, d.val)
            if op.fn is None:
                continue
            inst = op.fn(e)
            if op.signal:
                t = op.track
                k = t.nsig
                t.nsig += 1
                inc = 16 if t.is_dma else 1
                epn = EPOCH // inc
                ep = k // epn
                if ep >= len(t.sems):
                    t.sems.append(self.new_sem(t.name))
                op.sem = t.sems[ep]
                op.val = (k % epn + 1) * inc
                inst.then_inc(op.sem, inc)


class Arena:
    def __init__(self, tensor, nwords):
        self.t = tensor
        self.n = nwords
        self.pos = 0
        self.live = []
        self.pending = {}

    def reset(self):
        for b in self.live:
            if b.w is not None:
                o = self.pending.get(b.w.track)
                if o is None or o.seq < b.w.seq:
                    self.pending[b.w.track] = b.w
            for tr, op in b.r.items():
                o = self.pending.get(tr)
                if o is None or o.seq < op.seq:
                    self.pending[tr] = op
        self.live = []
        self.pos = 0
        self.rawstg = None

    def bufs(self, n):
        out = []
        for _ in range(n):
            b = Buf("arb")
            b.r = dict(self.pending)
            self.live.append(b)
            out.append(b)
        return out

    def alloc(self, words, dtype=F32, pat=None, **kw):
        words = (words + 7) // 8 * 8
        assert self.pos + words <= self.n, f"arena overflow {self.pos}+{words}>{self.n}"
        ap = self.t[:, self.pos:self.pos + words]
        self.pos += words
        if dtype == BF16:
            ap = ap.bitcast(BF16)
        if pat is not None:
            ap = ap.rearrange(pat, **kw)
        b = Buf("ar")
        b.r = dict(self.pending)
        self.live.append(b)
        return ap, b


class WStream:
    NSLOT = 4

    def __init__(self, K):
        self.K = K
        self.plan = []
        self.run = False
        self.i = 0
        self.issued = 0
        nc = self.K.nc
        self.slots = [nc.alloc_sbuf_tensor(f"wslot{i}", [128, 8, 512], BF16) for i in range(self.NSLOT)]
        self.bufs = [Buf(f"wslot{i}") for i in range(self.NSLOT)]
        self.tracks = [Track(f"wsl{i}", True) for i in range(self.NSLOT)]

    def start(self):
        self.run = True
        self.i = 0
        self.issued = 0

    def _issue(self, j):
        name, l, kc0, n0, nk, ncols, kind = self.plan[j]
        s = j % self.NSLOT
        src = self.K.wb[(l, name)][:, kc0:kc0 + nk, n0:n0 + ncols]
        dst = self.slots[s][:, 0:nk, 0:ncols]
        if kind == "qperm":
            for e_ in range(2):
                for i_ in range(4):
                    sp = src[:, :, (e_ * 4 + i_) * 64:(e_ * 4 + i_ + 1) * 64]
                    dp = dst[:, :, (i_ * 2 + e_) * 64:(i_ * 2 + e_ + 1) * 64]
                    self.K.S.dma(self.tracks[s], lambda e, dp=dp, sp=sp: e.dma_start(out=dp, in_=sp),
                                 reads=[self.K.wbuf[(l, name)]], writes=[self.bufs[s]])
            return
        self.K.S.dma(self.tracks[s], lambda e, dst=dst, src=src: e.dma_start(out=dst, in_=src),
                     reads=[self.K.wbuf[(l, name)]], writes=[self.bufs[s]])

    def get(self, name, l, kc0, n0, nk=8, ncols=512, kind=None, hold_prev=False):
        key = (name, l, kc0, n0, nk, ncols, kind)
        if not self.run:
            self.plan.append(key)
            return self.slots[0], Buf("dummy")
        assert self.plan[self.i] == key, (self.plan[self.i], key)
        while self.issued < min(len(self.plan), self.i + self.NSLOT - (1 if hold_prev else 0)):
            self._issue(self.issued)
            self.issued += 1
        s = self.i % self.NSLOT
        self.i += 1
        return self.slots[s], self.bufs[s]


class KB:
    def __init__(self, T, L, TC=256, dbg=()):
        self.T, self.L, self.TC = T, L, TC
        self.NT = TC // 128
        self.NCH = T // TC
        self.dbg = set(dbg)

    def PE(self, fn, reads, writes):
        self.S.op("pe", fn, reads, writes)

    def ACT(self, fn, reads, writes):
        self.S.op("act", fn, reads, writes)

    def DVE(self, fn, reads, writes):
        self.S.op("dve", fn, reads, writes)

    def mm(self, out, lhsT, rhs, start, stop, reads, writes):
        self.S.op("pe", lambda e: e.matmul(out, lhsT, rhs, start=start, stop=stop), reads, writes)

    def act(self, out, in_, func, reads, writes, **kw):
        self.S.op("act", lambda e: e.activation(out, in_, func, **kw), reads, writes)

    def tt(self, out, in0, in1, op, reads, writes):
        self.S.op("dve", lambda e: e.tensor_tensor(out=out, in0=in0, in1=in1, op=op), reads, writes)

    def stt(self, out, in0, scalar, in1, op0, op1, reads, writes):
        self.S.op("dve", lambda e: e.scalar_tensor_tensor(out=out, in0=in0, scalar=scalar, in1=in1, op0=op0, op1=op1), reads, writes)

    def ts(self, out, in0, s1, s2, op0, op1, reads, writes):
        if s2 is None:
            self.S.op("dve", lambda e: e.tensor_scalar(out=out, in0=in0, scalar1=s1, scalar2=None, op0=op0), reads, writes)
        else:
            self.S.op("dve", lambda e: e.tensor_scalar(out=out, in0=in0, scalar1=s1, scalar2=s2, op0=op0, op1=op1), reads, writes)

    def bank(self, grp):
        lst = self.bgrp[grp]
        i = self.bpos[grp]
        self.bpos[grp] = (i + 1) % len(lst)
        j = lst[i]
        return self.ps[j], self.psb[j]

    def rsqrt(self, out, in_, scale, reads, writes):
        self.act(out, in_, AF.Ln, reads, writes, scale=scale, bias=EPS)
        self.act(out, out, AF.Exp, writes, writes, scale=-0.5)

    def pcv(self, l, off, n=1):
        return self.pc[:, l * NPC + off: l * NPC + off + n]

    def prv(self, l, off, n):
        return self.pr[:, l * NPR + off: l * NPR + off + n]

    def build(self):
        nc = bass.Bass("TRN2", target_bir_lowering=False)
        self.nc = nc
        T, L, TC = self.T, self.L, self.TC

        def din(n, s, d=F32):
            return nc.dram_tensor(n, s, d, kind="ExternalInput").ap()

        self.x = din("x", [T, D])
        self.mem = din("mem", [256, D])
        self.consts_d = din("consts", [128, 512])
        self.swab_d = din("swab", [128, 2 * 16 * 128])
        self.pcol_d = din("pcol", [128, L * NPC])
        self.prow_d = din("prow", [128, L * NPR])
        self.wd = {n: din(n, [L, k, c]) for n, (k, c) in WSHAPES.items()}
        self.y = nc.dram_tensor("y", [T, D], F32, kind="ExternalOutput").ap()
        self.wb = {}
        self.wbuf = {}
        for l in range(L):
            for n, (k, c) in WSHAPES.items():
                self.wb[(l, n)] = nc.dram_tensor(f"wb_{n}_{l}", [128, k // 128, c], BF16, kind="Internal").ap()
                self.wbuf[(l, n)] = Buf(f"wb_{n}_{l}")
        self.kvs = [nc.dram_tensor(f"kvs_{l}", [128, 4096], BF16, kind="Internal").ap() for l in range(L)]
        self.kvs_b = [Buf(f"kvs{l}") for l in range(L)]
        self.dbg_d = {}
        for name in self.dbg:
            if name == "raw":
                continue
            self.dbg_d[name] = nc.dram_tensor("dbg_" + name, [128, 16 if name in ("ybin",) else 8, T], F32, kind="ExternalOutput").ap()
        self.dbg.discard("raw") if False else None
        self.dbg_b = {n: Buf("dbg" + n) for n in self.dbg}
        self.dbg_t = {n: Track("dbg" + n, True) for n in self.dbg}

        with ExitStack() as st:
            self.W = WStream(self)
            self.S = Sched(nc, st, dry=True)
            self.alloc_all(dry=True)
            self.program()
            self.S = Sched(nc, st, dry=False)
            self.W.start()
            self.program()
            self.S.emit()
        return nc

    def alloc_all(self, dry):
        nc = self.nc
        L, W = self.L, self.TC
        A = nc.alloc_sbuf_tensor

        def R(name, shape, dt=F32):
            return A(name, shape, dt), Buf(name)

        self.hT, _ = R("hT", [128, 8, W])
        self.hTb = [Buf(f"hT{c}") for c in range(8)]
        self.uT, self.uTb = R("uT", [128, 8, W], BF16)
        self.mrg, _ = R("mrg", [128, 8, W])
        self.mrgb = [Buf(f"mrg{c}") for c in range(8)]
        self.state, _ = R("state", [128, L, 2048])
        self.stateb = [[Buf(f"st{l}_{g}") for g in range(4)] for l in range(L)]
        self.bias, self.biasb = R("swabias", [128, 2, 16, 128])
        self.pc, self.pcb = R("pc", [128, L * NPC])
        self.pr, self.prb = R("pr", [128, L * NPR])
        self.arow, self.arowb = R("arow", [128, L * 32])
        self.esink, self.esinkb = R("esink", [128, L * 16])
        self.ctail, _ = R("ctail", [128, L, 8, 30])
        self.ctailb = [[Buf("ct") for c in range(8)] for l in range(L)]
        self.xtail, _ = R("xtail", [128, L, 24, 3])
        self.xtailb = [[Buf("xt") for c in range(24)] for l in range(L)]
        self.kprev, _ = R("kprev", [128, L, 4, 128], BF16)
        self.kprevb = [Buf("kp") for l in range(L)]
        self.vprev, _ = R("vprev", [128, L, 512], BF16)
        self.vprevb = [Buf("vp") for l in range(L)]
        self.cst, self.cstb = R("cst", [128, 512])
        self.ident = self.cst[:, 0:128]
        self.triu = self.cst[:, 128:256]
        self.negm = self.cst[:, 256:384]
        self.blk64 = self.cst[:, 384:512]
        self.identb, self.identbb = R("identb", [128, 128], BF16)
        self.negmb, self.negmbb = R("negmb", [128, 128], BF16)
        self.onesb, self.onesbb = R("onesb", [128, 128], BF16)
        self.onesf, self.onesfb = R("onesf", [128, 128])
        self.ps = [nc.alloc_psum_tensor(f"ps{i}", [128, 512], F32) for i in range(8)]
        self.psb = [Buf(f"ps{i}", excl=True) for i in range(8)]
        self.bgrp = {"mm": [0, 1, 2, 3], "st": [4, 5], "tr": [6, 7]}
        self.bpos = {"mm": 0, "st": 0, "tr": 0}
        nw = (nc.sbuf_bytes_remaining - 2048) // 4
        nw = nw // 8 * 8
        self.ar_t = A("arena", [128, nw], F32)
        self.arena = Arena(self.ar_t, nw)

    def program(self):
        if not self.S.dry:
            self.reset_bufs()
        self.init_phase()
        for ck in range(self.NCH):
            self.load_x(ck)
            import os
            PH = os.environ.get("PH", "ncswmxp")
            for l in range(self.L):
                if "n" in PH:
                    self.rmsnorm(l, PC_NMIX)
                self.dump("u", self.uT[:], self.uTb, ck) if l == 0 else None
                if "c" in PH:
                    self.conv_phase(l, ck)
                if "s" in PH:
                    self.ssd_phase(l, ck)
                if "w" in PH:
                    self.swa_phase(l, ck)
                if "m" in PH:
                    self.mix_phase(l, ck)
                if "x" in PH:
                    if ck == 0:
                        self.kv_precompute(l)
                    self.xattn_phase(l, ck)
                if "p" in PH:
                    self.mlp_phase(l, ck)
            self.store_y(ck)
        self.S.wait_all("sp", [self.yb])

    def reset_bufs(self):
        self.arena = Arena(self.ar_t, self.arena.n)
        self.bpos = {"mm": 0, "st": 0, "tr": 0}

    def dump(self, name, ap, bufs, ck, nchunks=8):
        if name not in self.dbg:
            return
        W = self.TC
        stg, sb = self.arena_dbg(nchunks)
        bl = bufs if isinstance(bufs, list) else [bufs]
        self.ACT(lambda e: e.activation(stg, ap, AF.Identity), bl, [sb])
        dst = self.dbg_d[name][:, :, ck * W:(ck + 1) * W]
        self.S.dma(self.dbg_t[name], lambda e: e.dma_start(out=dst, in_=stg), reads=[sb], writes=[self.dbg_b[name]])

    def rawdump(self, name, ap, bufs, n):
        if "raw" not in self.dbg:
            return
        if not hasattr(self, "raw_d"):
            self.raw_d = {}
        if name not in self.raw_d:
            self.raw_d[name] = self.nc.dram_tensor("raw_" + name, [128, n], F32, kind="ExternalOutput").ap()
        if getattr(self.arena, "rawstg", None) is None:
            self.arena.rawstg = self.arena.alloc(2048)
        stg, sb = self.arena.rawstg
        stg = stg[:, 0:n]
        bl = bufs if isinstance(bufs, list) else [bufs]
        self.ACT(lambda e: e.activation(stg, ap, AF.Identity), bl, [sb])
        dst = self.raw_d[name]
        self.S.dma(Track("raw" + name, True), lambda e: e.dma_start(out=dst, in_=stg), reads=[sb], writes=[Buf("rawo")])

    def arena_dbg(self, nchunks):
        W = self.TC
        return self.arena.alloc(nchunks * W, F32, "p (c w) -> p c w", c=nchunks)

    def init_phase(self):
        S, nc, L = self.S, self.nc, self.L
        self.yb = Buf("y")
        self.ytrack = Track("ytr", True)
        self.xtrack = Track("xtr", True)
        self.kvtrack = Track("kvtr", True)
        self.kvltrack = Track("kvltr", True)
        tr = [Track(f"init{i}", True) for i in range(4)]
        S.dma(tr[0], lambda e: e.dma_start(out=self.cst[:], in_=self.consts_d), writes=[self.cstb])
        S.dma(tr[1], lambda e: e.dma_start(out=self.bias[:].rearrange("p a h q -> p (a h q)"), in_=self.swab_d), writes=[self.biasb])
        S.dma(tr[2], lambda e: e.dma_start(out=self.pc[:], in_=self.pcol_d), writes=[self.pcb])
        S.dma(tr[3], lambda e: e.dma_start(out=self.pr[:], in_=self.prow_d), writes=[self.prb])
        self.cvtrack = {}
        for l in range(L):
            for n in WORDER:
                k, c = WSHAPES[n]
                t = Track(f"cv_{n}_{l}", True)
                for kc in range(k // 128):
                    for n0 in range(0, c, 4096):
                        n1 = min(c, n0 + 4096)
                        dst = self.wb[(l, n)][:, kc, n0:n1]
                        src = self.wd[n][l, kc * 128:(kc + 1) * 128, n0:n1]
                        S.dma(t, lambda e, dst=dst, src=src: e.dma_start(out=dst, in_=src), writes=[self.wbuf[(l, n)]], eng="pool")
        self.DVE(lambda e: e.memset(self.onesb[:], 1.0), [], [self.onesbb])
        self.DVE(lambda e: e.memset(self.onesf[:], 1.0), [], [self.onesfb])
        self.DVE(lambda e: e.tensor_copy(self.identb[:], self.ident), [self.cstb], [self.identbb])
        self.DVE(lambda e: e.tensor_copy(self.negmb[:], self.negm), [self.cstb], [self.negmbb])
        allst = [b for row in self.stateb for b in row]
        self.DVE(lambda e: e.memset(self.state[:].rearrange("p l n -> p (l n)"), 0.0), [], allst)
        self.DVE(lambda e: e.memset(self.ctail[:].rearrange("p l c j -> p (l c j)"), 0.0), [], [b for row in self.ctailb for b in row])
        self.DVE(lambda e: e.memset(self.xtail[:].rearrange("p l c j -> p (l c j)"), 0.0), [], [b for row in self.xtailb for b in row])
        self.DVE(lambda e: e.memset(self.kprev[:].rearrange("p l g k -> p (l g k)"), 0.0), [], self.kprevb)
        self.DVE(lambda e: e.memset(self.vprev[:].rearrange("p l n -> p (l n)"), 0.0), [], self.vprevb)
        for l in range(L):
            a = self.arow[:, l * 32:(l + 1) * 32]
            self.act(a, self.prv(l, PR_ALOG, 32), AF.Exp, [self.prb], [self.arowb])
            self.ts(a, a, -1.0, None, ALU.mult, None, [self.arowb], [self.arowb])
            self.act(self.esink[:, l * 16:(l + 1) * 16], self.prv(l, PR_SINK, 16), AF.Exp, [self.prb], [self.esinkb])

    def load_x(self, ck):
        self.S.tag = 'load_x' + str((ck))
        W, NT = self.TC, self.NT
        ar = self.arena
        ar.reset()
        xin, xb = ar.alloc(NT * 1024, F32, "p (t d) -> p t d", t=NT)
        src = self.x[ck * W:(ck + 1) * W, :].rearrange("(t p) d -> p t d", p=128)
        self.S.dma(self.xtrack, lambda e: e.dma_start(out=xin, in_=src), writes=[xb])
        cpb = 512 // W
        for c0 in range(0, 8, cpb):
            pt, pb = self.bank("tr")
            for cc in range(cpb):
                for t in range(NT):
                    o = pt[:, cc * W + t * 128: cc * W + (t + 1) * 128]
                    i = xin[:, t, (c0 + cc) * 128:(c0 + cc + 1) * 128]
                    self.PE(lambda e, o=o, i=i: e.transpose(o, i, self.ident), [xb, self.cstb], [pb])
            self.DVE(lambda e, c0=c0, pt=pt: e.tensor_copy(self.hT[:, c0:c0 + cpb, :], pt[:, 0:cpb * W].rearrange("p (c w) -> p c w", c=cpb)),
                     [pb], self.hTb[c0:c0 + cpb])

    def store_y(self, ck):
        self.S.tag = 'store_y' + str((ck))
        W, NT = self.TC, self.NT
        ar = self.arena
        ar.reset()
        yo, yob = ar.alloc(NT * 1024, F32, "p (t d) -> p t d", t=NT)
        for t in range(NT):
            for c0 in range(0, 8, 4):
                pt, pb = self.bank("tr")
                for cc in range(4):
                    o = pt[:, cc * 128:(cc + 1) * 128]
                    i = self.hT[:, c0 + cc, t * 128:(t + 1) * 128]
                    self.PE(lambda e, o=o, i=i: e.transpose(o, i, self.ident), [self.hTb[c0 + cc], self.cstb], [pb])
                self.ACT(lambda e, t=t, c0=c0, pt=pt: e.activation(yo[:, t, c0 * 128:(c0 + 4) * 128], pt[:, 0:512], AF.Identity), [pb], [yob])
        dst = self.y[ck * W:(ck + 1) * W, :].rearrange("(t p) d -> p t d", p=128)
        self.S.dma(self.ytrack, lambda e: e.dma_start(out=dst, in_=yo), reads=[yob], writes=[self.yb])

    def rmsnorm(self, l, off):
        self.S.tag = 'rmsnorm' + str((l, off))
        W = self.TC
        ar = self.arena
        ar.reset()
        sq = [ar.alloc(W) for _ in range(2)]
        rs, rsb = ar.alloc(W)
        pst, pstb = self.bank("st")
        for c in range(8):
            s, sb = sq[c % 2]
            self.act(s, self.hT[:, c, :], AF.Square, [self.hTb[c]], [sb])
            self.mm(pst[:, 0:W], self.onesf[:], s, c == 0, c == 7, [sb, self.onesfb], [pstb])
        self.rsqrt(rs, pst[:, 0:W], 1.0 / D, [pstb], [rsb])
        for c in range(8):
            self.stt(self.uT[:, c, :], self.hT[:, c, :], self.pcv(l, off + c), rs, ALU.mult, ALU.mult,
                     [self.hTb[c], rsb, self.pcb], [self.uTb])

    def branch_out(self, l, k, srcT, srcb, wname, KC, scratch=None):
        W = self.TC
        ar = self.arena
        if scratch is None:
            gs = [ar.alloc(W) for _ in range(4)]
            tmp = [ar.alloc(W) for _ in range(2)]
        else:
            gs, tmp = scratch[0:4], scratch[4:6]
        for half in range(2):
            wg, wgb = self.W.get("w_in", l, 0, OFF_GATE + k * 1024 + half * 512)
            for cc in range(4):
                oc = half * 4 + cc
                pg, pgb = self.bank("st")
                for kc in range(8):
                    self.mm(pg[:, 0:W], wg[:, kc, cc * 128:(cc + 1) * 128] if wg is not None else None, self.uT[:, kc, :], kc == 0, kc == 7, [wgb, self.uTb], [pgb])
                self.act(gs[cc][0], pg[:, 0:W], AF.Sigmoid, [pgb, self.pcb], [gs[cc][1]], bias=self.pcv(l, PC_GB + k * 8 + oc))
            banks = [self.bank("mm") for _ in range(4)]
            for j in range(KC // 8):
                wt, wtb = self.W.get(wname, l, j * 8, half * 512)
                for cc in range(4):
                    py, pyb = banks[cc]
                    for kk in range(8):
                        kc = j * 8 + kk
                        self.mm(py[:, 0:W], wt[:, kk, cc * 128:(cc + 1) * 128] if wt is not None else None, srcT[:, kc, :], kc == 0, kc == KC - 1, [wtb, srcb], [pyb])
            for cc in range(4):
                oc = half * 4 + cc
                py, pyb = banks[cc]
                if k == 0:
                    self.tt(self.mrg[:, oc, :], py[:, 0:W], gs[cc][0], ALU.mult, [pyb, gs[cc][1]], [self.mrgb[oc]])
                else:
                    t_, tb_ = tmp[cc % 2]
                    self.tt(t_, py[:, 0:W], gs[cc][0], ALU.mult, [pyb, gs[cc][1]], [tb_])
                    self.tt(self.mrg[:, oc, :], self.mrg[:, oc, :], t_, ALU.add, [tb_, self.mrgb[oc]], [self.mrgb[oc]])

    def conv_phase(self, l, ck):
        self.S.tag = 'conv_phase' + str((l, ck))
        W = self.TC
        ar = self.arena
        ar.reset()
        convT, _ = ar.alloc(8 * W, F32, "p (c w) -> p c w", c=8)
        convb = ar.bufs(8)
        actT, actb = ar.alloc(4 * W, BF16, "p (c w) -> p c w", c=8)
        glu = [ar.alloc(W + 32) for _ in range(4)]
        sig = [ar.alloc(W) for _ in range(2)]
        sq = [ar.alloc(W) for _ in range(2)]
        mean, meanb = ar.alloc(W)
        m2, m2b = ar.alloc(W)
        rstd, rstdb = ar.alloc(W)
        nmr, nmrb = ar.alloc(W)
        tmpn = [ar.alloc(W) for _ in range(2)]
        ps1, ps1b = self.bank("st")
        ps2, ps2b = self.bank("st")
        for half in range(2):
            wa, wab = self.W.get("w_in", l, 0, OFF_CONV + half * 512)
            wg, wgb = self.W.get("w_in", l, 0, OFF_CONV + 1024 + half * 512, hold_prev=True)
            for cc in range(4):
                c = half * 4 + cc
                g_, gb_ = glu[cc]
                s_, sb_ = sig[c % 2]
                pa, pab = self.bank("mm")
                pg, pgb = self.bank("mm")
                for kc in range(8):
                    self.mm(pa[:, 0:W], wa[:, kc, cc * 128:(cc + 1) * 128], self.uT[:, kc, :], kc == 0, kc == 7, [wab, self.uTb], [pab])
                for kc in range(8):
                    self.mm(pg[:, 0:W], wg[:, kc, cc * 128:(cc + 1) * 128], self.uT[:, kc, :], kc == 0, kc == 7, [wgb, self.uTb], [pgb])
                self.act(s_, pg[:, 0:W], AF.Sigmoid, [pgb], [sb_])
                self.act(g_[:, 0:30], self.ctail[:, l, c, :], AF.Identity, [self.ctailb[l][c]], [gb_])
                self.tt(g_[:, 30:30 + W], pa[:, 0:W], s_, ALU.mult, [pab, sb_], [gb_])
            for j in range(31):
                for cc in range(4):
                    c = half * 4 + cc
                    g_, gb_ = glu[cc]
                    cw = PC_CW + c * 31
                    if j == 0:
                        self.ts(convT[:, c, :], g_[:, 0:W], self.pcv(l, cw), self.pcv(l, PC_CB + c), ALU.mult, ALU.add, [gb_, self.pcb], [convb[c]])
                    else:
                        self.stt(convT[:, c, :], g_[:, j:j + W], self.pcv(l, cw + j), convT[:, c, :], ALU.mult, ALU.add, [gb_, self.pcb, convb[c]], [convb[c]])
            for cc in range(4):
                c = half * 4 + cc
                g_, gb_ = glu[cc]
                q_, qb_ = sq[c % 2]
                self.act(self.ctail[:, l, c, :], g_[:, W:W + 30], AF.Identity, [gb_], [self.ctailb[l][c]])
                self.act(q_, convT[:, c, :], AF.Square, [convb[c]], [qb_])
                self.mm(ps1[:, 0:W], self.onesf[:], convT[:, c, :], c == 0, c == 7, [convb[c], self.onesfb], [ps1b])
                self.mm(ps2[:, 0:W], self.onesf[:], q_, c == 0, c == 7, [qb_, self.onesfb], [ps2b])
        if l == 0 and ck == 0:
            self.rawdump("glu7", glu[3][0][:, 0:W + 32], glu[3][1], W + 32)
            self.rawdump("sig7", sig[1][0], sig[1][1], W)
            self.rawdump("conv7", convT[:, 7, :], convb[7], W)
            self.rawdump("conv0", convT[:, 0, :], convb[0], W)
        self.act(mean, ps1[:, 0:W], AF.Identity, [ps1b], [meanb], scale=1.0 / D)
        self.tt(m2, mean, mean, ALU.mult, [meanb], [m2b])
        self.stt(rstd, ps2[:, 0:W], 1.0 / D, m2, ALU.mult, ALU.subtract, [ps2b, m2b], [rstdb])
        self.rsqrt(rstd, rstd, 1.0, [rstdb], [rstdb])
        self.stt(nmr, mean, -1.0, rstd, ALU.mult, ALU.mult, [meanb, rstdb], [nmrb])
        for c in range(8):
            t_, tb_ = tmpn[c % 2]
            self.tt(t_, convT[:, c, :], rstd, ALU.mult, [convb[c], rstdb], [tb_])
            self.tt(t_, t_, nmr, ALU.add, [tb_, nmrb], [tb_])
            self.act(actT[:, c, :], t_, AF.Silu, [tb_, self.pcb], [actb], scale=self.pcv(l, PC_LG + c), bias=self.pcv(l, PC_LB + c))
        if l == 0 and ck == 0:
            self.rawdump("mean", mean, meanb, W)
            self.rawdump("rstd", rstd, rstdb, W)
        self.dump("cact", actT, actb, ck) if l == 0 else None
        self.branch_out(l, 0, actT, actb, "w_conv_out", 8)
        self.dump("m0", self.mrg[:], self.mrgb, ck) if l == 0 else None

    def ssd_phase(self, l, ck):
        self.S.tag = 'ssd_phase' + str((l, ck))
        W, NT = self.TC, self.NT
        ar = self.arena
        ar.reset()
        xtok, xtokb = ar.alloc(NT * 1024, BF16, "p (t n) -> p t n", t=NT)
        zs, zsb = ar.alloc(NT * 1024, BF16, "p (t n) -> p t n", t=NT)
        yT, yTb = ar.alloc(8 * W, BF16, "p (c w) -> p c w", c=16)
        bcT, _ = ar.alloc(4 * W, BF16, "p (c w) -> p c w", c=8)
        bcb = ar.bufs(8)
        btok, btokb = ar.alloc(NT * 256, BF16, "p (t n) -> p t n", t=NT)
        xdt, xdtb = ar.alloc(1024, BF16)
        xdte, xdteb = ar.alloc(1024, BF16)
        yw, _ = ar.alloc(2048)
        ywb = ar.bufs(4)
        ytok, ytokb = ar.alloc(1024, BF16)
        stbf, stbfb = ar.alloc(1024, BF16)
        xw = [ar.alloc(W + 8) for _ in range(3)]
        acc = [ar.alloc(W) for _ in range(3)]
        xc = [ar.alloc(W // 2, BF16) for _ in range(3)]
        lt = [ar.alloc(512) for _ in range(2)]
        sc = [ar.alloc(256, BF16, "p (j q) -> p j q", j=4) for _ in range(2)]
        cbm, cbmb = ar.alloc(512, F32, "p (g q) -> p g q", g=4)
        t1 = [ar.alloc(512) for _ in range(2)]
        junk, junkb = ytok.bitcast(F32)[:, 0:512], ytokb
        dtall, dtallb = ar.alloc(NT * 32, F32, "p (t h) -> p t h", t=NT)
        daall, daallb = ar.alloc(NT * 32, F32, "p (t h) -> p t h", t=NT)
        cs, csb = ar.alloc(32)
        dfs, dfsb = ar.alloc(32)
        cd, cdb = ar.alloc(32)
        dte, dteb = ar.alloc(32)
        ss, ssb = ar.alloc(8)
        rs4, rs4b = ar.alloc(8)

        for zt in range(4):
            wz, wzb = self.W.get("w_in", l, 0, OFF_Z + zt * 512)
            for t in range(NT):
                pz, pzb = self.bank("mm")
                for kc in range(8):
                    self.mm(pz[:, 0:512], self.uT[:, kc, t * 128:(t + 1) * 128], wz[:, kc, :] if wz is not None else None, kc == 0, kc == 7, [wzb, self.uTb], [pzb])
                self.act(zs[:, t, zt * 512:(zt + 1) * 512], pz[:, 0:512], AF.Silu, [pzb], [zsb])
        st = {}

        def stage_a(c):
            if c % 4 == 0:
                st["w"] = self.W.get("w_in", l, 0, OFF_XBC + (c // 4) * 512)
            wx, wxb = st["w"]
            cc = c % 4
            xw_, xwb_ = xw[c % 3]
            pb_, pbb_ = self.bank("mm")
            for kc in range(8):
                self.mm(pb_[:, 0:W], wx[:, kc, cc * 128:(cc + 1) * 128], self.uT[:, kc, :], kc == 0, kc == 7, [wxb, self.uTb], [pbb_])
            self.act(xw_[:, 0:3], self.xtail[:, l, c, :], AF.Identity, [self.xtailb[l][c]], [xwb_])
            self.act(xw_[:, 3:3 + W], pb_[:, 0:W], AF.Identity, [pbb_], [xwb_])

        def stage_b(c):
            xw_, xwb_ = xw[c % 3]
            a_, ab_ = acc[c % 3]
            sw = PC_SW + c * 4
            self.ts(a_, xw_[:, 0:W], self.pcv(l, sw), self.pcv(l, PC_SB + c), ALU.mult, ALU.add, [xwb_, self.pcb], [ab_])
            for j in range(1, 4):
                self.stt(a_, xw_[:, j:j + W], self.pcv(l, sw + j), a_, ALU.mult, ALU.add, [xwb_, self.pcb, ab_], [ab_])
            self.act(self.xtail[:, l, c, :], xw_[:, W:W + 3], AF.Identity, [xwb_], [self.xtailb[l][c]])
            if c < 16:
                self.act(xc[c % 3][0], a_, AF.Silu, [ab_], [xc[c % 3][1]])
            else:
                self.act(bcT[:, c - 16, :], a_, AF.Silu, [ab_], [bcb[c - 16]])

        def stage_c(c):
            if c >= 20:
                return
            if c < 16:
                src, srcb = xc[c % 3]
            else:
                src, srcb = bcT[:, c - 16, :], bcb[c - 16]
            pt, ptb = self.bank("tr")
            ptv = pt[:].bitcast(BF16)
            for t in range(NT):
                self.PE(lambda e, o=ptv[:, t * 128:(t + 1) * 128], i=src[:, t * 128:(t + 1) * 128]: e.transpose(o, i, self.identb[:]), [srcb, self.identbb], [ptb])
            if c < 16:
                self.DVE(lambda e, c=c, ptv=ptv: e.tensor_copy(xtok[:, :, c * 128:(c + 1) * 128], ptv[:, 0:NT * 128].rearrange("p (t n) -> p t n", t=NT)), [ptb], [xtokb])
            else:
                self.DVE(lambda e, c=c, ptv=ptv: e.tensor_copy(btok[:, :, (c - 16) * 128:(c - 15) * 128], ptv[:, 0:NT * 128].rearrange("p (t n) -> p t n", t=NT)), [ptb], [btokb])

        for step in range(26):
            if step < 24:
                stage_a(step)
            if 0 <= step - 1 < 24:
                stage_b(step - 1)
            if 0 <= step - 2 < 24:
                stage_c(step - 2)
        wdt, wdtb = self.W.get("w_in", l, 0, OFF_DT, 8, 32)
        for t in range(NT):
            pd, pdb = self.bank("tr")
            for kc in range(8):
                self.mm(pd[:, 0:32], self.uT[:, kc, t * 128:(t + 1) * 128], wdt[:, kc, 0:32] if wdt is not None else None, kc == 0, kc == 7, [wdtb, self.uTb], [pdb])
            self.tt(dtall[:, t, :], pd[:, 0:32], self.prv(l, PR_DTB, 32), ALU.add, [pdb, self.prb], [dtallb])
        dflat = dtall.rearrange("p t h -> p (t h)")
        self.act(dflat, dflat, AF.Exp, [dtallb], [dtallb])
        self.act(dflat, dflat, AF.Ln, [dtallb], [dtallb], bias=1.0)
        for t in range(NT):
            self.tt(daall[:, t, :], dtall[:, t, :], self.arow[:, l * 32:(l + 1) * 32], ALU.mult, [dtallb, self.arowb], [daallb])

        stl = self.state[:, l, :]
        stb = self.stateb[l]
        import os
        for t in (range(NT) if 'core' not in os.environ.get('SSDSKIP', '') else []):
            tsl = slice(t * 128, (t + 1) * 128)
            dtv = dtall[:, t, :]
            dav = daall[:, t, :]
            self.act(stbf, stl, AF.Identity, stb, [stbfb])
            pc_, pcb_ = self.bank("tr")
            self.mm(pc_[:, 0:32], self.triu, dav, True, True, [self.cstb, daallb], [pcb_])
            self.mm(pc_[:, 32:64], self.onesf[:], dav, True, True, [self.onesfb, daallb], [pcb_])
            self.act(cs, pc_[:, 0:32], AF.Identity, [pcb_], [csb])
            self.act(dfs, pc_[:, 0:32], AF.Exp, [pcb_], [dfsb])
            self.act(cd, pc_[:, 32:64], AF.Exp, [pcb_], [cdb])
            self.tt(dte, pc_[:, 32:64], cs, ALU.subtract, [pcb_, csb], [dteb])
            self.act(dte, dte, AF.Exp, [dteb], [dteb])
            x3 = xtok[:, t, :].rearrange("p (h d) -> p h d", h=32)
            self.tt(xdt.rearrange("p (h d) -> p h d", h=32), x3, dtv.unsqueeze(2).to_broadcast([128, 32, 64]), ALU.mult, [xtokb, dtallb], [xdtb])
            self.tt(xdte.rearrange("p (h d) -> p h d", h=32), xdt.rearrange("p (h d) -> p h d", h=32), dte.unsqueeze(2).to_broadcast([128, 32, 64]), ALU.mult, [xdtb, dteb], [xdteb])
            pcb2, pcb2b = self.bank("mm")
            for g in range(4):
                self.mm(pcb2[:, g * 128:(g + 1) * 128], bcT[:, g, tsl], bcT[:, 4 + g, tsl], True, True, [bcb[g], bcb[4 + g]], [pcb2b])
            self.tt(cbm, pcb2[:, 0:512].rearrange("p (g q) -> p g q", g=4), self.triu.unsqueeze(1).to_broadcast([128, 4, 128]), ALU.mult, [pcb2b, self.cstb], [cbmb])
            self.DVE(lambda e: e.memset(ss, 0.0), [], [ssb])
            mmb = [(self.ps[i], self.psb[i]) for i in self.bgrp["mm"]]
            stbk = [(self.ps[i], self.psb[i]) for i in self.bgrp["st"]]

            def csq(b):
                h0 = b * 4
                pq, pqb = stbk[b % 2]
                for j in range(4):
                    self.mm(pq[:, j * 128:(j + 1) * 128], dav[:, h0 + j:h0 + j + 1].to_broadcast([128, 128]), self.triu, True, True, [daallb, self.cstb], [pqb])

            csq(0)
            for g in range(4):
                py, pyb = mmb[g % 2]
                po, pob = mmb[2]
                pn, pnb = mmb[3]
                self.mm(po[:, 0:512], bcT[:, 4 + g, tsl], stbf[:, g * 512:(g + 1) * 512], True, True, [bcb[4 + g], stbfb], [pob])
                self.mm(pn[:, 0:512], btok[:, t, g * 128:(g + 1) * 128], xdte[:, g * 512:(g + 1) * 512], True, True, [btokb, xdteb], [pnb])
                ta, tab = t1[0]
                tb, tbb = t1[1]
                self.tt(ta.rearrange("p (h d) -> p h d", h=8), po[:, 0:512].rearrange("p (h d) -> p h d", h=8), dfs[:, g * 8:(g + 1) * 8].unsqueeze(2).to_broadcast([128, 8, 64]), ALU.mult, [pob, dfsb], [tab])
                sg = stl[:, g * 512:(g + 1) * 512]
                self.tt(sg.rearrange("p (h d) -> p h d", h=8), sg.rearrange("p (h d) -> p h d", h=8), cd[:, g * 8:(g + 1) * 8].unsqueeze(2).to_broadcast([128, 8, 64]), ALU.mult, [stb[g], cdb, stbfb], [stb[g]])
                self.tt(sg, sg, pn[:, 0:512], ALU.add, [stb[g], pnb], [stb[g]])
                for hb in range(2):
                    b = g * 2 + hb
                    h0 = b * 4
                    lt_, ltb_ = lt[hb]
                    sc_, scb_ = sc[hb]
                    pq, pqb = stbk[b % 2]
                    if b + 1 < 8:
                        csq(b + 1)
                    lt3 = lt_.rearrange("p (j q) -> p j q", j=4)
                    self.tt(lt3, pq[:, 0:512].rearrange("p (j q) -> p j q", j=4), cs[:, h0:h0 + 4].unsqueeze(2).to_broadcast([128, 4, 128]), ALU.subtract, [pqb, csb], [ltb_])
                    self.act(lt_, lt_, AF.Relu, [ltb_], [ltb_], scale=-1.0)
                    self.act(lt_, lt_, AF.Exp, [ltb_], [ltb_], scale=-1.0)
                    self.tt(sc_, lt3, cbm[:, g, :].unsqueeze(1).to_broadcast([128, 4, 128]), ALU.mult, [ltb_, cbmb], [scb_])
                    for j in range(4):
                        h = h0 + j
                        self.mm(py[:, (hb * 4 + j) * 64:(hb * 4 + j + 1) * 64], sc_[:, j, :], xdt[:, h * 64:(h + 1) * 64], True, True, [scb_, xdtb], [pyb])
                ywg = yw[:, g * 512:(g + 1) * 512]
                self.tt(ywg, py[:, 0:512], ta, ALU.add, [pyb, tab], [ywb[g]])
                self.tt(tb.rearrange("p (h d) -> p h d", h=8), xtok[:, t, g * 512:(g + 1) * 512].rearrange("p (h d) -> p h d", h=8),
                        self.prv(l, PR_D + g * 8, 8).unsqueeze(2).to_broadcast([128, 8, 64]), ALU.mult, [xtokb, self.prb], [tbb])
                self.tt(ywg, ywg, tb, ALU.add, [ywb[g], tbb], [ywb[g]])
                self.tt(ywg, ywg, zs[:, t, g * 512:(g + 1) * 512], ALU.mult, [ywb[g], zsb], [ywb[g]])
                self.act(junk, ywg, AF.Square, [ywb[g]], [junkb, ssb], accum_out=ss[:, g:g + 1])
            if l == 0 and ck == 0 and t == 0:
                self.rawdump("dt", dtall.rearrange("p t h -> p (t h)"), dtallb, NT * 32)
                self.rawdump("da", daall.rearrange("p t h -> p (t h)"), daallb, NT * 32)
                self.rawdump("cs", cs, csb, 32)
                self.rawdump("dfs", dfs, dfsb, 32)
                self.rawdump("cd", cd, cdb, 32)
                self.rawdump("dte", dte, dteb, 32)
                self.rawdump("xtok", xtok[:, 0, :], xtokb, 2048)
                self.rawdump("btok", btok[:, 0, :], btokb, 512)
                self.rawdump("xdt", xdt, xdtb, 2048)
                self.rawdump("zs", zs[:, 0, :], zsb, 2048)
                self.rawdump("cbm", cbm.rearrange("p g q -> p (g q)"), cbmb, 512)
                self.rawdump("lt", lt[1][0], lt[1][1], 512)
                self.rawdump("yw", yw, ywb, 2048)
                self.rawdump("ss", ss, ssb, 8)
                self.rawdump("bc", bcT.rearrange("p c w -> p (c w)"), bcb, 8 * W)
            self.rsqrt(rs4[:, 0:4], ss[:, 0:4], 1.0 / 512, [ssb], [rs4b])
            self.tt(ytok.rearrange("p (g n) -> p g n", g=4), yw.rearrange("p (g n) -> p g n", g=4), rs4[:, 0:4].unsqueeze(2).to_broadcast([128, 4, 512]), ALU.mult, ywb + [rs4b], [ytokb])
            for half in range(2):
                pt, ptb = self.bank("tr")
                ptv = pt[:].bitcast(BF16)
                for j in range(8):
                    c = half * 8 + j
                    self.PE(lambda e, o=ptv[:, j * 128:(j + 1) * 128], i=ytok[:, c * 128:(c + 1) * 128]: e.transpose(o, i, self.identb[:]), [ytokb, self.identbb], [ptb])
                self.tt(yT[:, half * 8:(half + 1) * 8, tsl], ptv[:, 0:1024].rearrange("p (c q) -> p c q", c=8),
                        self.pcv(l, PC_NG + half * 8, 8).unsqueeze(2).to_broadcast([128, 8, 128]), ALU.mult, [ptb, self.pcb], [yTb])
        self.dump("ybin", yT, yTb, ck, 16) if l == 0 else None
        WW = 512 // 2
        scr = [(lt[0][0][:, 0:W], lt[0][1]), (lt[0][0][:, WW:WW + W], lt[0][1]), (lt[1][0][:, 0:W], lt[1][1]), (lt[1][0][:, WW:WW + W], lt[1][1]),
               (t1[0][0][:, 0:W], t1[0][1]), (t1[1][0][:, 0:W], t1[1][1])] if W <= 256 else None
        self.branch_out(l, 1, yT, yTb, "w_ssd_out", 16, scratch=scr)
        self.dump("m1", self.mrg[:], self.mrgb, ck) if l == 0 else None

    def qknorm_chunk(self, pq, pqb, hd_scale, stat_lhsT, stat_b, sq, rs):
        pass

    def swa_phase(self, l, ck):
        self.S.tag = 'swa_phase' + str((l, ck))
        W, NT = self.TC, self.NT
        ar = self.arena
        ar.reset()
        qT, qTb = ar.alloc(4 * W, BF16, "p (c w) -> p c w", c=8)
        oT, oTb = ar.alloc(4 * W, BF16, "p (c w) -> p c w", c=8)
        KW = 128 + W
        kT, kTb = ar.alloc(2 * KW, BF16, "p (g k) -> p g k", g=4)
        vt, vtb = ar.alloc((1 + NT) * 256, BF16, "p (t n) -> p t n", t=1 + NT)
        sq = [ar.alloc(W) for _ in range(2)]
        rs = [ar.alloc(W) for _ in range(2)]
        pts = [[ar.alloc(256, BF16) for _ in range(2)] for _ in range(2)]
        tbs = [[ar.alloc(512) for _ in range(2)] for _ in range(2)]
        dtot, dtotb = ar.alloc(512)
        rd, rdb = ar.alloc(512)
        self.DVE(lambda e: e.tensor_copy(kT[:, :, 0:128], self.kprev[:, l]), [self.kprevb[l]], [kTb])
        self.DVE(lambda e: e.tensor_copy(vt[:, 0, :], self.vprev[:, l, :]), [self.vprevb[l]], [vtb])
        kT2 = kT.rearrange("p (m e) k -> p m e k", e=2)
        self.DVE(lambda e: e.memset(kT2[64:128, :, 0, 128:KW], 0.0), [], [kTb])
        self.DVE(lambda e: e.memset(kT2[0:64, :, 1, 128:KW], 0.0), [], [kTb])
        gq = self.pcv(l, PC_QN)
        gk = self.pcv(l, PC_KN)
        for tile in range(2):
            wq, wqb = self.W.get("w_in", l, 0, OFF_Q + tile * 512, kind="qperm")
            for cc in range(4):
                c = tile * 4 + cc
                s_, sb_ = sq[c % 2]
                r_, rb_ = rs[c % 2]
                pq, pqb = self.bank("mm")
                for kc in range(8):
                    self.mm(pq[:, 0:W], wq[:, kc, cc * 128:(cc + 1) * 128], self.uT[:, kc, :], kc == 0, kc == 7, [wqb, self.uTb], [pqb])
                self.act(s_, pq[:, 0:W], AF.Square, [pqb], [sb_])
                pst, pstb = self.bank("st")
                self.mm(pst[:, 0:W], self.blk64, s_, True, True, [self.cstb, sb_], [pstb])
                self.rsqrt(r_, pst[:, 0:W], 1.0 / 64, [pstb], [rb_])
                self.stt(qT[:, c, :], pq[:, 0:W], gq, r_, ALU.mult, ALU.mult, [pqb, rb_, self.pcb], [qTb])
        wkv, wkvb = self.W.get("w_in", l, 0, OFF_K)
        for m in range(2):
            s_, sb_ = sq[m % 2]
            r_, rb_ = rs[m % 2]
            pk, pkb = self.bank("mm")
            for kc in range(8):
                self.mm(pk[:, 0:W], wkv[:, kc, m * 128:(m + 1) * 128], self.uT[:, kc, :], kc == 0, kc == 7, [wkvb, self.uTb], [pkb])
            self.act(s_, pk[:, 0:W], AF.Square, [pkb], [sb_])
            pst, pstb = self.bank("st")
            self.mm(pst[:, 0:W], self.blk64, s_, True, True, [self.cstb, sb_], [pstb])
            self.rsqrt(r_, pst[:, 0:W], 1.0 / 64, [pstb], [rb_])
            self.stt(kT[0:64, 2 * m, 128:KW], pk[0:64, 0:W], gk[0:64], r_[0:64], ALU.mult, ALU.mult, [pkb, rb_, self.pcb], [kTb])
            self.stt(kT[64:128, 2 * m + 1, 128:KW], pk[64:128, 0:W], gk[64:128], r_[64:128], ALU.mult, ALU.mult, [pkb, rb_, self.pcb], [kTb])
        for t in range(NT):
            pv, pvb = self.bank("mm")
            for kc in range(8):
                self.mm(pv[:, 0:256], self.uT[:, kc, t * 128:(t + 1) * 128], wkv[:, kc, 256:512], kc == 0, kc == 7, [wkvb, self.uTb], [pvb])
            self.DVE(lambda e, t=t, pv=pv: e.tensor_copy(vt[:, 1 + t, :].rearrange("p (g e d) -> p g e d", g=4, e=2),
                                                       pv[:, 0:256].rearrange("p (g d) -> p g d", g=4).unsqueeze(2).to_broadcast([128, 4, 2, 64])), [pvb], [vtb])
        steps = [(t, g) for t in range(NT) for g in range(4)]

        def sides_of(t):
            return ([0] if ck * NT + t > 0 else []) + [1]

        def score_stage(i):
            t, g = steps[i]
            tsl = slice(t * 128, (t + 1) * 128)
            for side in sides_of(t):
                ko = t * 128 + side * 128
                pS, pSb = self.bank("mm")
                for ii in range(4):
                    self.mm(pS[:, ii * 128:(ii + 1) * 128], kT[:, g, ko:ko + 128], qT[:, 4 * (g // 2) + ii, tsl], True, True, [kTb, qTb], [pSb])
                tb_, tbb_ = tbs[i % 2][side]
                p_, pb_ = pts[i % 2][side]
                self.stt(tb_, pS[:, 0:512], 0.125, self.bias[:, side, g * 4:(g + 1) * 4, :].rearrange("p h q -> p (h q)"), ALU.mult, ALU.add, [pSb, self.biasb], [tbb_])
                self.act(p_, tb_, AF.Exp, [tbb_], [pb_])

        def pv_stage(i):
            t, g = steps[i]
            tsl = slice(t * 128, (t + 1) * 128)
            sides = sides_of(t)
            pO, pOb = self.bank("mm")
            pD, pDb = self.bank("st")
            for k_, side in enumerate(sides):
                p_, pb_ = pts[i % 2][side]
                self.mm(pO[:, 0:512], vt[:, t + side, g * 128:(g + 1) * 128], p_, k_ == 0, k_ == len(sides) - 1, [vtb, pb_], [pOb])
            for k_, side in enumerate(sides):
                p_, pb_ = pts[i % 2][side]
                self.mm(pD[:, 0:512], self.onesb[:], p_, k_ == 0, k_ == len(sides) - 1, [self.onesbb, pb_], [pDb])
            self.tt(dtot.rearrange("p (h q) -> p h q", h=4), pD[:, 0:512].rearrange("p (h q) -> p h q", h=4),
                    self.esink[:, l * 16 + g * 4:l * 16 + g * 4 + 4].unsqueeze(2).to_broadcast([128, 4, 128]), ALU.add, [pDb, self.esinkb], [dtotb])
            self.DVE(lambda e: e.reciprocal(rd, dtot), [dtotb], [rdb])
            for e_ in range(2):
                rows = slice(64 * e_, 64 * e_ + 64)
                self.tt(oT[rows, 2 * g:2 * g + 2, tsl], pO[rows, 0:512].rearrange("p (c e q) -> p c e q", c=2, e=2)[:, :, e_, :],
                        rd[rows, 0:512].rearrange("p (c e q) -> p c e q", c=2, e=2)[:, :, e_, :], ALU.mult, [pOb, rdb], [oTb])

        score_stage(0)
        for i in range(len(steps)):
            if i + 1 < len(steps):
                score_stage(i + 1)
            pv_stage(i)
        self.DVE(lambda e: e.tensor_copy(self.kprev[:, l], kT[:, :, W:W + 128]), [kTb], [self.kprevb[l]])
        self.DVE(lambda e: e.tensor_copy(self.vprev[:, l, :], vt[:, NT, :]), [vtb], [self.vprevb[l]])
        self.dump("ycin", oT, oTb, ck) if l == 0 else None
        self.branch_out(l, 2, oT, oTb, "w_attn_out", 8)
        self.dump("m2", self.mrg[:], self.mrgb, ck) if l == 0 else None

    def proj_add(self, l, wname, srcT, srcb):
        W = self.TC
        for half in range(2):
            wt, wtb = self.W.get(wname, l, 0, half * 512)
            for cc in range(4):
                oc = half * 4 + cc
                py, pyb = self.bank("mm")
                for kc in range(8):
                    self.mm(py[:, 0:W], wt[:, kc, cc * 128:(cc + 1) * 128] if wt is not None else None, srcT[:, kc, :], kc == 0, kc == 7, [wtb, srcb], [pyb])
                self.tt(self.hT[:, oc, :], self.hT[:, oc, :], py[:, 0:W], ALU.add, [pyb, self.hTb[oc]], [self.hTb[oc]])

    def mix_phase(self, l, ck):
        self.S.tag = 'mix_phase' + str((l, ck))
        W = self.TC
        ar = self.arena
        ar.reset()
        mb, mbb = ar.alloc(4 * W, BF16, "p (c w) -> p c w", c=8)
        for c in range(8):
            self.act(mb[:, c, :], self.mrg[:, c, :], AF.Identity, [self.mrgb[c]], [mbb])
        self.proj_add(l, "w_mix_out", mb, mbb)
        self.dump("h1", self.hT[:], self.hTb, ck) if l == 0 else None

    def kv_precompute(self, l):
        self.S.tag = 'kv_precompute' + str((l))
        ar = self.arena
        ar.reset()
        memx, memxb = ar.alloc(2048, F32, "p (t d) -> p t d", t=2)
        memn, memnb = ar.alloc(1024, BF16, "p (t d) -> p t d", t=2)
        memT, memTb = ar.alloc(1024, BF16, "p (c k) -> p c k", c=8)
        kvo, kvob = ar.alloc(2048, BF16)
        junk, junkb = ar.alloc(1024)
        ms, msb = ar.alloc(8)
        sq = [ar.alloc(256) for _ in range(2)]
        rs, rsb = ar.alloc(256)
        src = self.mem.rearrange("(t p) d -> p t d", p=128)
        self.S.dma(self.kvltrack, lambda e: e.dma_start(out=memx, in_=src), writes=[memxb])
        self.DVE(lambda e: e.memset(ms, 0.0), [], [msb])
        for kt in range(2):
            self.act(junk, memx[:, kt, :], AF.Square, [memxb], [junkb, msb], accum_out=ms[:, kt:kt + 1])
        self.rsqrt(ms[:, 0:2], ms[:, 0:2], 1.0 / D, [msb], [msb])
        for kt in range(2):
            self.ts(memn[:, kt, :], memx[:, kt, :], ms[:, kt:kt + 1], None, ALU.mult, None, [memxb, msb], [memnb])
        for kt in range(2):
            pt, ptb = self.bank("tr")
            ptv = pt[:].bitcast(BF16)
            for c in range(8):
                self.PE(lambda e, o=ptv[:, c * 128:(c + 1) * 128], i=memn[:, kt, c * 128:(c + 1) * 128]: e.transpose(o, i, self.identb[:]), [memnb, self.identbb], [ptb])
            self.tt(memT[:, :, kt * 128:(kt + 1) * 128], ptv[:, 0:1024].rearrange("p (c k) -> p c k", c=8),
                    self.pcv(l, PC_NM, 8).unsqueeze(2).to_broadcast([128, 8, 128]), ALU.mult, [ptb, self.pcb], [memTb])
        kT = kvo[:, 0:2048].rearrange("p (c k) -> p c k", c=8)
        vv = kvo[:, 2048:4096].rearrange("p (t d) -> p t d", t=2)
        for tile in range(2):
            wk, wkb = self.W.get("w_xkv", l, 0, tile * 512)
            for hh2 in range(2):
                hh = tile * 2 + hh2
                pk = [self.bank("mm") for _ in range(2)]
                pst, pstb = self.bank("st")
                for j in range(2):
                    cc = hh2 * 2 + j
                    for kc in range(8):
                        self.mm(pk[j][0][:, 0:256], wk[:, kc, cc * 128:(cc + 1) * 128] if wk is not None else None, memT[:, kc, :], kc == 0, kc == 7, [wkb, memTb], [pk[j][1]])
                    self.act(sq[j][0], pk[j][0][:, 0:256], AF.Square, [pk[j][1]], [sq[j][1]])
                    self.mm(pst[:, 0:256], self.onesf[:], sq[j][0], j == 0, j == 1, [self.onesfb, sq[j][1]], [pstb])
                self.rsqrt(rs, pst[:, 0:256], 1.0 / 256, [pstb], [rsb])
                for j in range(2):
                    self.stt(kT[:, 2 * hh + j, :], pk[j][0][:, 0:256], self.pcv(l, PC_XK + j), rs, ALU.mult, ALU.mult, [pk[j][1], rsb, self.pcb], [kvob])
        for tile in range(2):
            wv, wvb = self.W.get("w_xkv", l, 0, 1024 + tile * 512)
            for kt in range(2):
                pv, pvb = self.bank("mm")
                for kc in range(8):
                    self.mm(pv[:, 0:512], memT[:, kc, kt * 128:(kt + 1) * 128], wv[:, kc, :] if wv is not None else None, kc == 0, kc == 7, [wvb, memTb], [pvb])
                self.act(vv[:, kt, tile * 512:(tile + 1) * 512], pv[:, 0:512], AF.Identity, [pvb], [kvob])
        self.S.dma(self.kvtrack, lambda e: e.dma_start(out=self.kvs[l], in_=kvo), reads=[kvob], writes=[self.kvs_b[l]])

    def xattn_phase(self, l, ck):
        self.S.tag = 'xattn_phase' + str((l, ck))
        W = self.TC
        self.rmsnorm(l, PC_NX)
        ar = self.arena
        ar.reset()
        kvb, kvbb = ar.alloc(2048, BF16)
        qT, qTb = ar.alloc(4 * W, BF16, "p (c w) -> p c w", c=8)
        oT, oTb = ar.alloc(4 * W, BF16, "p (c w) -> p c w", c=8)
        sq = [ar.alloc(W) for _ in range(2)]
        rs, rsb = ar.alloc(W)
        rd, rdb = ar.alloc(W)
        pts = [[ar.alloc(W // 2, BF16) for _ in range(2)] for _ in range(2)]
        self.S.dma(self.kvltrack, lambda e: e.dma_start(out=kvb, in_=self.kvs[l]), reads=[self.kvs_b[l]], writes=[kvbb])
        kT = kvb[:, 0:2048].rearrange("p (c k) -> p c k", c=8)
        vv = kvb[:, 2048:4096].rearrange("p (t d) -> p t d", t=2)
        for tile in range(2):
            wq, wqb = self.W.get("w_xq", l, 0, tile * 512)
            for hh2 in range(2):
                hh = tile * 2 + hh2
                pq = [self.bank("mm") for _ in range(2)]
                pst, pstb = self.bank("st")
                for j in range(2):
                    cc = hh2 * 2 + j
                    for kc in range(8):
                        self.mm(pq[j][0][:, 0:W], wq[:, kc, cc * 128:(cc + 1) * 128] if wq is not None else None, self.uT[:, kc, :], kc == 0, kc == 7, [wqb, self.uTb], [pq[j][1]])
                    self.act(sq[j][0], pq[j][0][:, 0:W], AF.Square, [pq[j][1]], [sq[j][1]])
                    self.mm(pst[:, 0:W], self.onesf[:], sq[j][0], j == 0, j == 1, [self.onesfb, sq[j][1]], [pstb])
                self.rsqrt(rs, pst[:, 0:W], 1.0 / 256, [pstb], [rsb])
                for j in range(2):
                    self.stt(qT[:, 2 * hh + j, :], pq[j][0][:, 0:W], self.pcv(l, PC_XQ + j), rs, ALU.mult, ALU.mult, [pq[j][1], rsb, self.pcb], [qTb])
        def xscore(hh):
            pp = pts[hh % 2]
            for kt in range(2):
                pS, pSb = self.bank("mm")
                for j in range(2):
                    self.mm(pS[:, 0:W], kT[:, 2 * hh + j, kt * 128:(kt + 1) * 128], qT[:, 2 * hh + j, :], j == 0, j == 1, [kvbb, qTb], [pSb])
                self.act(pp[kt][0], pS[:, 0:W], AF.Exp, [pSb], [pp[kt][1]], scale=1.0 / 16)

        def xpv(hh):
            pp = pts[hh % 2]
            pD, pDb = self.bank("st")
            for kt in range(2):
                self.mm(pD[:, 0:W], self.onesb[:], pp[kt][0], kt == 0, kt == 1, [self.onesbb, pp[kt][1]], [pDb])
            self.DVE(lambda e, pD=pD: e.reciprocal(rd, pD[:, 0:W]), [pDb], [rdb])
            for dc in range(2):
                c = 2 * hh + dc
                pO, pOb = self.bank("mm")
                for kt in range(2):
                    self.mm(pO[:, 0:W], vv[:, kt, c * 128:(c + 1) * 128], pp[kt][0], kt == 0, kt == 1, [kvbb, pp[kt][1]], [pOb])
                self.tt(oT[:, c, :], pO[:, 0:W], rd, ALU.mult, [pOb, rdb], [oTb])

        xscore(0)
        for hh in range(4):
            if hh + 1 < 4:
                xscore(hh + 1)
            xpv(hh)
        self.proj_add(l, "w_xo", oT, oTb)
        self.dump("h2", self.hT[:], self.hTb, ck) if l == 0 else None

    def mlp_phase(self, l, ck):
        self.S.tag = 'mlp_phase' + str((l, ck))
        W = self.TC
        self.rmsnorm(l, PC_NMLP)
        ar = self.arena
        ar.reset()
        actT, _ = ar.alloc(16 * W, BF16, "p (c w) -> p c w", c=32)
        actb = ar.bufs(4)
        rl = [ar.alloc(W) for _ in range(2)]
        for tile in range(8):
            wu, wub = self.W.get("w_mlp_up", l, 0, tile * 512)
            for cc in range(4):
                hc = tile * 4 + cc
                r_, rb_ = rl[hc % 2]
                pu, pub = self.bank("mm")
                for kc in range(8):
                    self.mm(pu[:, 0:W], wu[:, kc, cc * 128:(cc + 1) * 128] if wu is not None else None, self.uT[:, kc, :], kc == 0, kc == 7, [wub, self.uTb], [pub])
                self.act(r_, pu[:, 0:W], AF.Relu, [pub], [rb_])
                self.tt(actT[:, hc, :], r_, r_, ALU.mult, [rb_], [actb[hc // 8]])
        for half in range(2):
            banks = [self.bank("mm") for _ in range(4)]
            for j in range(4):
                wd_, wdb_ = self.W.get("w_mlp_down", l, j * 8, half * 512)
                for cc in range(4):
                    py, pyb = banks[cc]
                    for kk in range(8):
                        self.mm(py[:, 0:W], wd_[:, kk, cc * 128:(cc + 1) * 128] if wd_ is not None else None, actT[:, j * 8 + kk, :], j == 0 and kk == 0, j == 3 and kk == 7, [wdb_, actb[j]], [pyb])
            for cc in range(4):
                oc = half * 4 + cc
                py, pyb = banks[cc]
                self.tt(self.hT[:, oc, :], self.hT[:, oc, :], py[:, 0:W], ALU.add, [pyb, self.hTb[oc]], [self.hTb[oc]])
        self.dump("h3", self.hT[:], self.hTb, ck) if l == 0 else None


HPERM = list(range(16))


def host_consts():
    c = np.zeros((128, 512), np.float32)
    c[:, 0:128] = np.eye(128, dtype=np.float32)
    c[:, 128:256] = np.triu(np.ones((128, 128), np.float32))
    c[:, 256:384] = np.tril(np.ones((128, 128), np.float32), -1) * NEG
    blk = np.zeros((128, 128), np.float32)
    blk[0:64, 0:64] = 1.0
    blk[64:128, 64:128] = 1.0
    c[:, 384:512] = blk
    return c


def host_swab(rel_table):
    qi = np.arange(128)[:, None] + 128
    kj = np.arange(256)[None, :]
    dist = qi - kj
    max_exact = 16
    d = np.maximum(dist, 1).astype(np.float32)
    large = max_exact + (np.log(d / np.float32(max_exact)) / np.float32(math.log(128 / max_exact)) * np.float32(32 - max_exact)).astype(np.int32)
    large = np.minimum(large, 31)
    bucket = np.where(dist < max_exact, np.maximum(dist, 0), large)
    valid = (dist >= 0) & (dist < 128)
    bias = rel_table[bucket]
    bias = np.where(valid[:, :, None], bias, np.float32(NEG)).astype(np.float32)
    out = np.zeros((128, 2, 16, 128), np.float32)
    for side in range(2):
        blkb = bias[:, side * 128:(side + 1) * 128, :]
        out[:, side] = np.transpose(blkb[:, :, HPERM], (1, 2, 0))
    return out.reshape(128, 2 * 16 * 128)


def colv(v):
    return np.ascontiguousarray(np.asarray(v, np.float32).reshape(-1, 128).T)


def host_params(inp, L):
    pcol = np.zeros((128, L, NPC), np.float32)
    prow = np.zeros((128, L, NPR), np.float32)
    for l in range(L):
        pc = pcol[:, l]
        pc[:, PC_NMIX:PC_NMIX + 8] = colv(inp["norm_mix"][l])
        for k in range(3):
            pc[:, PC_GB + k * 8:PC_GB + k * 8 + 8] = colv(inp["gate_bias"][l, k])
        cw = inp["conv_dw_w"][l]
        pc[:, PC_CW:PC_CW + 248] = np.transpose(cw.reshape(31, 8, 128), (2, 1, 0)).reshape(128, 248)
        pc[:, PC_CB:PC_CB + 8] = colv(inp["conv_dw_b"][l])
        pc[:, PC_LG:PC_LG + 8] = colv(inp["conv_ln_g"][l])
        pc[:, PC_LB:PC_LB + 8] = colv(inp["conv_ln_b"][l])
        sw = inp["ssd_conv_w"][l]
        pc[:, PC_SW:PC_SW + 96] = np.transpose(sw.reshape(4, 24, 128), (2, 1, 0)).reshape(128, 96)
        pc[:, PC_SB:PC_SB + 24] = colv(inp["ssd_conv_b"][l])
        pc[:, PC_NG:PC_NG + 16] = colv(inp["ssd_norm_g"][l])
        pc[:, PC_QN] = np.tile(inp["attn_q_norm"][l], 2)
        pc[:, PC_KN] = np.tile(inp["attn_k_norm"][l], 2)
        pc[:, PC_NX:PC_NX + 8] = colv(inp["norm_xattn"][l])
        pc[:, PC_NM:PC_NM + 8] = colv(inp["norm_mem"][l])
        pc[:, PC_XQ:PC_XQ + 2] = colv(inp["xattn_q_norm"][l])
        pc[:, PC_XK:PC_XK + 2] = colv(inp["xattn_k_norm"][l])
        pc[:, PC_NMLP:PC_NMLP + 8] = colv(inp["norm_mlp"][l])
        pr = prow[:, l]
        pr[:, PR_DTB:PR_DTB + 32] = inp["ssd_dt_bias"][l][None, :]
        pr[:, PR_ALOG:PR_ALOG + 32] = inp["ssd_A_log"][l][None, :]
        pr[:, PR_D:PR_D + 32] = inp["ssd_D"][l][None, :]
        pr[:, PR_SINK:PR_SINK + 16] = inp["attn_sinks"][l][HPERM][None, :]
    return pcol.reshape(128, L * NPC), prow.reshape(128, L * NPR)


def make_in_maps(inp, T, L, batches):
    consts = host_consts()
    swab = host_swab(np.asarray(inp["rel_table"], np.float32))
    pcol, prow = host_params(inp, L)
    shared = {"consts": consts, "swab": swab, "pcol": pcol, "prow": prow}
    for n in WSHAPES:
        shared[n] = np.ascontiguousarray(np.asarray(inp[n], np.float32)[:L])
    maps = []
    for b in batches:
        m = dict(shared)
        m["x"] = np.ascontiguousarray(np.asarray(inp["x"], np.float32)[b, :T])
        m["mem"] = np.ascontiguousarray(np.asarray(inp["mem"], np.float32)[b])
        maps.append(m)
    return maps


_NC_CACHE = {}


def kernel(**inputs):
    T, L, TC = 4096, 4, 256
    key = (T, L, TC)
    if key not in _NC_CACHE:
        _NC_CACHE[key] = KB(T, L, TC).build()
    nc = _NC_CACHE[key]
    maps = make_in_maps(inputs, T, L, list(range(8)))
    res = run_bass_kernel_spmd(nc, maps, core_ids=list(range(8)))
    return np.stack([np.asarray(r["y"], np.float32) for r in res.results], axis=0)
```
